# Optimizing a Trainium2 kernel written in Bass

```python
import math
import jax, jax.numpy as jnp
from jax import lax
import numpy as np

D_MODEL = 1024
BATCH = 8
SEQ = 2048
DEPTH = 4

N_A_LAYERS = DEPTH // 2
N_B_LAYERS = DEPTH - N_A_LAYERS
D_FF = 256 * ((8 * D_MODEL // 3 + 255) // 256)
CONV_WIDTH = 3
HEAD_DIM = 64
N_HEADS = D_MODEL // (2 * HEAD_DIM)
Q_BLOCK = 128
EPS = 1e-6
LAMBDA_STD = 0.1

kernel_name = "yoco_shortconv_diffattn_macaron"


def rms_norm(x, g):
    xf = x.astype(jnp.float32)
    y = xf * lax.rsqrt(jnp.mean(xf * xf, axis=-1, keepdims=True) + EPS)
    return (y * g.astype(jnp.float32)).astype(x.dtype)


def swiglu(h, w_gate, w_up, w_down):
    return (jax.nn.silu(h @ w_gate) * (h @ w_up)) @ w_down


def short_conv_mixer(h, w_in, w_conv, w_out):
    gate_b, gate_c, u = jnp.split(h @ w_in, 3, axis=-1)
    z = lax.conv_general_dilated(
        gate_c * u, w_conv[:, None, :], window_strides=(1,),
        padding=[(CONV_WIDTH - 1, 0)], dimension_numbers=("NWC", "WIO", "NWC"),
        feature_group_count=D_MODEL)
    return (gate_b * z) @ w_out


def shared_kv(x, g_kv, w_kv):
    b, s, _ = x.shape
    k, v = jnp.split(rms_norm(x, g_kv) @ w_kv, 2, axis=-1)
    k = k.reshape(b, s, N_HEADS, 2, HEAD_DIM)
    v = v.reshape(b, s, N_HEADS, 2 * HEAD_DIM)
    return k, v


def diff_attention(h, k, v, w_q, lq1, lk1, lq2, lk2, g_subln, w_o, lambda_init):
    b, s, _ = h.shape
    q = (h @ w_q).reshape(b, s, N_HEADS, 2, HEAD_DIM) * (HEAD_DIM ** -0.5)
    f32 = jnp.float32
    lam = (jnp.exp(jnp.sum(lq1.astype(f32) * lk1.astype(f32)))
           - jnp.exp(jnp.sum(lq2.astype(f32) * lk2.astype(f32))) + lambda_init)
    n_blocks = s // Q_BLOCK
    q_blocks = jnp.moveaxis(q.reshape(b, n_blocks, Q_BLOCK, N_HEADS, 2, HEAD_DIM), 1, 0)
    key_pos = jnp.arange(s)

    def block(args):
        qb, i = args
        scores = jnp.einsum("bqhcd,bkhcd->bhcqk", qb, k).astype(f32)
        q_pos = i * Q_BLOCK + jnp.arange(Q_BLOCK)
        causal = key_pos[None, :] <= q_pos[:, None]
        p = jax.nn.softmax(jnp.where(causal, scores, -jnp.inf), axis=-1)
        a = p[:, :, 0] - lam * p[:, :, 1]
        return jnp.einsum("bhqk,bkhe->bqhe", a.astype(v.dtype), v)

    o = lax.map(block, (q_blocks, jnp.arange(n_blocks)))
    o = jnp.moveaxis(o, 0, 1).reshape(b, s, N_HEADS, 2 * HEAD_DIM)
    o = rms_norm(o, g_subln) * (1.0 - lambda_init)
    return o.reshape(b, s, D_MODEL) @ w_o


def setup_inputs(seed: int = 0) -> dict:
    key = jax.random.key(seed)
    ks = jax.random.split(key, 20)
    f32 = jnp.float32
    nrm = lambda k, shape, fan_in: jax.random.normal(k, shape, f32) * (fan_in ** -0.5)
    return {
        "x": jax.random.normal(ks[0], (BATCH, SEQ, D_MODEL), f32),
        "g_norm": 1.0 + 0.02 * jax.random.normal(ks[1], (DEPTH, 6, D_MODEL), f32),
        "w_ffn_gate": nrm(ks[2], (DEPTH, 2, D_MODEL, D_FF), D_MODEL),
        "w_ffn_up": nrm(ks[3], (DEPTH, 2, D_MODEL, D_FF), D_MODEL),
        "w_ffn_down": nrm(ks[4], (DEPTH, 2, D_FF, D_MODEL), D_FF),
        "w_conv_in": nrm(ks[5], (N_A_LAYERS, D_MODEL, 3 * D_MODEL), D_MODEL),
        "w_conv": nrm(ks[6], (N_A_LAYERS, CONV_WIDTH, D_MODEL), CONV_WIDTH),
        "w_conv_out": nrm(ks[7], (N_A_LAYERS, D_MODEL, D_MODEL), D_MODEL),
        "g_kv": 1.0 + 0.02 * jax.random.normal(ks[8], (D_MODEL,), f32),
        "w_kv": nrm(ks[9], (D_MODEL, 2 * D_MODEL), D_MODEL),
        "w_q": nrm(ks[10], (N_B_LAYERS, D_MODEL, D_MODEL), D_MODEL),
        "lambda_q1": LAMBDA_STD * jax.random.normal(ks[11], (N_B_LAYERS, HEAD_DIM), f32),
        "lambda_k1": LAMBDA_STD * jax.random.normal(ks[12], (N_B_LAYERS, HEAD_DIM), f32),
        "lambda_q2": LAMBDA_STD * jax.random.normal(ks[13], (N_B_LAYERS, HEAD_DIM), f32),
        "lambda_k2": LAMBDA_STD * jax.random.normal(ks[14], (N_B_LAYERS, HEAD_DIM), f32),
        "g_subln": 1.0 + 0.02 * jax.random.normal(ks[15], (N_B_LAYERS, 2 * HEAD_DIM), f32),
        "w_o": nrm(ks[16], (N_B_LAYERS, D_MODEL, D_MODEL), D_MODEL),
    }


def reference(x, g_norm, w_ffn_gate, w_ffn_up, w_ffn_down, w_conv_in, w_conv, w_conv_out,
              g_kv, w_kv, w_q, lambda_q1, lambda_k1, lambda_q2, lambda_k2, g_subln, w_o):
    k = v = None
    for layer in range(DEPTH):
        g = g_norm[layer]
        f = swiglu(rms_norm(x, g[0]), w_ffn_gate[layer, 0], w_ffn_up[layer, 0], w_ffn_down[layer, 0])
        x = x + 0.5 * rms_norm(f, g[1])
        h = rms_norm(x, g[2])
        if layer < N_A_LAYERS:
            m = short_conv_mixer(h, w_conv_in[layer], w_conv[layer], w_conv_out[layer])
        else:
            j = layer - N_A_LAYERS
            lambda_init = 0.8 - 0.6 * math.exp(-0.3 * layer)
            m = diff_attention(h, k, v, w_q[j], lambda_q1[j], lambda_k1[j], lambda_q2[j],
                               lambda_k2[j], g_subln[j], w_o[j], lambda_init)
        x = x + rms_norm(m, g[3])
        f = swiglu(rms_norm(x, g[4]), w_ffn_gate[layer, 1], w_ffn_up[layer, 1], w_ffn_down[layer, 1])
        x = x + 0.5 * rms_norm(f, g[5])
        if layer == N_A_LAYERS - 1:
            k, v = shared_kv(x, g_kv, w_kv)
    return x
```

```python
import math
from contextlib import ExitStack

import numpy as np
import concourse.bass as bass
import concourse.mybir as mybir
from concourse.bass_utils import run_bass_kernel_spmd

F32 = mybir.dt.float32
BF16 = mybir.dt.bfloat16
AF = mybir.ActivationFunctionType
ALU = mybir.AluOpType
AX = mybir.AxisListType

D = 1024
SEQ = 2048
NB = 8
DFF = 2816
EPS = 1e-6
TCH = 1024
NCH = SEQ // TCH
SAME_ENG_SYNC = True


class _Op:
    __slots__ = ("eng", "fn", "dma", "lane", "deps", "idx", "ms", "cnt", "waits")


class Prog:
    def __init__(self):
        self.ops = []
        self.last_w = {}
        self.readers = {}
        self.lane_last = {}

    def add(self, eng, fn, reads=(), writes=(), lane=None):
        op = _Op()
        op.eng = eng
        op.fn = fn
        op.dma = lane is not None
        op.lane = lane
        op.idx = len(self.ops)
        op.ms = False
        op.cnt = 0
        deps = set()
        for t in reads:
            w = self.last_w.get(t)
            if w is not None:
                deps.add(w)
        for t in writes:
            w = self.last_w.get(t)
            if w is not None:
                deps.add(w)
            rs = self.readers.get(t)
            if rs:
                deps.update(rs)
        if lane is not None:
            p = self.lane_last.get(lane)
            if p is not None:
                deps.add(p)
            self.lane_last[lane] = op.idx
        for t in reads:
            self.readers.setdefault(t, []).append(op.idx)
        for t in writes:
            self.last_w[t] = op.idx
            self.readers[t] = []
        deps.discard(op.idx)
        op.deps = deps
        self.ops.append(op)
        return op

    @staticmethod
    def _needs_sync(p, op):
        if p.dma:
            return True
        if p.eng != op.eng:
            return True
        if p.eng == "pe":
            return False
        return SAME_ENG_SYNC

    def emit(self, nc, stack):
        ops = self.ops
        for op in ops:
            for d in op.deps:
                p = ops[d]
                if (not p.dma) and self._needs_sync(p, op):
                    p.ms = True
        eng_cnt = {}
        lane_cnt = {}
        for op in ops:
            if op.dma:
                lane_cnt[op.lane] = lane_cnt.get(op.lane, 0) + 1
                op.cnt = 16 * lane_cnt[op.lane]
            elif op.ms:
                eng_cnt[op.eng] = eng_cnt.get(op.eng, 0) + 1
                op.cnt = eng_cnt[op.eng]
        waited = {}
        for op in ops:
            need = {}
            for d in op.deps:
                p = ops[d]
                if not self._needs_sync(p, op):
                    continue
                key = ("lane", p.lane) if p.dma else ("eng", p.eng)
                if need.get(key, 0) < p.cnt:
                    need[key] = p.cnt
            wd = waited.setdefault(op.eng, {})
            op.waits = []
            for k, v in need.items():
                if wd.get(k, 0) < v:
                    op.waits.append((k, v))
                    wd[k] = v
        sems = {}
        for e in sorted(eng_cnt):
            sems[("eng", e)] = stack.enter_context(nc.semaphore("s_" + e))
        for i, l in enumerate(lane_cnt):
            sems[("lane", l)] = stack.enter_context(nc.semaphore("l_%d" % i))
        per = {}
        for op in ops:
            per.setdefault(op.eng, []).append(op)
        self.stats = {e: len(v) for e, v in per.items()}
        self.stats["sems"] = len(sems)

        def mk(name):
            lst = per.get(name, [])

            def body(e):
                for op in lst:
                    for k, v in op.waits:
                        e.wait_ge(sems[k], v)
                    if op.fn is None:
                        continue
                    ins = op.fn(e)
                    if op.dma:
                        ins.then_inc(sems[("lane", op.lane)], 16)
                    elif op.ms:
                        ins.then_inc(sems[("eng", op.eng)], 1)

            return body

        with nc.Block() as block:
            block.tensor(mk("pe"))
            block.scalar(mk("act"))
            block.vector(mk("dve"))
            block.gpsimd(mk("pool"))
            block.sync(mk("sp"))


def lambda_init(layer):
    return 0.8 - 0.6 * math.exp(-0.3 * layer)


DBG = {"ffn_stage": 99, "lite": False}


def build_nc(nstop=999):
    nc = bass.Bass("TRN2", target_bir_lowering=False)
    nl4 = 1 if DBG["lite"] else 4
    nl2 = 1 if DBG["lite"] else 2

    def din(name, shape):
        return nc.dram_tensor(name, list(shape), F32, kind="ExternalInput").ap()

    x_d = din("x", (SEQ, D))
    gnorm_d = din("g_norm", (4, 6, D))
    wg_d = din("w_ffn_gate", (nl4, nl2, D, DFF))
    wu_d = din("w_ffn_up", (nl4, nl2, D, DFF))
    wd_d = din("w_ffn_down", (nl4, nl2, DFF, D))
    wci_d = din("w_conv_in", (nl2, D, 3 * D))
    wc_d = din("w_conv", (2, 3, D))
    wco_d = din("w_conv_out", (nl2, D, D))
    gkv_d = din("g_kv", (D,))
    wkv_d = din("w_kv", (D, 2 * D))
    wq_d = din("w_q", (nl2, D, D))
    lq1_d = din("lambda_q1", (2, 64))
    lk1_d = din("lambda_k1", (2, 64))
    lq2_d = din("lambda_q2", (2, 64))
    lk2_d = din("lambda_k2", (2, 64))
    gsub_d = din("g_subln", (2, 128))
    wo_d = din("w_o", (nl2, D, D))
    cst_d = din("consts", (128, 256))
    out_d = nc.dram_tensor("out", [SEQ, D], F32, kind="ExternalOutput").ap()

    P = Prog()
    with ExitStack() as st:
        def sb(name, shape, dt):
            return st.enter_context(nc.sbuf_tensor(name, list(shape), dt))

        X = sb("X", (128, 8, TCH), F32)
        H = sb("H", (128, 8, TCH), BF16)
        Fr = sb("Fr", (128, 16, 512), F32)
        A = sb("A", (128, 11, TCH), BF16)
        KT = sb("KT", (128, 8, SEQ), BF16)
        V = sb("V", (128, 16, D), BF16)
        slots = [sb("slot%d" % i, (128, 6144), BF16) for i in range(2)]
        cst = sb("cst", (128, 256), F32)
        tri = sb("tri", (128, 128), BF16)
        ones = sb("ones", (128, 128), BF16)
        cols = sb("cols", (128, 256), F32)
        eps_t = sb("eps", (128, 1), F32)
        sq = [sb("sq%d" % i, (128, 512), BF16) for i in range(3)]
        sd = sb("sd", (128, 512), F32)
        rstd = [sb("rstd%d" % i, (128, 512), F32) for i in range(2)]
        sg = [sb("sg%d" % i, (128, 512), F32) for i in range(2)]
        halo = sb("halo", (128, 2, 8, 2), F32)
        small = sb("small", (128, 16), F32)
        PS = st.enter_context(nc.psum_tensor("PS", [128, 8, 512], F32))
        ident = cst[:, 0:128]

        def pst(b):
            return [("ps", b, 0), ("ps", b, 1)]

        def tbs(tb):
            return slice(tb * 512, (tb + 1) * 512)

        def mm(out, lhsT, rhs, start, stop, reads, writes):
            P.add("pe", lambda e: e.matmul(out, lhsT, rhs, start=start, stop=stop), reads, writes)

        ctr = {"slot": 0, "stat": 0, "sq": 0, "rstd": 0, "u": 0}

        def next_slot():
            s = ctr["slot"] % 2
            ctr["slot"] += 1
            return s

        def stoks(s, a, b):
            return [("slot", s, i) for i in range(a, b)]

        P.add("dve", lambda e: e.memset(ones[:], 1.0), writes=["ones"])
        P.add("dve", lambda e: e.memset(eps_t[:], EPS), writes=["eps"])
        P.add("sp", lambda e: e.dma_start(out=cst[:], in_=cst_d), writes=["cst"], lane="cst")
        P.add("dve", lambda e: e.tensor_copy(out=tri[:], in_=cst[:, 128:256]), reads=["cst"], writes=["tri"])
        gview = gnorm_d.rearrange("l n (j p) -> (l n j) p", p=128)
        P.add("sp", lambda e: e.dma_start(out=Fr[0:128, 0, 0:128], in_=gview[0:128, :]),
              writes=[("F", 0)], lane="p0")
        P.add("sp", lambda e: e.dma_start(out=Fr[0:64, 1, 0:128], in_=gview[128:192, :]),
              writes=[("F", 1)], lane="p1")
        P.add("sp", lambda e: e.dma_start(out=Fr[64:72, 1, 0:128], in_=gkv_d.rearrange("(j p) -> j p", p=128)),
              writes=[("F", 1)], lane="p2")
        P.add("sp", lambda e: e.dma_start(out=Fr[72:120, 1, 0:128],
                                          in_=wc_d.rearrange("l w (j p) -> (l w j) p", p=128)),
              writes=[("F", 1)], lane="p3")
        P.add("sp", lambda e: e.dma_start(out=Fr[120:122, 1, 0:128], in_=gsub_d),
              writes=[("F", 1)], lane="p4")
        P.add("pe", lambda e: e.transpose(PS[:, 0, 0:128], Fr[0:128, 0, 0:128], ident),
              reads=[("F", 0), "cst"], writes=pst(0))
        P.add("pe", lambda e: e.transpose(PS[:, 0, 128:250], Fr[0:122, 1, 0:128], cst[0:122, 0:122]),
              reads=[("F", 1), "cst"], writes=pst(0))
        P.add("dve", lambda e: e.tensor_copy(out=cols[:, 0:250], in_=PS[:, 0, 0:250]), reads=pst(0), writes=["cols"])
        gv4 = cols[:, 0:192].rearrange("p (l n j) -> p l n j", l=4, n=6)
        for n in (1, 5):
            P.add("dve", (lambda e, n=n: e.tensor_scalar(out=gv4[:, :, n, :], in0=gv4[:, :, n, :], scalar1=0.5,
                                                         scalar2=None, op0=ALU.mult)),
                  reads=["cols"], writes=["cols"])
        lv = Fr[:, 2, :].rearrange("p (a b) -> p a b", a=4)
        for i, ld in enumerate((lq1_d, lk1_d, lq2_d, lk2_d)):
            P.add("sp", (lambda e, i=i, ld=ld: e.dma_start(
                out=lv[:, i, :], in_=ld.rearrange("a b -> (a b)").partition_broadcast(128))),
                writes=[("F", 2)], lane="l%d" % i)
        pr = Fr[:, 3, 0:256].rearrange("p (a b) -> p a b", a=2)
        P.add("dve", lambda e: e.tensor_tensor(out=pr[:, 0, :], in0=lv[:, 0, :], in1=lv[:, 1, :], op=ALU.mult),
              reads=[("F", 2)], writes=[("F", 3)])
        P.add("dve", lambda e: e.tensor_tensor(out=pr[:, 1, :], in0=lv[:, 2, :], in1=lv[:, 3, :], op=ALU.mult),
              reads=[("F", 2)], writes=[("F", 3)])
        P.add("dve", lambda e: e.reduce_sum(out=small[:, 0:4],
                                            in_=Fr[:, 3, 0:256].rearrange("p (a b) -> p a b", a=4), axis=AX.X),
              reads=[("F", 3)], writes=["small"])
        P.add("act", lambda e: e.activation(out=small[:, 4:8], in_=small[:, 0:4], func=AF.Exp),
              reads=["small"], writes=["small"])
        P.add("dve", lambda e: e.tensor_tensor(out=small[:, 8:10], in0=small[:, 6:8], in1=small[:, 4:6],
                                               op=ALU.subtract), reads=["small"], writes=["small"])
        for jl in range(2):
            li = lambda_init(jl + 2)
            P.add("dve", (lambda e, jl=jl, li=li: e.tensor_scalar(out=small[:, 8 + jl:9 + jl], in0=small[:, 8 + jl:9 + jl],
                                                                  scalar1=-li, scalar2=None, op0=ALU.add)),
                  reads=["small"], writes=["small"])
            P.add("dve", (lambda e, jl=jl, li=li: e.tensor_scalar(out=small[:, 10 + jl:11 + jl],
                                                                  in0=cols[:, 248 + jl:249 + jl],
                                                                  scalar1=1.0 - li, scalar2=None, op0=ALU.mult)),
                  reads=["small", "cols"], writes=["small"])

        def gidx(l, n):
            return (l * 6 + n) * 8

        def rstd_of(src_fn, src_tok, ncols, invd, nj):
            b = 6 + ctr["stat"] % 2
            ctr["stat"] += 1
            ssb = PS[:, b, 0:ncols]
            for j in range(nj):
                r = ctr["sq"] % 3
                ctr["sq"] += 1
                P.add("act", (lambda e, j=j, r=r: e.activation(out=sq[r][:, 0:ncols], in_=src_fn(j), func=AF.Square)),
                      reads=[src_tok(j)], writes=[("sq", r)])
                mm(ssb, ones[:], sq[r][:, 0:ncols], j == 0, j == nj - 1, [("sq", r), "ones"], pst(b))
            P.add("act", lambda e: e.activation(out=sd[:, 0:ncols], in_=ssb, func=AF.Sqrt, bias=eps_t[:, 0:1],
                                                scale=invd), reads=pst(b) + ["eps"], writes=["sd"])
            rb = ctr["rstd"] % 2
            ctr["rstd"] += 1
            P.add("dve", lambda e: e.reciprocal(out=rstd[rb][:, 0:ncols], in_=sd[:, 0:ncols]),
                  reads=["sd"], writes=[("rstd", rb)])
            return rstd[rb][:, 0:ncols], ("rstd", rb)

        def prenorm(gbase, tb):
            rs, rtok = rstd_of(lambda j: X[:, j, tbs(tb)], lambda j: ("X", j, tb), 512, 1.0 / D, 8)
            for j in range(8):
                P.add("dve", (lambda e, j=j: e.scalar_tensor_tensor(
                    out=H[:, j, tbs(tb)], in0=X[:, j, tbs(tb)], scalar=cols[:, gbase + j:gbase + j + 1], in1=rs,
                    op0=ALU.mult, op1=ALU.mult)),
                    reads=[("X", j, tb), rtok, "cols"], writes=[("H", j, tb)])

        def postnorm(gbase, tb):
            rs, rtok = rstd_of(lambda j: Fr[:, 2 * j + tb, :], lambda j: ("F", 2 * j + tb), 512, 1.0 / D, 8)
            for j in range(8):
                blk = 2 * j + tb
                P.add("dve", (lambda e, j=j, blk=blk: e.scalar_tensor_tensor(
                    out=Fr[:, blk, :], in0=Fr[:, blk, :], scalar=cols[:, gbase + j:gbase + j + 1], in1=rs,
                    op0=ALU.mult, op1=ALU.mult)),
                    reads=[("F", blk), rtok, "cols"], writes=[("F", blk)])
                P.add("dve", (lambda e, j=j, blk=blk: e.tensor_tensor(
                    out=X[:, j, tbs(tb)], in0=X[:, j, tbs(tb)], in1=Fr[:, blk, :], op=ALU.add)),
                    reads=[("F", blk), ("X", j, tb)], writes=[("X", j, tb)])

        def wview(w2d):
            return w2d.rearrange("(k p) f -> p k f", p=128)

        def proj(Wv, c0, src, srctok, evac):
            for mh in range(2):
                s = next_slot()
                wv = slots[s][:, 0:4096].rearrange("p (k f) -> p k f", k=8)
                P.add("pool", (lambda e, wv=wv, mh=mh: e.dma_start(out=wv, in_=Wv[:, :, c0 + mh * 512:c0 + (mh + 1) * 512])),
                      writes=stoks(s, 0, 4), lane=("w", s, 0))
                for mi in range(4):
                    m = mh * 4 + mi
                    for tb in range(2):
                        bank = ctr["u"] % 2
                        ctr["u"] += 1
                        Pp = PS[:, bank, :]
                        for k in range(8):
                            mm(Pp, wv[:, k, mi * 128:(mi + 1) * 128], src[:, k, tbs(tb)], k == 0, k == 7,
                               stoks(s, 0, 4) + [(srctok, k, tb)], pst(bank))
                        evac(m, tb, Pp, bank)

        def evac_to_F(m, tb, Pp, bank):
            blk = 2 * m + tb
            P.add("act", lambda e: e.copy(out=Fr[:, blk, :], in_=Pp), reads=pst(bank), writes=[("F", blk)])

        def ffn(l, w):
            gpre = gidx(l, 0 if w == 0 else 4)
            gpost = gidx(l, 1 if w == 0 else 5)
            for tb in range(2):
                prenorm(gpre, tb)
            if DBG["ffn_stage"] <= 1:
                return
            Wg = wview(wg_d[l, w])
            Wu = wview(wu_d[l, w])
            Wd = wd_d[l, w].rearrange("(f p) m -> p f m", p=128)
            for hf in range(2):
                for (f0, n) in ((0, 3), (3, 3), (6, 3), (9, 2)):
                    s = next_slot()
                    gv = slots[s][:, 0:8 * n * 128].rearrange("p (k f) -> p k f", k=8)
                    uv = slots[s][:, 3072:3072 + 8 * n * 128].rearrange("p (k f) -> p k f", k=8)
                    fa = hf * 11 + f0
                    P.add("pool", (lambda e, gv=gv, fa=fa, n=n: e.dma_start(out=gv, in_=Wg[:, :, fa * 128:(fa + n) * 128])),
                          writes=stoks(s, 0, 3), lane=("w", s, 0))
                    P.add("pool", (lambda e, uv=uv, fa=fa, n=n: e.dma_start(out=uv, in_=Wu[:, :, fa * 128:(fa + n) * 128])),
                          writes=stoks(s, 3, 6), lane=("w", s, 3))
                    for fi in range(n):
                        fl = f0 + fi
                        for tb in range(2):
                            par = ctr["u"] % 2
                            ctr["u"] += 1
                            G = PS[:, 2 * par, :]
                            U = PS[:, 2 * par + 1, :]
                            for k in range(8):
                                mm(G, gv[:, k, fi * 128:(fi + 1) * 128], H[:, k, tbs(tb)], k == 0, k == 7,
                                   stoks(s, 0, 3) + [("H", k, tb)], pst(2 * par))
                            for k in range(8):
                                mm(U, uv[:, k, fi * 128:(fi + 1) * 128], H[:, k, tbs(tb)], k == 0, k == 7,
                                   stoks(s, 3, 6) + [("H", k, tb)], pst(2 * par + 1))
                            P.add("act", (lambda e, G=G, par=par: e.activation(out=sg[par][:], in_=G, func=AF.Silu)),
                                  reads=pst(2 * par), writes=[("sg", par)])
                            P.add("dve", (lambda e, U=U, par=par, fl=fl, tb=tb: e.tensor_tensor(
                                out=A[:, fl, tbs(tb)], in0=U, in1=sg[par][:], op=ALU.mult)),
                                reads=pst(2 * par + 1) + [("sg", par)], writes=[("A", fl, tb)])
                    if DBG["ffn_stage"] <= 2:
                        return
                for m0 in (0, 4):
                    s = next_slot()
                    dv = slots[s][:, 0:5632].rearrange("p (f m) -> p f m", f=11)
                    P.add("pool", (lambda e, dv=dv, hf=hf, m0=m0: e.dma_start(
                        out=dv, in_=Wd[:, hf * 11:(hf + 1) * 11, m0 * 128:(m0 + 4) * 128])),
                        writes=stoks(s, 0, 6), lane=("w", s, 0))
                    for mi in range(4):
                        m = m0 + mi
                        for tb in range(2):
                            par = ctr["u"] % 2
                            ctr["u"] += 1
                            Dp = PS[:, 4 + par, :]
                            for fl in range(11):
                                mm(Dp, dv[:, fl, mi * 128:(mi + 1) * 128], A[:, fl, tbs(tb)], fl == 0, fl == 10,
                                   stoks(s, 0, 6) + [("A", fl, tb)], pst(4 + par))
                            blk = 2 * m + tb
                            if hf == 0:
                                P.add("act", (lambda e, Dp=Dp, blk=blk: e.copy(out=Fr[:, blk, :], in_=Dp)),
                                      reads=pst(4 + par), writes=[("F", blk)])
                            else:
                                P.add("dve", (lambda e, Dp=Dp, blk=blk: e.tensor_tensor(
                                    out=Fr[:, blk, :], in0=Dp, in1=Fr[:, blk, :], op=ALU.add)),
                                    reads=pst(4 + par) + [("F", blk)], writes=[("F", blk)])
                if DBG["ffn_stage"] <= 3:
                    return
            for tb in range(2):
                postnorm(gpost, tb)

        def conv_mixer(l, c):
            for tb in range(2):
                prenorm(gidx(l, 2), tb)
            Win = wview(wci_d[l])
            for jp in range(4):
                s = next_slot()
                parts = [slots[s][:, pt * 2048:(pt + 1) * 2048].rearrange("p (k f) -> p k f", k=8) for pt in range(3)]
                for pt in range(3):
                    P.add("pool", (lambda e, pt=pt, jp=jp, parts=parts: e.dma_start(
                        out=parts[pt], in_=Win[:, :, pt * 1024 + jp * 256:pt * 1024 + (jp + 1) * 256])),
                        writes=stoks(s, 2 * pt, 2 * pt + 2), lane=("w", s, 2 * pt))
                for jj in range(2):
                    j = 2 * jp + jj
                    cub = j % 2
                    cu = Fr[:, cub * 3:cub * 3 + 3, :].rearrange("p a b -> p (a b)")
                    cutoks = [("F", cub * 3 + i) for i in range(3)]
                    if c == 0:
                        P.add("dve", (lambda e, cu=cu: e.memset(cu[:, 0:2], 0.0)), writes=cutoks)
                    else:
                        P.add("dve", (lambda e, cu=cu, j=j: e.tensor_copy(out=cu[:, 0:2], in_=halo[:, l, j, :])),
                              reads=[("halo", l, j)], writes=cutoks)
                    w0 = 200 + (l * 3 + 0) * 8 + j
                    w1 = 200 + (l * 3 + 1) * 8 + j
                    w2 = 200 + (l * 3 + 2) * 8 + j
                    for tb in range(2):
                        par = ctr["u"] % 2
                        ctr["u"] += 1
                        Pb, Pc, Pu = (PS[:, 3 * par + i, :] for i in range(3))
                        for pt in range(3):
                            for k in range(8):
                                mm(PS[:, 3 * par + pt, :], parts[pt][:, k, jj * 128:(jj + 1) * 128], H[:, k, tbs(tb)],
                                   k == 0, k == 7, stoks(s, 2 * pt, 2 * pt + 2) + [("H", k, tb)], pst(3 * par + pt))
                        ucp = Fr[:, 6 + par, :]
                        z = Fr[:, 8 + par, :]
                        o2 = 2 + tb * 512
                        P.add("act", (lambda e, ucp=ucp, Pu=Pu: e.copy(out=ucp, in_=Pu)),
                              reads=pst(3 * par + 2), writes=[("F", 6 + par)])
                        P.add("dve", (lambda e, cu=cu, o2=o2, Pc=Pc, ucp=ucp: e.tensor_tensor(
                            out=cu[:, o2:o2 + 512], in0=Pc, in1=ucp, op=ALU.mult)),
                            reads=pst(3 * par + 1) + [("F", 6 + par)], writes=cutoks)
                        P.add("act", (lambda e, z=z, cu=cu, o2=o2, w2=w2: e.mul(out=z, in_=cu[:, o2:o2 + 512],
                                                                               mul=cols[:, w2:w2 + 1])),
                              reads=cutoks + ["cols"], writes=[("F", 8 + par)])
                        P.add("dve", (lambda e, z=z, cu=cu, o2=o2, w1=w1: e.scalar_tensor_tensor(
                            out=z, in0=cu[:, o2 - 1:o2 + 511], scalar=cols[:, w1:w1 + 1], in1=z,
                            op0=ALU.mult, op1=ALU.add)), reads=cutoks + ["cols", ("F", 8 + par)],
                            writes=[("F", 8 + par)])
                        P.add("dve", (lambda e, z=z, cu=cu, o2=o2, w0=w0: e.scalar_tensor_tensor(
                            out=z, in0=cu[:, o2 - 2:o2 + 510], scalar=cols[:, w0:w0 + 1], in1=z,
                            op0=ALU.mult, op1=ALU.add)), reads=cutoks + ["cols", ("F", 8 + par)],
                            writes=[("F", 8 + par)])
                        P.add("dve", (lambda e, z=z, Pb=Pb, j=j, tb=tb: e.tensor_tensor(
                            out=A[:, j, tbs(tb)], in0=Pb, in1=z, op=ALU.mult)),
                            reads=pst(3 * par) + [("F", 8 + par)], writes=[("A", j, tb)])
                    if c == 0:
                        P.add("dve", (lambda e, cu=cu, j=j: e.tensor_copy(out=halo[:, l, j, :], in_=cu[:, 1024:1026])),
                              reads=cutoks, writes=[("halo", l, j)])
            proj(wview(wco_d[l]), 0, A, "A", evac_to_F)
            for tb in range(2):
                postnorm(gidx(l, 3), tb)

        def kv_proj(c):
            for tb in range(2):
                prenorm(192, tb)
            Wkv = wview(wkv_d)

            def evac_k(m, tb, Pp, bank):
                P.add("act", lambda e: e.copy(out=KT[:, m, c * TCH + tb * 512:c * TCH + (tb + 1) * 512], in_=Pp),
                      reads=pst(bank), writes=[("K", m, c)])

            proj(Wkv, 0, H, "H", evac_k)
            for eh in range(2):
                s = next_slot()
                wv = slots[s][:, 0:4096].rearrange("p (k f) -> p k f", k=8)
                P.add("pool", (lambda e, wv=wv, eh=eh: e.dma_start(out=wv, in_=Wkv[:, :, D + eh * 512:D + (eh + 1) * 512])),
                      writes=stoks(s, 0, 4), lane=("w", s, 0))
                for tt in range(8):
                    bank = ctr["u"] % 2
                    ctr["u"] += 1
                    Pp = PS[:, bank, :]
                    for k in range(8):
                        mm(Pp, H[:, k, tt * 128:(tt + 1) * 128], wv[:, k, :], k == 0, k == 7,
                           stoks(s, 0, 4) + [("H", k, tt // 4)], pst(bank))
                    kt = c * 8 + tt
                    if tt % 2 == 0:
                        P.add("act", (lambda e, kt=kt, eh=eh, Pp=Pp: e.copy(out=V[:, kt, eh * 512:(eh + 1) * 512], in_=Pp)),
                              reads=pst(bank), writes=[("V", kt)])
                    else:
                        P.add("dve", (lambda e, kt=kt, eh=eh, Pp=Pp: e.tensor_copy(out=V[:, kt, eh * 512:(eh + 1) * 512], in_=Pp)),
                              reads=pst(bank), writes=[("V", kt)])

        def attn(l, c):
            jl = l - 2
            for tb in range(2):
                prenorm(gidx(l, 2), tb)

            def evac_q(m, tb, Pp, bank):
                P.add("act", lambda e: e.mul(out=A[:, m, tbs(tb)], in_=Pp, mul=0.125),
                      reads=pst(bank), writes=[("A", m, tb)])

            proj(wview(wq_d[jl]), 0, H, "H", evac_q)

            steps = []
            for hd in range(8):
                for u in range(4 * c, 4 * c + 4):
                    for kt in range(2 * u + 2):
                        steps.append((hd, u, kt))
            Eb = [Fr[:, r, :].bitcast(BF16)[:, 0:512].rearrange("p (c q) -> p c q", c=2) for r in range(3)]
            neglam = small[:, 8 + jl:9 + jl]
            gs = small[:, 10 + jl:11 + jl]
            tri_b = tri[:].unsqueeze(1).to_broadcast([128, 2, 128])

            def geom(i):
                hd, u, kt = steps[i]
                d = kt - 2 * u
                q0 = 128 if d == 1 else 0
                ql = (u - 4 * c) * 256
                return hd, u, kt, d, q0, ql

            def emit_S(i):
                hd, u, kt, d, q0, ql = geom(i)
                par = i % 2
                for cm in range(2):
                    mm(PS[:, 2 * par + cm, q0:256],
                       KT[cm * 64:(cm + 1) * 64, hd, kt * 128:(kt + 1) * 128],
                       A[cm * 64:(cm + 1) * 64, hd, ql + q0:ql + 256], True, True,
                       [("K", hd, kt // 8), ("A", hd, ql // 512)], pst(2 * par + cm))

            def emit_exp(i):
                hd, u, kt, d, q0, ql = geom(i)
                par = i % 2
                r = i % 3
                P.add("act", lambda e: e.activation(out=Eb[r][:, :, q0:256],
                                                    in_=PS[:, 2 * par:2 * par + 2, q0:256], func=AF.Exp),
                      reads=pst(2 * par) + pst(2 * par + 1), writes=[("F", r)])
                if d >= 0:
                    P.add("dve", lambda e: e.tensor_tensor(out=Eb[r][:, :, q0:q0 + 128], in0=Eb[r][:, :, q0:q0 + 128],
                                                           in1=tri_b, op=ALU.mult),
                          reads=[("F", r), "tri"], writes=[("F", r)])

            pending = []

            def emit_PV(i, it):
                hd, u, kt, d, q0, ql = geom(i)
                r = i % 3
                ipar = it % 2
                Ab = PS[:, 4, :].rearrange("p (c q) -> p c q", c=2)
                Sb = PS[:, 5, :].rearrange("p (c q) -> p c q", c=2)
                first = kt == 0
                last = kt == 2 * u + 1
                if q0 == 0:
                    E2 = Fr[:, r, :].bitcast(BF16)[:, 0:512]
                    mm(PS[:, 4, :], V[:, kt, hd * 128:(hd + 1) * 128], E2, first, last,
                       [("V", kt), ("F", r)], pst(4))
                    mm(PS[:, 5, :], ones[:], E2, first, last, [("F", r), "ones"], pst(5))
                else:
                    for cm in range(2):
                        mm(Ab[:, cm, q0:256], V[:, kt, hd * 128:(hd + 1) * 128], Eb[r][:, cm, q0:256], first,
                           last and cm == 1, [("V", kt), ("F", r)], pst(4))
                    for cm in range(2):
                        mm(Sb[:, cm, q0:256], ones[:], Eb[r][:, cm, q0:256], first, last and cm == 1,
                           [("F", r), "ones"], pst(5))
                if last:
                    b0 = 3 + 3 * ipar
                    rr = Fr[:, b0, :]
                    tt_ = Fr[:, b0 + 1, :]
                    ob = Fr[:, b0 + 2, 0:256]
                    P.add("dve", lambda e: e.reciprocal(out=rr, in_=PS[:, 5, :]),
                          reads=pst(5), writes=[("F", b0)])
                    P.add("dve", lambda e: e.tensor_tensor(out=tt_, in0=PS[:, 4, :], in1=rr, op=ALU.mult),
                          reads=pst(4) + [("F", b0)], writes=[("F", b0 + 1)])
                    P.add("dve", lambda e: e.scalar_tensor_tensor(out=ob, in0=tt_[:, 256:512], scalar=neglam,
                                                                  in1=tt_[:, 0:256], op0=ALU.mult, op1=ALU.add),
                          reads=[("F", b0 + 1), "small"], writes=[("F", b0 + 2)])
                    r2 = ctr["sq"] % 3
                    ctr["sq"] += 1
                    P.add("act", lambda e: e.activation(out=sq[r2][:, 0:256], in_=ob, func=AF.Square),
                          reads=[("F", b0 + 2)], writes=[("sq", r2)])

                    def tail():
                        b = 6 + ctr["stat"] % 2
                        ctr["stat"] += 1
                        ssb = PS[:, b, 0:256]
                        mm(ssb, ones[:], sq[r2][:, 0:256], True, True, [("sq", r2), "ones"], pst(b))
                        P.add("act", lambda e: e.activation(out=sd[:, 0:256], in_=ssb, func=AF.Sqrt,
                                                            bias=eps_t[:, 0:1], scale=1.0 / 128.0),
                              reads=pst(b) + ["eps"], writes=["sd"])
                        rb = ctr["rstd"] % 2
                        ctr["rstd"] += 1
                        P.add("dve", lambda e: e.reciprocal(out=rstd[rb][:, 0:256], in_=sd[:, 0:256]),
                              reads=["sd"], writes=[("rstd", rb)])
                        P.add("dve", lambda e: e.scalar_tensor_tensor(
                            out=H[:, hd, ql:ql + 256], in0=ob, scalar=gs, in1=rstd[rb][:, 0:256],
                            op0=ALU.mult, op1=ALU.mult),
                            reads=[("F", b0 + 2), ("rstd", rb), "small"], writes=[("H", hd, ql // 512)])

                    pending.append([4, tail])

            n = len(steps)
            emit_S(0)
            it = 0
            for i in range(n):
                if i + 1 < n:
                    emit_S(i + 1)
                emit_exp(i)
                emit_PV(i, it)
                if steps[i][2] == 2 * steps[i][1] + 1:
                    it += 1
                for pnd in list(pending):
                    pnd[0] -= 1
                    if pnd[0] <= 0:
                        pending.remove(pnd)
                        pnd[1]()
            for pnd in pending:
                pnd[1]()
            proj(wview(wo_d[jl]), 0, H, "H", evac_to_F)
            for tb in range(2):
                postnorm(gidx(l, 3), tb)

        def load_x(c):
            for tt in range(8):
                P.add("sp", (lambda e, tt=tt: e.dma_start(
                    out=Fr[:, 2 * tt:2 * tt + 2, :].rearrange("p a b -> p (a b)"),
                    in_=x_d[c * TCH + tt * 128:c * TCH + (tt + 1) * 128, :])),
                    writes=[("F", 2 * tt), ("F", 2 * tt + 1)], lane=("x", tt % 4))
            for j in range(8):
                for g in range(2):
                    bank = ctr["u"] % 4
                    ctr["u"] += 1
                    for ti in range(4):
                        tt = g * 4 + ti
                        xs = Fr[:, 2 * tt:2 * tt + 2, :].rearrange("p a b -> p (a b)")
                        P.add("pe", (lambda e, bank=bank, ti=ti, xs=xs, j=j: e.transpose(
                            PS[:, bank, ti * 128:(ti + 1) * 128], xs[:, j * 128:(j + 1) * 128], ident)),
                            reads=[("F", 2 * tt), ("F", 2 * tt + 1), "cst"], writes=pst(bank))
                    if (j + g) % 2 == 0:
                        P.add("act", (lambda e, bank=bank, j=j, g=g: e.copy(out=X[:, j, tbs(g)], in_=PS[:, bank, :])),
                              reads=pst(bank), writes=[("X", j, g)])
                    else:
                        P.add("dve", (lambda e, bank=bank, j=j, g=g: e.tensor_copy(out=X[:, j, tbs(g)], in_=PS[:, bank, :])),
                              reads=pst(bank), writes=[("X", j, g)])

        def store_x(c):
            for tt in range(8):
                for g in range(2):
                    bank = ctr["u"] % 4
                    ctr["u"] += 1
                    for ji in range(4):
                        j = g * 4 + ji
                        P.add("pe", (lambda e, bank=bank, ji=ji, j=j, tt=tt: e.transpose(
                            PS[:, bank, ji * 128:(ji + 1) * 128], X[:, j, tt * 128:(tt + 1) * 128], ident)),
                            reads=[("X", j, tt // 4), "cst"], writes=pst(bank))
                    blk = 2 * tt + g
                    if g == 0:
                        P.add("act", (lambda e, bank=bank, blk=blk: e.copy(out=Fr[:, blk, :], in_=PS[:, bank, :])),
                              reads=pst(bank), writes=[("F", blk)])
                    else:
                        P.add("dve", (lambda e, bank=bank, blk=blk: e.tensor_copy(out=Fr[:, blk, :], in_=PS[:, bank, :])),
                              reads=pst(bank), writes=[("F", blk)])
                P.add("sp", (lambda e, tt=tt: e.dma_start(
                    out=out_d[c * TCH + tt * 128:c * TCH + (tt + 1) * 128, :],
                    in_=Fr[:, 2 * tt:2 * tt + 2, :].rearrange("p a b -> p (a b)"))),
                    reads=[("F", 2 * tt), ("F", 2 * tt + 1)], writes=[("OUT", c, tt)], lane=("o", tt % 4))

        for c in range(NCH):
            load_x(c)
            nsub = 0

            def go():
                return nsub < nstop

            for l in range(4):
                if go():
                    ffn(l, 0)
                nsub += 1
                if go():
                    if l < 2:
                        conv_mixer(l, c)
                    else:
                        attn(l, c)
                nsub += 1
                if go():
                    ffn(l, 1)
                nsub += 1
                if l == 1 and go():
                    kv_proj(c)
            store_x(c)
        P.add("sp", None, reads=[("OUT", c, tt) for c in range(NCH) for tt in range(8)])
        P.emit(nc, st)
    nc._prog_stats = P.stats
    return nc


_CACHE = {}


def _consts():
    c = np.zeros((128, 256), np.float32)
    c[:, 0:128] = np.eye(128, dtype=np.float32)
    c[:, 128:256] = np.triu(np.ones((128, 128), np.float32))
    return c


def kernel(**inputs):
    nstop = int(inputs.pop("_nstop", 999))
    ncores = int(inputs.pop("_ncores", NB))
    if nstop not in _CACHE:
        _CACHE[nstop] = build_nc(nstop)
    nc = _CACHE[nstop]
    names = ["g_norm", "w_ffn_gate", "w_ffn_up", "w_ffn_down", "w_conv_in", "w_conv", "w_conv_out", "g_kv",
             "w_kv", "w_q", "lambda_q1", "lambda_k1", "lambda_q2", "lambda_k2", "g_subln", "w_o"]
    shared = {k: np.asarray(inputs[k], dtype=np.float32) for k in names}
    if DBG["lite"]:
        for k in ("w_ffn_gate", "w_ffn_up", "w_ffn_down"):
            shared[k] = shared[k][:1, :1]
        for k in ("w_conv_in", "w_conv_out", "w_q", "w_o"):
            shared[k] = shared[k][:1]
    shared = {k: np.ascontiguousarray(v) for k, v in shared.items()}
    shared["consts"] = _consts()
    x = np.asarray(inputs["x"], dtype=np.float32)
    in_maps = []
    for b in range(ncores):
        m = dict(shared)
        m["x"] = np.ascontiguousarray(x[b])
        in_maps.append(m)
    res = run_bass_kernel_spmd(nc, in_maps, core_ids=list(range(ncores)))
    out = np.stack([np.asarray(res.results[b]["out"], dtype=np.float32) for b in range(ncores)], axis=0)
    return out
```

```python
import math
from contextlib import ExitStack

import numpy as np
import concourse.bass as bass
import concourse.mybir as mybir
from concourse.bass_utils import run_bass_kernel_spmd

F32 = mybir.dt.float32
BF16 = mybir.dt.bfloat16
AF = mybir.ActivationFunctionType
ALU = mybir.AluOpType
AX = mybir.AxisListType

D = 1024
SEQ = 2048
NB = 8
DFF = 2816
EPS = 1e-6
TCH = 1024
NCH = SEQ // TCH
SAME_ENG_SYNC = True


class _Op:
    __slots__ = ("eng", "fn", "dma", "lane", "deps", "idx", "ms", "cnt", "waits")


class Prog:
    def __init__(self):
        self.ops = []
        self.last_w = {}
        self.readers = {}
        self.lane_last = {}

    def add(self, eng, fn, reads=(), writes=(), lane=None):
        op = _Op()
        op.eng = eng
        op.fn = fn
        op.dma = lane is not None
        op.lane = lane
        op.idx = len(self.ops)
        op.ms = False
        op.cnt = 0
        deps = set()
        for t in reads:
            w = self.last_w.get(t)
            if w is not None:
                deps.add(w)
        for t in writes:
            w = self.last_w.get(t)
            if w is not None:
                deps.add(w)
            rs = self.readers.get(t)
            if rs:
                deps.update(rs)
        if lane is not None:
            p = self.lane_last.get(lane)
            if p is not None:
                deps.add(p)
            self.lane_last[lane] = op.idx
        for t in reads:
            self.readers.setdefault(t, []).append(op.idx)
        for t in writes:
            self.last_w[t] = op.idx
            self.readers[t] = []
        deps.discard(op.idx)
        op.deps = deps
        self.ops.append(op)
        return op

    @staticmethod
    def _needs_sync(p, op):
        if p.dma:
            return True
        if p.eng != op.eng:
            return True
        if p.eng == "pe":
            return False
        return SAME_ENG_SYNC

    def emit(self, nc, stack):
        ops = self.ops
        for op in ops:
            best = {}
            for d in op.deps:
                p = ops[d]
                if (not p.dma) and self._needs_sync(p, op):
                    if best.get(p.eng, -1) < d:
                        best[p.eng] = d
            for d in best.values():
                ops[d].ms = True
        eng_cnt = {}
        lane_cnt = {}
        for op in ops:
            if op.dma:
                lane_cnt[op.lane] = lane_cnt.get(op.lane, 0) + 1
                op.cnt = 16 * lane_cnt[op.lane]
            elif op.ms:
                eng_cnt[op.eng] = eng_cnt.get(op.eng, 0) + 1
                op.cnt = eng_cnt[op.eng]
        waited = {}
        for op in ops:
            need = {}
            for d in op.deps:
                p = ops[d]
                if not self._needs_sync(p, op):
                    continue
                key = ("lane", p.lane) if p.dma else ("eng", p.eng)
                if need.get(key, 0) < p.cnt:
                    need[key] = p.cnt
                assert p.dma or p.cnt > 0 or any(
                    (not ops[d2].dma) and ops[d2].eng == p.eng and d2 > d for d2 in op.deps)
            wd = waited.setdefault(op.eng, {})
            op.waits = []
            for k, v in need.items():
                if wd.get(k, 0) < v:
                    op.waits.append((k, v))
                    wd[k] = v
        sems = {}
        for e in sorted(eng_cnt):
            sems[("eng", e)] = stack.enter_context(nc.semaphore("s_" + e))
        for i, l in enumerate(lane_cnt):
            sems[("lane", l)] = stack.enter_context(nc.semaphore("l_%d" % i))
        per = {}
        for op in ops:
            per.setdefault(op.eng, []).append(op)
        self.stats = {e: len(v) for e, v in per.items()}
        self.stats["sems"] = len(sems)

        def mk(name):
            lst = per.get(name, [])

            def body(e):
                for op in lst:
                    for k, v in op.waits:
                        e.wait_ge(sems[k], v)
                    if op.fn is None:
                        continue
                    ins = op.fn(e)
                    if op.dma:
                        ins.then_inc(sems[("lane", op.lane)], 16)
                    elif op.ms:
                        ins.then_inc(sems[("eng", op.eng)], 1)

            return body

        with nc.Block() as block:
            block.tensor(mk("pe"))
            block.scalar(mk("act"))
            block.vector(mk("dve"))
            block.gpsimd(mk("pool"))
            block.sync(mk("sp"))


def lambda_init(layer):
    return 0.8 - 0.6 * math.exp(-0.3 * layer)


DBG = {"ffn_stage": 99, "lite": False}


def build_nc(nstop=999):
    nc = bass.Bass("TRN2", target_bir_lowering=False)
    nl4 = 1 if DBG["lite"] else 4
    nl2 = 1 if DBG["lite"] else 2

    def din(name, shape):
        return nc.dram_tensor(name, list(shape), F32, kind="ExternalInput").ap()

    x_d = din("x", (SEQ, D))
    gnorm_d = din("g_norm", (4, 6, D))
    wg_d = din("w_ffn_gate", (nl4, nl2, D, DFF))
    wu_d = din("w_ffn_up", (nl4, nl2, D, DFF))
    wd_d = din("w_ffn_down", (nl4, nl2, DFF, D))
    wci_d = din("w_conv_in", (nl2, D, 3 * D))
    wc_d = din("w_conv", (2, 3, D))
    wco_d = din("w_conv_out", (nl2, D, D))
    gkv_d = din("g_kv", (D,))
    wkv_d = din("w_kv", (D, 2 * D))
    wq_d = din("w_q", (nl2, D, D))
    lq1_d = din("lambda_q1", (2, 64))
    lk1_d = din("lambda_k1", (2, 64))
    lq2_d = din("lambda_q2", (2, 64))
    lk2_d = din("lambda_k2", (2, 64))
    gsub_d = din("g_subln", (2, 128))
    wo_d = din("w_o", (nl2, D, D))
    cst_d = din("consts", (128, 256))
    out_d = nc.dram_tensor("out", [SEQ, D], F32, kind="ExternalOutput").ap()

    P = Prog()
    with ExitStack() as st:
        def sb(name, shape, dt):
            return st.enter_context(nc.sbuf_tensor(name, list(shape), dt))

        X = sb("X", (128, 8, TCH), F32)
        H = sb("H", (128, 8, TCH), BF16)
        Fr = sb("Fr", (128, 16, 512), F32)
        A = sb("A", (128, 11, TCH), BF16)
        KT = sb("KT", (128, 8, SEQ), BF16)
        V = sb("V", (128, 16, D), BF16)
        slots = [sb("slot%d" % i, (128, 6144), BF16) for i in range(2)]
        cst = sb("cst", (128, 256), F32)
        tri = sb("tri", (128, 128), BF16)
        ones = sb("ones", (128, 128), BF16)
        cols = sb("cols", (128, 256), F32)
        eps_t = sb("eps", (128, 1), F32)
        sq = [sb("sq%d" % i, (128, 512), BF16) for i in range(3)]
        sd = sb("sd", (128, 512), F32)
        rstd = [sb("rstd%d" % i, (128, 512), F32) for i in range(2)]
        sg = [sb("sg%d" % i, (128, 512), F32) for i in range(2)]
        halo = sb("halo", (128, 2, 8, 2), F32)
        small = sb("small", (128, 16), F32)
        PS = st.enter_context(nc.psum_tensor("PS", [128, 8, 512], F32))
        ident = cst[:, 0:128]

        def pst(b):
            return [("ps", b, 0), ("ps", b, 1)]

        def tbs(tb):
            return slice(tb * 512, (tb + 1) * 512)

        def mm(out, lhsT, rhs, start, stop, reads, writes):
            P.add("pe", lambda e: e.matmul(out, lhsT, rhs, start=start, stop=stop), reads, writes)

        ctr = {"slot": 0, "stat": 0, "sq": 0, "rstd": 0, "u": 0}

        def next_slot():
            s = ctr["slot"] % 2
            ctr["slot"] += 1
            return s

        def stoks(s, a, b):
            return [("slot", s, i) for i in range(a, b)]

        P.add("dve", lambda e: e.memset(ones[:], 1.0), writes=["ones"])
        P.add("dve", lambda e: e.memset(eps_t[:], EPS), writes=["eps"])
        P.add("sp", lambda e: e.dma_start(out=cst[:], in_=cst_d), writes=["cst"], lane="cst")
        P.add("dve", lambda e: e.tensor_copy(out=tri[:], in_=cst[:, 128:256]), reads=["cst"], writes=["tri"])
        gview = gnorm_d.rearrange("l n (j p) -> (l n j) p", p=128)
        P.add("sp", lambda e: e.dma_start(out=Fr[0:128, 0, 0:128], in_=gview[0:128, :]),
              writes=[("F", 0)], lane="p0")
        P.add("sp", lambda e: e.dma_start(out=Fr[0:64, 1, 0:128], in_=gview[128:192, :]),
              writes=[("F", 1)], lane="p1")
        P.add("sp", lambda e: e.dma_start(out=Fr[64:72, 1, 0:128], in_=gkv_d.rearrange("(j p) -> j p", p=128)),
              writes=[("F", 1)], lane="p2")
        P.add("sp", lambda e: e.dma_start(out=Fr[72:120, 1, 0:128],
                                          in_=wc_d.rearrange("l w (j p) -> (l w j) p", p=128)),
              writes=[("F", 1)], lane="p3")
        P.add("sp", lambda e: e.dma_start(out=Fr[120:122, 1, 0:128], in_=gsub_d),
              writes=[("F", 1)], lane="p4")
        P.add("pe", lambda e: e.transpose(PS[:, 0, 0:128], Fr[0:128, 0, 0:128], ident),
              reads=[("F", 0), "cst"], writes=pst(0))
        P.add("pe", lambda e: e.transpose(PS[:, 0, 128:250], Fr[0:122, 1, 0:128], cst[0:122, 0:122]),
              reads=[("F", 1), "cst"], writes=pst(0))
        P.add("dve", lambda e: e.tensor_copy(out=cols[:, 0:250], in_=PS[:, 0, 0:250]), reads=pst(0), writes=["cols"])
        gv4 = cols[:, 0:192].rearrange("p (l n j) -> p l n j", l=4, n=6)
        for n in (1, 5):
            P.add("dve", (lambda e, n=n: e.tensor_scalar(out=gv4[:, :, n, :], in0=gv4[:, :, n, :], scalar1=0.5,
                                                         scalar2=None, op0=ALU.mult)),
                  reads=["cols"], writes=["cols"])
        lv = Fr[:, 2, :].rearrange("p (a b) -> p a b", a=4)
        for i, ld in enumerate((lq1_d, lk1_d, lq2_d, lk2_d)):
            P.add("sp", (lambda e, i=i, ld=ld: e.dma_start(
                out=lv[:, i, :], in_=ld.rearrange("a b -> (a b)").partition_broadcast(128))),
                writes=[("F", 2)], lane="l%d" % i)
        pr = Fr[:, 3, 0:256].rearrange("p (a b) -> p a b", a=2)
        P.add("dve", lambda e: e.tensor_tensor(out=pr[:, 0, :], in0=lv[:, 0, :], in1=lv[:, 1, :], op=ALU.mult),
              reads=[("F", 2)], writes=[("F", 3)])
        P.add("dve", lambda e: e.tensor_tensor(out=pr[:, 1, :], in0=lv[:, 2, :], in1=lv[:, 3, :], op=ALU.mult),
              reads=[("F", 2)], writes=[("F", 3)])
        P.add("dve", lambda e: e.reduce_sum(out=small[:, 0:4],
                                            in_=Fr[:, 3, 0:256].rearrange("p (a b) -> p a b", a=4), axis=AX.X),
              reads=[("F", 3)], writes=["small"])
        P.add("act", lambda e: e.activation(out=small[:, 4:8], in_=small[:, 0:4], func=AF.Exp),
              reads=["small"], writes=["small"])
        P.add("dve", lambda e: e.tensor_tensor(out=small[:, 8:10], in0=small[:, 6:8], in1=small[:, 4:6],
                                               op=ALU.subtract), reads=["small"], writes=["small"])
        for jl in range(2):
            li = lambda_init(jl + 2)
            P.add("dve", (lambda e, jl=jl, li=li: e.tensor_scalar(out=small[:, 8 + jl:9 + jl], in0=small[:, 8 + jl:9 + jl],
                                                                  scalar1=-li, scalar2=None, op0=ALU.add)),
                  reads=["small"], writes=["small"])
            P.add("dve", (lambda e, jl=jl, li=li: e.tensor_scalar(out=small[:, 10 + jl:11 + jl],
                                                                  in0=cols[:, 248 + jl:249 + jl],
                                                                  scalar1=1.0 - li, scalar2=None, op0=ALU.mult)),
                  reads=["small", "cols"], writes=["small"])

        def gidx(l, n):
            return (l * 6 + n) * 8

        def rstd_of(src_fn, src_tok, ncols, invd, nj):
            b = 6 + ctr["stat"] % 2
            ctr["stat"] += 1
            ssb = PS[:, b, 0:ncols]
            for j in range(nj):
                r = ctr["sq"] % 3
                ctr["sq"] += 1
                P.add("act", (lambda e, j=j, r=r: e.activation(out=sq[r][:, 0:ncols], in_=src_fn(j), func=AF.Square)),
                      reads=[src_tok(j)], writes=[("sq", r)])
                mm(ssb, ones[:], sq[r][:, 0:ncols], j == 0, j == nj - 1, [("sq", r), "ones"], pst(b))
            P.add("act", lambda e: e.activation(out=sd[:, 0:ncols], in_=ssb, func=AF.Sqrt, bias=eps_t[:, 0:1],
                                                scale=invd), reads=pst(b) + ["eps"], writes=["sd"])
            rb = ctr["rstd"] % 2
            ctr["rstd"] += 1
            P.add("dve", lambda e: e.reciprocal(out=rstd[rb][:, 0:ncols], in_=sd[:, 0:ncols]),
                  reads=["sd"], writes=[("rstd", rb)])
            return rstd[rb][:, 0:ncols], ("rstd", rb)

        def prenorm(gbase, tb):
            rs, rtok = rstd_of(lambda j: X[:, j, tbs(tb)], lambda j: ("X", j, tb), 512, 1.0 / D, 8)
            for j in range(8):
                P.add("dve", (lambda e, j=j: e.scalar_tensor_tensor(
                    out=H[:, j, tbs(tb)], in0=X[:, j, tbs(tb)], scalar=cols[:, gbase + j:gbase + j + 1], in1=rs,
                    op0=ALU.mult, op1=ALU.mult)),
                    reads=[("X", j, tb), rtok, "cols"], writes=[("H", j, tb)])

        def postnorm(gbase, tb):
            rs, rtok = rstd_of(lambda j: Fr[:, 2 * j + tb, :], lambda j: ("F", 2 * j + tb), 512, 1.0 / D, 8)
            for j in range(8):
                blk = 2 * j + tb
                P.add("dve", (lambda e, j=j, blk=blk: e.scalar_tensor_tensor(
                    out=Fr[:, blk, :], in0=Fr[:, blk, :], scalar=cols[:, gbase + j:gbase + j + 1], in1=rs,
                    op0=ALU.mult, op1=ALU.mult)),
                    reads=[("F", blk), rtok, "cols"], writes=[("F", blk)])
                P.add("dve", (lambda e, j=j, blk=blk: e.tensor_tensor(
                    out=X[:, j, tbs(tb)], in0=X[:, j, tbs(tb)], in1=Fr[:, blk, :], op=ALU.add)),
                    reads=[("F", blk), ("X", j, tb)], writes=[("X", j, tb)])

        def wview(w2d):
            return w2d.rearrange("(k p) f -> p k f", p=128)

        def proj(Wv, c0, src, srctok, evac):
            for mh in range(2):
                s = next_slot()
                wv = slots[s][:, 0:4096].rearrange("p (k f) -> p k f", k=8)
                P.add("pool", (lambda e, wv=wv, mh=mh: e.dma_start(out=wv, in_=Wv[:, :, c0 + mh * 512:c0 + (mh + 1) * 512])),
                      writes=stoks(s, 0, 4), lane=("w", s, 0))
                for mi in range(4):
                    m = mh * 4 + mi
                    for tb in range(2):
                        bank = ctr["u"] % 2
                        ctr["u"] += 1
                        Pp = PS[:, bank, :]
                        for k in range(8):
                            mm(Pp, wv[:, k, mi * 128:(mi + 1) * 128], src[:, k, tbs(tb)], k == 0, k == 7,
                               stoks(s, 0, 4) + [(srctok, k, tb)], pst(bank))
                        evac(m, tb, Pp, bank)

        def evac_to_F(m, tb, Pp, bank):
            blk = 2 * m + tb
            P.add("act", lambda e: e.copy(out=Fr[:, blk, :], in_=Pp), reads=pst(bank), writes=[("F", blk)])

        def ffn(l, w):
            gpre = gidx(l, 0 if w == 0 else 4)
            gpost = gidx(l, 1 if w == 0 else 5)
            for tb in range(2):
                prenorm(gpre, tb)
            if DBG["ffn_stage"] <= 1:
                return
            Wg = wview(wg_d[l, w])
            Wu = wview(wu_d[l, w])
            Wd = wd_d[l, w].rearrange("(f p) m -> p f m", p=128)
            for hf in range(2):
                for (f0, n) in ((0, 3), (3, 3), (6, 3), (9, 2)):
                    s = next_slot()
                    gv = slots[s][:, 0:8 * n * 128].rearrange("p (k f) -> p k f", k=8)
                    uv = slots[s][:, 3072:3072 + 8 * n * 128].rearrange("p (k f) -> p k f", k=8)
                    fa = hf * 11 + f0
                    P.add("pool", (lambda e, gv=gv, fa=fa, n=n: e.dma_start(out=gv, in_=Wg[:, :, fa * 128:(fa + n) * 128])),
                          writes=stoks(s, 0, 3), lane=("w", s, 0))
                    P.add("pool", (lambda e, uv=uv, fa=fa, n=n: e.dma_start(out=uv, in_=Wu[:, :, fa * 128:(fa + n) * 128])),
                          writes=stoks(s, 3, 6), lane=("w", s, 3))
                    for fi in range(n):
                        fl = f0 + fi
                        for tb in range(2):
                            par = ctr["u"] % 2
                            ctr["u"] += 1
                            G = PS[:, 2 * par, :]
                            U = PS[:, 2 * par + 1, :]
                            for k in range(8):
                                mm(G, gv[:, k, fi * 128:(fi + 1) * 128], H[:, k, tbs(tb)], k == 0, k == 7,
                                   stoks(s, 0, 3) + [("H", k, tb)], pst(2 * par))
                            for k in range(8):
                                mm(U, uv[:, k, fi * 128:(fi + 1) * 128], H[:, k, tbs(tb)], k == 0, k == 7,
                                   stoks(s, 3, 6) + [("H", k, tb)], pst(2 * par + 1))
                            P.add("act", (lambda e, G=G, par=par: e.activation(out=sg[par][:], in_=G, func=AF.Silu)),
                                  reads=pst(2 * par), writes=[("sg", par)])
                            P.add("dve", (lambda e, U=U, par=par, fl=fl, tb=tb: e.tensor_tensor(
                                out=A[:, fl, tbs(tb)], in0=U, in1=sg[par][:], op=ALU.mult)),
                                reads=pst(2 * par + 1) + [("sg", par)], writes=[("A", fl, tb)])
                    if DBG["ffn_stage"] <= 2:
                        return
                for m0 in (0, 4):
                    s = next_slot()
                    dv = slots[s][:, 0:5632].rearrange("p (f m) -> p f m", f=11)
                    P.add("pool", (lambda e, dv=dv, hf=hf, m0=m0: e.dma_start(
                        out=dv, in_=Wd[:, hf * 11:(hf + 1) * 11, m0 * 128:(m0 + 4) * 128])),
                        writes=stoks(s, 0, 6), lane=("w", s, 0))
                    for mi in range(4):
                        m = m0 + mi
                        for tb in range(2):
                            par = ctr["u"] % 2
                            ctr["u"] += 1
                            Dp = PS[:, 4 + par, :]
                            for fl in range(11):
                                mm(Dp, dv[:, fl, mi * 128:(mi + 1) * 128], A[:, fl, tbs(tb)], fl == 0, fl == 10,
                                   stoks(s, 0, 6) + [("A", fl, tb)], pst(4 + par))
                            blk = 2 * m + tb
                            if hf == 0:
                                P.add("act", (lambda e, Dp=Dp, blk=blk: e.copy(out=Fr[:, blk, :], in_=Dp)),
                                      reads=pst(4 + par), writes=[("F", blk)])
                            else:
                                P.add("dve", (lambda e, Dp=Dp, blk=blk: e.tensor_tensor(
                                    out=Fr[:, blk, :], in0=Dp, in1=Fr[:, blk, :], op=ALU.add)),
                                    reads=pst(4 + par) + [("F", blk)], writes=[("F", blk)])
                if DBG["ffn_stage"] <= 3:
                    return
            for tb in range(2):
                postnorm(gpost, tb)

        def conv_mixer(l, c):
            for tb in range(2):
                prenorm(gidx(l, 2), tb)
            Win = wview(wci_d[l])
            for jp in range(4):
                s = next_slot()
                parts = [slots[s][:, pt * 2048:(pt + 1) * 2048].rearrange("p (k f) -> p k f", k=8) for pt in range(3)]
                for pt in range(3):
                    P.add("pool", (lambda e, pt=pt, jp=jp, parts=parts: e.dma_start(
                        out=parts[pt], in_=Win[:, :, pt * 1024 + jp * 256:pt * 1024 + (jp + 1) * 256])),
                        writes=stoks(s, 2 * pt, 2 * pt + 2), lane=("w", s, 2 * pt))
                for jj in range(2):
                    j = 2 * jp + jj
                    cub = j % 2
                    cu = Fr[:, cub * 3:cub * 3 + 3, :].rearrange("p a b -> p (a b)")
                    cutoks = [("F", cub * 3 + i) for i in range(3)]
                    if c == 0:
                        P.add("dve", (lambda e, cu=cu: e.memset(cu[:, 0:2], 0.0)), writes=cutoks)
                    else:
                        P.add("dve", (lambda e, cu=cu, j=j: e.tensor_copy(out=cu[:, 0:2], in_=halo[:, l, j, :])),
                              reads=[("halo", l, j)], writes=cutoks)
                    w0 = 200 + (l * 3 + 0) * 8 + j
                    w1 = 200 + (l * 3 + 1) * 8 + j
                    w2 = 200 + (l * 3 + 2) * 8 + j
                    for tb in range(2):
                        par = ctr["u"] % 2
                        ctr["u"] += 1
                        Pb, Pc, Pu = (PS[:, 3 * par + i, :] for i in range(3))
                        for pt in range(3):
                            for k in range(8):
                                mm(PS[:, 3 * par + pt, :], parts[pt][:, k, jj * 128:(jj + 1) * 128], H[:, k, tbs(tb)],
                                   k == 0, k == 7, stoks(s, 2 * pt, 2 * pt + 2) + [("H", k, tb)], pst(3 * par + pt))
                        ucp = Fr[:, 6 + par, :]
                        z = Fr[:, 8 + par, :]
                        o2 = 2 + tb * 512
                        P.add("act", (lambda e, ucp=ucp, Pu=Pu: e.copy(out=ucp, in_=Pu)),
                              reads=pst(3 * par + 2), writes=[("F", 6 + par)])
                        P.add("dve", (lambda e, cu=cu, o2=o2, Pc=Pc, ucp=ucp: e.tensor_tensor(
                            out=cu[:, o2:o2 + 512], in0=Pc, in1=ucp, op=ALU.mult)),
                            reads=pst(3 * par + 1) + [("F", 6 + par)], writes=cutoks)
                        P.add("act", (lambda e, z=z, cu=cu, o2=o2, w2=w2: e.mul(out=z, in_=cu[:, o2:o2 + 512],
                                                                               mul=cols[:, w2:w2 + 1])),
                              reads=cutoks + ["cols"], writes=[("F", 8 + par)])
                        P.add("dve", (lambda e, z=z, cu=cu, o2=o2, w1=w1: e.scalar_tensor_tensor(
                            out=z, in0=cu[:, o2 - 1:o2 + 511], scalar=cols[:, w1:w1 + 1], in1=z,
                            op0=ALU.mult, op1=ALU.add)), reads=cutoks + ["cols", ("F", 8 + par)],
                            writes=[("F", 8 + par)])
                        P.add("dve", (lambda e, z=z, cu=cu, o2=o2, w0=w0: e.scalar_tensor_tensor(
                            out=z, in0=cu[:, o2 - 2:o2 + 510], scalar=cols[:, w0:w0 + 1], in1=z,
                            op0=ALU.mult, op1=ALU.add)), reads=cutoks + ["cols", ("F", 8 + par)],
                            writes=[("F", 8 + par)])
                        P.add("dve", (lambda e, z=z, Pb=Pb, j=j, tb=tb: e.tensor_tensor(
                            out=A[:, j, tbs(tb)], in0=Pb, in1=z, op=ALU.mult)),
                            reads=pst(3 * par) + [("F", 8 + par)], writes=[("A", j, tb)])
                    if c == 0:
                        P.add("dve", (lambda e, cu=cu, j=j: e.tensor_copy(out=halo[:, l, j, :], in_=cu[:, 1024:1026])),
                              reads=cutoks, writes=[("halo", l, j)])
            proj(wview(wco_d[l]), 0, A, "A", evac_to_F)
            for tb in range(2):
                postnorm(gidx(l, 3), tb)

        def kv_proj(c):
            for tb in range(2):
                prenorm(192, tb)
            Wkv = wview(wkv_d)

            def evac_k(m, tb, Pp, bank):
                P.add("act", lambda e: e.copy(out=KT[:, m, c * TCH + tb * 512:c * TCH + (tb + 1) * 512], in_=Pp),
                      reads=pst(bank), writes=[("K", m, c)])

            proj(Wkv, 0, H, "H", evac_k)
            for eh in range(2):
                s = next_slot()
                wv = slots[s][:, 0:4096].rearrange("p (k f) -> p k f", k=8)
                P.add("pool", (lambda e, wv=wv, eh=eh: e.dma_start(out=wv, in_=Wkv[:, :, D + eh * 512:D + (eh + 1) * 512])),
                      writes=stoks(s, 0, 4), lane=("w", s, 0))
                for tt in range(8):
                    bank = ctr["u"] % 2
                    ctr["u"] += 1
                    Pp = PS[:, bank, :]
                    for k in range(8):
                        mm(Pp, H[:, k, tt * 128:(tt + 1) * 128], wv[:, k, :], k == 0, k == 7,
                           stoks(s, 0, 4) + [("H", k, tt // 4)], pst(bank))
                    kt = c * 8 + tt
                    if tt % 2 == 0:
                        P.add("act", (lambda e, kt=kt, eh=eh, Pp=Pp: e.copy(out=V[:, kt, eh * 512:(eh + 1) * 512], in_=Pp)),
                              reads=pst(bank), writes=[("V", kt)])
                    else:
                        P.add("dve", (lambda e, kt=kt, eh=eh, Pp=Pp: e.tensor_copy(out=V[:, kt, eh * 512:(eh + 1) * 512], in_=Pp)),
                              reads=pst(bank), writes=[("V", kt)])

        def attn(l, c):
            jl = l - 2
            for tb in range(2):
                prenorm(gidx(l, 2), tb)

            def evac_q(m, tb, Pp, bank):
                P.add("act", lambda e: e.mul(out=A[:, m, tbs(tb)], in_=Pp, mul=0.125),
                      reads=pst(bank), writes=[("A", m, tb)])

            proj(wview(wq_d[jl]), 0, H, "H", evac_q)

            steps = []
            for hd in range(8):
                for u in range(4 * c, 4 * c + 4):
                    for kt in range(2 * u + 2):
                        steps.append((hd, u, kt))
            Eb = [Fr[:, r, :].bitcast(BF16)[:, 0:512].rearrange("p (c q) -> p c q", c=2) for r in range(3)]
            neglam = small[:, 8 + jl:9 + jl]
            gs = small[:, 10 + jl:11 + jl]
            tri_b = tri[:].unsqueeze(1).to_broadcast([128, 2, 128])

            def geom(i):
                hd, u, kt = steps[i]
                d = kt - 2 * u
                q0 = 128 if d == 1 else 0
                ql = (u - 4 * c) * 256
                return hd, u, kt, d, q0, ql

            it_of = []
            _it = 0
            for (hd_, u_, kt_) in steps:
                it_of.append(_it)
                if kt_ == 2 * u_ + 1:
                    _it += 1
            qz = [Fr[:, 9 + p_, :].bitcast(BF16)[:, 0:512].rearrange("p (c q) -> p c q", c=2) for p_ in range(2)]
            for p_ in range(2):
                P.add("pool", (lambda e, p_=p_: e.memset(Fr[:, 9 + p_, :].bitcast(BF16)[:, 0:512], 0.0)),
                      writes=[("F", 9 + p_)])

            def emit_S(i):
                hd, u, kt, d, q0, ql = geom(i)
                par = i % 2
                ipar = it_of[i] % 2
                if kt == 0:
                    P.add("pool", lambda e: e.tensor_copy(out=qz[ipar][0:64, 0, :], in_=A[0:64, hd, ql:ql + 256]),
                          reads=[("A", hd, ql // 512)], writes=[("F", 9 + ipar)])
                    P.add("pool", lambda e: e.tensor_copy(out=qz[ipar][64:128, 1, :], in_=A[64:128, hd, ql:ql + 256]),
                          reads=[("A", hd, ql // 512)], writes=[("F", 9 + ipar)])
                for cm in range(2):
                    mm(PS[:, par, cm * 256 + q0:(cm + 1) * 256],
                       KT[:, hd, kt * 128:(kt + 1) * 128],
                       qz[ipar][:, cm, q0:256], True, True,
                       [("K", hd, kt // 8), ("F", 9 + ipar)], pst(par))

            def emit_exp(i):
                hd, u, kt, d, q0, ql = geom(i)
                par = i % 2
                r = i % 3
                P.add("act", lambda e: e.activation(
                    out=Eb[r][:, :, q0:256],
                    in_=PS[:, par, :].rearrange("p (c q) -> p c q", c=2)[:, :, q0:256], func=AF.Exp),
                    reads=pst(par), writes=[("F", r)])
                if d >= 0:
                    P.add("dve", lambda e: e.tensor_tensor(out=Eb[r][:, :, q0:q0 + 128], in0=Eb[r][:, :, q0:q0 + 128],
                                                           in1=tri_b, op=ALU.mult),
                          reads=[("F", r), "tri"], writes=[("F", r)])

            pending = []

            def emit_PV(i, it):
                hd, u, kt, d, q0, ql = geom(i)
                r = i % 3
                ipar = it % 2
                bA = 2 + ipar
                bS = 4 + ipar
                Ab = PS[:, bA, :].rearrange("p (c q) -> p c q", c=2)
                Sb = PS[:, bS, :].rearrange("p (c q) -> p c q", c=2)
                first = kt == 0
                last = kt == 2 * u + 1
                if q0 == 0:
                    E2 = Fr[:, r, :].bitcast(BF16)[:, 0:512]
                    mm(PS[:, bA, :], V[:, kt, hd * 128:(hd + 1) * 128], E2, first, last,
                       [("V", kt), ("F", r)], pst(bA))
                    mm(PS[:, bS, :], ones[:], E2, first, last, [("F", r), "ones"], pst(bS))
                else:
                    for cm in range(2):
                        mm(Ab[:, cm, q0:256], V[:, kt, hd * 128:(hd + 1) * 128], Eb[r][:, cm, q0:256], first,
                           last and cm == 1, [("V", kt), ("F", r)], pst(bA))
                    for cm in range(2):
                        mm(Sb[:, cm, q0:256], ones[:], Eb[r][:, cm, q0:256], first, last and cm == 1,
                           [("F", r), "ones"], pst(bS))
                if last:
                    b0 = 3 + 3 * ipar
                    rr = Fr[:, b0, :]
                    tt_ = Fr[:, b0 + 1, :]
                    ob = Fr[:, b0 + 2, 0:256]
                    P.add("dve", lambda e: e.reciprocal(out=rr, in_=PS[:, bS, :]),
                          reads=pst(bS), writes=[("F", b0)])
                    P.add("dve", lambda e: e.tensor_tensor(out=tt_, in0=PS[:, bA, :], in1=rr, op=ALU.mult),
                          reads=pst(bA) + [("F", b0)], writes=[("F", b0 + 1)])
                    P.add("dve", lambda e: e.scalar_tensor_tensor(out=ob, in0=tt_[:, 256:512], scalar=neglam,
                                                                  in1=tt_[:, 0:256], op0=ALU.mult, op1=ALU.add),
                          reads=[("F", b0 + 1), "small"], writes=[("F", b0 + 2)])
                    r2 = ctr["sq"] % 3
                    ctr["sq"] += 1
                    P.add("act", lambda e: e.activation(out=sq[r2][:, 0:256], in_=ob, func=AF.Square),
                          reads=[("F", b0 + 2)], writes=[("sq", r2)])

                    def tail():
                        b = 6 + ctr["stat"] % 2
                        ctr["stat"] += 1
                        ssb = PS[:, b, 0:256]
                        mm(ssb, ones[:], sq[r2][:, 0:256], True, True, [("sq", r2), "ones"], pst(b))
                        P.add("act", lambda e: e.activation(out=sd[:, 0:256], in_=ssb, func=AF.Sqrt,
                                                            bias=eps_t[:, 0:1], scale=1.0 / 128.0),
                              reads=pst(b) + ["eps"], writes=["sd"])
                        rb = ctr["rstd"] % 2
                        ctr["rstd"] += 1
                        P.add("dve", lambda e: e.reciprocal(out=rstd[rb][:, 0:256], in_=sd[:, 0:256]),
                              reads=["sd"], writes=[("rstd", rb)])
                        P.add("dve", lambda e: e.scalar_tensor_tensor(
                            out=H[:, hd, ql:ql + 256], in0=ob, scalar=gs, in1=rstd[rb][:, 0:256],
                            op0=ALU.mult, op1=ALU.mult),
                            reads=[("F", b0 + 2), ("rstd", rb), "small"], writes=[("H", hd, ql // 512)])

                    pending.append([4, tail])

            n = len(steps)
            emit_S(0)
            it = 0
            for i in range(n):
                if i + 1 < n:
                    emit_S(i + 1)
                emit_exp(i)
                emit_PV(i, it)
                if steps[i][2] == 2 * steps[i][1] + 1:
                    it += 1
                for pnd in list(pending):
                    pnd[0] -= 1
                    if pnd[0] <= 0:
                        pending.remove(pnd)
                        pnd[1]()
            for pnd in pending:
                pnd[1]()
            proj(wview(wo_d[jl]), 0, H, "H", evac_to_F)
            for tb in range(2):
                postnorm(gidx(l, 3), tb)

        def load_x(c):
            for tt in range(8):
                P.add("sp", (lambda e, tt=tt: e.dma_start(
                    out=Fr[:, 2 * tt:2 * tt + 2, :].rearrange("p a b -> p (a b)"),
                    in_=x_d[c * TCH + tt * 128:c * TCH + (tt + 1) * 128, :])),
                    writes=[("F", 2 * tt), ("F", 2 * tt + 1)], lane=("x", tt % 4))
            for j in range(8):
                for g in range(2):
                    bank = ctr["u"] % 4
                    ctr["u"] += 1
                    for ti in range(4):
                        tt = g * 4 + ti
                        xs = Fr[:, 2 * tt:2 * tt + 2, :].rearrange("p a b -> p (a b)")
                        P.add("pe", (lambda e, bank=bank, ti=ti, xs=xs, j=j: e.transpose(
                            PS[:, bank, ti * 128:(ti + 1) * 128], xs[:, j * 128:(j + 1) * 128], ident)),
                            reads=[("F", 2 * tt), ("F", 2 * tt + 1), "cst"], writes=pst(bank))
                    if (j + g) % 2 == 0:
                        P.add("act", (lambda e, bank=bank, j=j, g=g: e.copy(out=X[:, j, tbs(g)], in_=PS[:, bank, :])),
                              reads=pst(bank), writes=[("X", j, g)])
                    else:
                        P.add("dve", (lambda e, bank=bank, j=j, g=g: e.tensor_copy(out=X[:, j, tbs(g)], in_=PS[:, bank, :])),
                              reads=pst(bank), writes=[("X", j, g)])

        def store_x(c):
            for tt in range(8):
                for g in range(2):
                    bank = ctr["u"] % 4
                    ctr["u"] += 1
                    for ji in range(4):
                        j = g * 4 + ji
                        P.add("pe", (lambda e, bank=bank, ji=ji, j=j, tt=tt: e.transpose(
                            PS[:, bank, ji * 128:(ji + 1) * 128], X[:, j, tt * 128:(tt + 1) * 128], ident)),
                            reads=[("X", j, tt // 4), "cst"], writes=pst(bank))
                    blk = 2 * tt + g
                    if g == 0:
                        P.add("act", (lambda e, bank=bank, blk=blk: e.copy(out=Fr[:, blk, :], in_=PS[:, bank, :])),
                              reads=pst(bank), writes=[("F", blk)])
                    else:
                        P.add("dve", (lambda e, bank=bank, blk=blk: e.tensor_copy(out=Fr[:, blk, :], in_=PS[:, bank, :])),
                              reads=pst(bank), writes=[("F", blk)])
                P.add("sp", (lambda e, tt=tt: e.dma_start(
                    out=out_d[c * TCH + tt * 128:c * TCH + (tt + 1) * 128, :],
                    in_=Fr[:, 2 * tt:2 * tt + 2, :].rearrange("p a b -> p (a b)"))),
                    reads=[("F", 2 * tt), ("F", 2 * tt + 1)], writes=[("OUT", c, tt)], lane=("o", tt % 4))

        for c in range(NCH):
            load_x(c)
            nsub = 0

            def go():
                return nsub < nstop

            for l in range(4):
                if go():
                    ffn(l, 0)
                nsub += 1
                if go():
                    if l < 2:
                        conv_mixer(l, c)
                    else:
                        attn(l, c)
                nsub += 1
                if go():
                    ffn(l, 1)
                nsub += 1
                if l == 1 and go():
                    kv_proj(c)
            store_x(c)
        P.add("sp", None, reads=[("OUT", c, tt) for c in range(NCH) for tt in range(8)])
        P.emit(nc, st)
    nc._prog_stats = P.stats
    return nc


_CACHE = {}


def _consts():
    c = np.zeros((128, 256), np.float32)
    c[:, 0:128] = np.eye(128, dtype=np.float32)
    c[:, 128:256] = np.triu(np.ones((128, 128), np.float32))
    return c


def kernel(**inputs):
    nstop = int(inputs.pop("_nstop", 999))
    ncores = int(inputs.pop("_ncores", NB))
    if nstop not in _CACHE:
        _CACHE[nstop] = build_nc(nstop)
    nc = _CACHE[nstop]
    names = ["g_norm", "w_ffn_gate", "w_ffn_up", "w_ffn_down", "w_conv_in", "w_conv", "w_conv_out", "g_kv",
             "w_kv", "w_q", "lambda_q1", "lambda_k1", "lambda_q2", "lambda_k2", "g_subln", "w_o"]
    shared = {k: np.asarray(inputs[k], dtype=np.float32) for k in names}
    if DBG["lite"]:
        for k in ("w_ffn_gate", "w_ffn_up", "w_ffn_down"):
            shared[k] = shared[k][:1, :1]
        for k in ("w_conv_in", "w_conv_out", "w_q", "w_o"):
            shared[k] = shared[k][:1]
    shared = {k: np.ascontiguousarray(v) for k, v in shared.items()}
    shared["consts"] = _consts()
    x = np.asarray(inputs["x"], dtype=np.float32)
    in_maps = []
    for b in range(ncores):
        m = dict(shared)
        m["x"] = np.ascontiguousarray(x[b])
        in_maps.append(m)
    res = run_bass_kernel_spmd(nc, in_maps, core_ids=list(range(ncores)))
    out = np.stack([np.asarray(res.results[b]["out"], dtype=np.float32) for b in range(ncores)], axis=0)
    return out
```

```python
import math
from contextlib import ExitStack

import numpy as np
import concourse.bass as bass
import concourse.mybir as mybir
from concourse.bass_utils import run_bass_kernel_spmd

F32 = mybir.dt.float32
BF16 = mybir.dt.bfloat16
AF = mybir.ActivationFunctionType
ALU = mybir.AluOpType
AX = mybir.AxisListType

D = 1024
SEQ = 2048
NB = 8
DFF = 2816
EPS = 1e-6
TCH = 1024
NCH = SEQ // TCH
SAME_ENG_SYNC = True


class _Op:
    __slots__ = ("eng", "fn", "dma", "lane", "deps", "idx", "ms", "cnt", "waits")


class Prog:
    def __init__(self):
        self.ops = []
        self.last_w = {}
        self.readers = {}
        self.lane_last = {}

    def add(self, eng, fn, reads=(), writes=(), lane=None):
        op = _Op()
        op.eng = eng
        op.fn = fn
        op.dma = lane is not None
        op.lane = lane
        op.idx = len(self.ops)
        op.ms = False
        op.cnt = 0
        deps = set()
        for t in reads:
            w = self.last_w.get(t)
            if w is not None:
                deps.add(w)
        for t in writes:
            w = self.last_w.get(t)
            if w is not None:
                deps.add(w)
            rs = self.readers.get(t)
            if rs:
                deps.update(rs)
        if lane is not None:
            p = self.lane_last.get(lane)
            if p is not None:
                deps.add(p)
            self.lane_last[lane] = op.idx
        for t in reads:
            self.readers.setdefault(t, []).append(op.idx)
        for t in writes:
            self.last_w[t] = op.idx
            self.readers[t] = []
        deps.discard(op.idx)
        op.deps = deps
        self.ops.append(op)
        return op

    @staticmethod
    def _needs_sync(p, op):
        if p.dma:
            return True
        if p.eng != op.eng:
            return True
        if p.eng == "pe":
            return False
        return SAME_ENG_SYNC

    def emit(self, nc, stack):
        ops = self.ops
        for op in ops:
            best = {}
            for d in op.deps:
                p = ops[d]
                if (not p.dma) and self._needs_sync(p, op):
                    if best.get(p.eng, -1) < d:
                        best[p.eng] = d
            for d in best.values():
                ops[d].ms = True
        eng_cnt = {}
        lane_cnt = {}
        for op in ops:
            if op.dma:
                lane_cnt[op.lane] = lane_cnt.get(op.lane, 0) + 1
                op.cnt = 16 * lane_cnt[op.lane]
            elif op.ms:
                eng_cnt[op.eng] = eng_cnt.get(op.eng, 0) + 1
                op.cnt = eng_cnt[op.eng]
        waited = {}
        for op in ops:
            need = {}
            for d in op.deps:
                p = ops[d]
                if not self._needs_sync(p, op):
                    continue
                key = ("lane", p.lane) if p.dma else ("eng", p.eng)
                if need.get(key, 0) < p.cnt:
                    need[key] = p.cnt
                assert p.dma or p.cnt > 0 or any(
                    (not ops[d2].dma) and ops[d2].eng == p.eng and d2 > d for d2 in op.deps)
            wd = waited.setdefault(op.eng, {})
            op.waits = []
            for k, v in need.items():
                if wd.get(k, 0) < v:
                    op.waits.append((k, v))
                    wd[k] = v
        sems = {}
        for e in sorted(eng_cnt):
            sems[("eng", e)] = stack.enter_context(nc.semaphore("s_" + e))
        for i, l in enumerate(lane_cnt):
            sems[("lane", l)] = stack.enter_context(nc.semaphore("l_%d" % i))
        per = {}
        for op in ops:
            per.setdefault(op.eng, []).append(op)
        self.stats = {e: len(v) for e, v in per.items()}
        self.stats["sems"] = len(sems)

        def mk(name):
            lst = per.get(name, [])

            def body(e):
                for op in lst:
                    for k, v in op.waits:
                        e.wait_ge(sems[k], v)
                    if op.fn is None:
                        continue
                    ins = op.fn(e)
                    if op.dma:
                        ins.then_inc(sems[("lane", op.lane)], 16)
                    elif op.ms:
                        ins.then_inc(sems[("eng", op.eng)], 1)

            return body

        with nc.Block() as block:
            block.tensor(mk("pe"))
            block.scalar(mk("act"))
            block.vector(mk("dve"))
            block.gpsimd(mk("pool"))
            block.sync(mk("sp"))


def lambda_init(layer):
    return 0.8 - 0.6 * math.exp(-0.3 * layer)


DBG = {"ffn_stage": 99, "lite": False}


def build_nc(nstop=999):
    nc = bass.Bass("TRN2", target_bir_lowering=False)
    nl4 = 1 if DBG["lite"] else 4
    nl2 = 1 if DBG["lite"] else 2

    def din(name, shape):
        return nc.dram_tensor(name, list(shape), F32, kind="ExternalInput").ap()

    x_d = din("x", (SEQ, D))
    gnorm_d = din("g_norm", (4, 6, D))
    wg_d = din("w_ffn_gate", (nl4, nl2, D, DFF))
    wu_d = din("w_ffn_up", (nl4, nl2, D, DFF))
    wd_d = din("w_ffn_down", (nl4, nl2, DFF, D))
    wci_d = din("w_conv_in", (nl2, D, 3 * D))
    wc_d = din("w_conv", (2, 3, D))
    wco_d = din("w_conv_out", (nl2, D, D))
    gkv_d = din("g_kv", (D,))
    wkv_d = din("w_kv", (D, 2 * D))
    wq_d = din("w_q", (nl2, D, D))
    lq1_d = din("lambda_q1", (2, 64))
    lk1_d = din("lambda_k1", (2, 64))
    lq2_d = din("lambda_q2", (2, 64))
    lk2_d = din("lambda_k2", (2, 64))
    gsub_d = din("g_subln", (2, 128))
    wo_d = din("w_o", (nl2, D, D))
    cst_d = din("consts", (128, 256))
    out_d = nc.dram_tensor("out", [SEQ, D], F32, kind="ExternalOutput").ap()

    P = Prog()
    with ExitStack() as st:
        def sb(name, shape, dt):
            return st.enter_context(nc.sbuf_tensor(name, list(shape), dt))

        X = sb("X", (128, 8, TCH), F32)
        H = sb("H", (128, 8, TCH), BF16)
        Fr = sb("Fr", (128, 16, 512), F32)
        A = sb("A", (128, 11, TCH), BF16)
        KT = sb("KT", (128, 8, SEQ), BF16)
        V = sb("V", (128, 16, D), BF16)
        slots = [sb("slot%d" % i, (128, 6144), BF16) for i in range(2)]
        cst = sb("cst", (128, 256), F32)
        tri = sb("tri", (128, 128), BF16)
        ones = sb("ones", (128, 128), BF16)
        cols = sb("cols", (128, 256), F32)
        eps_t = sb("eps", (128, 1), F32)
        sq = [sb("sq%d" % i, (128, 512), BF16) for i in range(3)]
        sd = sb("sd", (128, 512), F32)
        rstd = [sb("rstd%d" % i, (128, 512), F32) for i in range(2)]
        sg = [sb("sg%d" % i, (128, 512), F32) for i in range(2)]
        halo = sb("halo", (128, 2, 8, 2), F32)
        small = sb("small", (128, 16), F32)
        PS = st.enter_context(nc.psum_tensor("PS", [128, 8, 512], F32))
        ident = cst[:, 0:128]

        def pst(b):
            return [("ps", b, 0), ("ps", b, 1)]

        def tbs(tb):
            return slice(tb * 512, (tb + 1) * 512)

        def mm(out, lhsT, rhs, start, stop, reads, writes):
            P.add("pe", lambda e: e.matmul(out, lhsT, rhs, start=start, stop=stop), reads, writes)

        ctr = {"slot": 0, "stat": 0, "sq": 0, "rstd": 0, "u": 0}

        def next_slot():
            s = ctr["slot"] % 2
            ctr["slot"] += 1
            return s

        def stoks(s, a, b):
            return [("slot", s, i) for i in range(a, b)]

        P.add("dve", lambda e: e.memset(ones[:], 1.0), writes=["ones"])
        P.add("dve", lambda e: e.memset(eps_t[:], EPS), writes=["eps"])
        P.add("dve", lambda e: e.memset(small[:, 12:13], -0.5), writes=["mhalf"])
        P.add("sp", lambda e: e.dma_start(out=cst[:], in_=cst_d), writes=["cst"], lane="cst")
        P.add("dve", lambda e: e.tensor_copy(out=tri[:], in_=cst[:, 128:256]), reads=["cst"], writes=["tri"])
        gview = gnorm_d.rearrange("l n (j p) -> (l n j) p", p=128)
        P.add("sp", lambda e: e.dma_start(out=Fr[0:128, 0, 0:128], in_=gview[0:128, :]),
              writes=[("F", 0)], lane="p0")
        P.add("sp", lambda e: e.dma_start(out=Fr[0:64, 1, 0:128], in_=gview[128:192, :]),
              writes=[("F", 1)], lane="p1")
        P.add("sp", lambda e: e.dma_start(out=Fr[64:72, 1, 0:128], in_=gkv_d.rearrange("(j p) -> j p", p=128)),
              writes=[("F", 1)], lane="p2")
        P.add("sp", lambda e: e.dma_start(out=Fr[72:120, 1, 0:128],
                                          in_=wc_d.rearrange("l w (j p) -> (l w j) p", p=128)),
              writes=[("F", 1)], lane="p3")
        P.add("sp", lambda e: e.dma_start(out=Fr[120:122, 1, 0:128], in_=gsub_d),
              writes=[("F", 1)], lane="p4")
        P.add("pe", lambda e: e.transpose(PS[:, 0, 0:128], Fr[0:128, 0, 0:128], ident),
              reads=[("F", 0), "cst"], writes=pst(0))
        P.add("pe", lambda e: e.transpose(PS[:, 0, 128:250], Fr[0:122, 1, 0:128], cst[0:122, 0:122]),
              reads=[("F", 1), "cst"], writes=pst(0))
        P.add("dve", lambda e: e.tensor_copy(out=cols[:, 0:250], in_=PS[:, 0, 0:250]), reads=pst(0), writes=["cols"])
        gv4 = cols[:, 0:192].rearrange("p (l n j) -> p l n j", l=4, n=6)
        for n in (1, 5):
            P.add("dve", (lambda e, n=n: e.tensor_scalar(out=gv4[:, :, n, :], in0=gv4[:, :, n, :], scalar1=0.5,
                                                         scalar2=None, op0=ALU.mult)),
                  reads=["cols"], writes=["cols"])
        lv = Fr[:, 2, :].rearrange("p (a b) -> p a b", a=4)
        for i, ld in enumerate((lq1_d, lk1_d, lq2_d, lk2_d)):
            P.add("sp", (lambda e, i=i, ld=ld: e.dma_start(
                out=lv[:, i, :], in_=ld.rearrange("a b -> (a b)").partition_broadcast(128))),
                writes=[("F", 2)], lane="l%d" % i)
        pr = Fr[:, 3, 0:256].rearrange("p (a b) -> p a b", a=2)
        P.add("dve", lambda e: e.tensor_tensor(out=pr[:, 0, :], in0=lv[:, 0, :], in1=lv[:, 1, :], op=ALU.mult),
              reads=[("F", 2)], writes=[("F", 3)])
        P.add("dve", lambda e: e.tensor_tensor(out=pr[:, 1, :], in0=lv[:, 2, :], in1=lv[:, 3, :], op=ALU.mult),
              reads=[("F", 2)], writes=[("F", 3)])
        P.add("dve", lambda e: e.reduce_sum(out=small[:, 0:4],
                                            in_=Fr[:, 3, 0:256].rearrange("p (a b) -> p a b", a=4), axis=AX.X),
              reads=[("F", 3)], writes=["small"])
        P.add("act", lambda e: e.activation(out=small[:, 4:8], in_=small[:, 0:4], func=AF.Exp),
              reads=["small"], writes=["small"])
        P.add("dve", lambda e: e.tensor_tensor(out=small[:, 8:10], in0=small[:, 6:8], in1=small[:, 4:6],
                                               op=ALU.subtract), reads=["small"], writes=["small"])
        for jl in range(2):
            li = lambda_init(jl + 2)
            P.add("dve", (lambda e, jl=jl, li=li: e.tensor_scalar(out=small[:, 8 + jl:9 + jl], in0=small[:, 8 + jl:9 + jl],
                                                                  scalar1=-li, scalar2=None, op0=ALU.add)),
                  reads=["small"], writes=["small"])
            P.add("dve", (lambda e, jl=jl, li=li: e.tensor_scalar(out=small[:, 10 + jl:11 + jl],
                                                                  in0=cols[:, 248 + jl:249 + jl],
                                                                  scalar1=1.0 - li, scalar2=None, op0=ALU.mult)),
                  reads=["small", "cols"], writes=["small"])

        def gidx(l, n):
            return (l * 6 + n) * 8

        def rstd_of(src_fn, src_tok, ncols, invd, nj):
            b = 6 + ctr["stat"] % 2
            ctr["stat"] += 1
            ssb = PS[:, b, 0:ncols]
            for j in range(nj):
                r = ctr["sq"] % 3
                ctr["sq"] += 1
                P.add("act", (lambda e, j=j, r=r: e.activation(out=sq[r][:, 0:ncols], in_=src_fn(j), func=AF.Square)),
                      reads=[src_tok(j)], writes=[("sq", r)])
                mm(ssb, ones[:], sq[r][:, 0:ncols], j == 0, j == nj - 1, [("sq", r), "ones"], pst(b))
            P.add("act", lambda e: e.activation(out=sd[:, 0:ncols], in_=ssb, func=AF.Sqrt, bias=eps_t[:, 0:1],
                                                scale=invd), reads=pst(b) + ["eps"], writes=["sd"])
            rb = ctr["rstd"] % 2
            ctr["rstd"] += 1
            P.add("dve", lambda e: e.reciprocal(out=rstd[rb][:, 0:ncols], in_=sd[:, 0:ncols]),
                  reads=["sd"], writes=[("rstd", rb)])
            return rstd[rb][:, 0:ncols], ("rstd", rb)

        def prenorm(gbase, tb):
            rs, rtok = rstd_of(lambda j: X[:, j, tbs(tb)], lambda j: ("X", j, tb), 512, 1.0 / D, 8)
            for j in range(8):
                P.add("dve", (lambda e, j=j: e.scalar_tensor_tensor(
                    out=H[:, j, tbs(tb)], in0=X[:, j, tbs(tb)], scalar=cols[:, gbase + j:gbase + j + 1], in1=rs,
                    op0=ALU.mult, op1=ALU.mult)),
                    reads=[("X", j, tb), rtok, "cols"], writes=[("H", j, tb)])

        def postnorm(gbase, tb):
            rs, rtok = rstd_of(lambda j: Fr[:, 2 * j + tb, :], lambda j: ("F", 2 * j + tb), 512, 1.0 / D, 8)
            for j in range(8):
                blk = 2 * j + tb
                P.add("dve", (lambda e, j=j, blk=blk: e.scalar_tensor_tensor(
                    out=Fr[:, blk, :], in0=Fr[:, blk, :], scalar=cols[:, gbase + j:gbase + j + 1], in1=rs,
                    op0=ALU.mult, op1=ALU.mult)),
                    reads=[("F", blk), rtok, "cols"], writes=[("F", blk)])
                P.add("dve", (lambda e, j=j, blk=blk: e.tensor_tensor(
                    out=X[:, j, tbs(tb)], in0=X[:, j, tbs(tb)], in1=Fr[:, blk, :], op=ALU.add)),
                    reads=[("F", blk), ("X", j, tb)], writes=[("X", j, tb)])

        def wview(w2d):
            return w2d.rearrange("(k p) f -> p k f", p=128)

        def proj(Wv, c0, src, srctok, evac):
            for mh in range(2):
                s = next_slot()
                wv = slots[s][:, 0:4096].rearrange("p (k f) -> p k f", k=8)
                P.add("pool", (lambda e, wv=wv, mh=mh: e.dma_start(out=wv, in_=Wv[:, :, c0 + mh * 512:c0 + (mh + 1) * 512])),
                      writes=stoks(s, 0, 4), lane=("w", s, 0))
                for mi in range(4):
                    m = mh * 4 + mi
                    for tb in range(2):
                        bank = ctr["u"] % 2
                        ctr["u"] += 1
                        Pp = PS[:, bank, :]
                        for k in range(8):
                            mm(Pp, wv[:, k, mi * 128:(mi + 1) * 128], src[:, k, tbs(tb)], k == 0, k == 7,
                               stoks(s, 0, 4) + [(srctok, k, tb)], pst(bank))
                        evac(m, tb, Pp, bank)

        def evac_to_F(m, tb, Pp, bank):
            blk = 2 * m + tb
            P.add("act", lambda e: e.copy(out=Fr[:, blk, :], in_=Pp), reads=pst(bank), writes=[("F", blk)])

        def ffn(l, w):
            gpre = gidx(l, 0 if w == 0 else 4)
            gpost = gidx(l, 1 if w == 0 else 5)
            for tb in range(2):
                prenorm(gpre, tb)
            if DBG["ffn_stage"] <= 1:
                return
            Wg = wview(wg_d[l, w])
            Wu = wview(wu_d[l, w])
            Wd = wd_d[l, w].rearrange("(f p) m -> p f m", p=128)
            for hf in range(2):
                for (f0, n) in ((0, 3), (3, 3), (6, 3), (9, 2)):
                    s = next_slot()
                    gv = slots[s][:, 0:8 * n * 128].rearrange("p (k f) -> p k f", k=8)
                    uv = slots[s][:, 3072:3072 + 8 * n * 128].rearrange("p (k f) -> p k f", k=8)
                    fa = hf * 11 + f0
                    P.add("pool", (lambda e, gv=gv, fa=fa, n=n: e.dma_start(out=gv, in_=Wg[:, :, fa * 128:(fa + n) * 128])),
                          writes=stoks(s, 0, 3), lane=("w", s, 0))
                    P.add("pool", (lambda e, uv=uv, fa=fa, n=n: e.dma_start(out=uv, in_=Wu[:, :, fa * 128:(fa + n) * 128])),
                          writes=stoks(s, 3, 6), lane=("w", s, 3))
                    for fi in range(n):
                        fl = f0 + fi
                        for tb in range(2):
                            par = ctr["u"] % 2
                            ctr["u"] += 1
                            G = PS[:, 2 * par, :]
                            U = PS[:, 2 * par + 1, :]
                            for k in range(8):
                                mm(G, gv[:, k, fi * 128:(fi + 1) * 128], H[:, k, tbs(tb)], k == 0, k == 7,
                                   stoks(s, 0, 3) + [("H", k, tb)], pst(2 * par))
                            for k in range(8):
                                mm(U, uv[:, k, fi * 128:(fi + 1) * 128], H[:, k, tbs(tb)], k == 0, k == 7,
                                   stoks(s, 3, 6) + [("H", k, tb)], pst(2 * par + 1))
                            P.add("act", (lambda e, G=G, par=par: e.activation(out=sg[par][:], in_=G, func=AF.Silu)),
                                  reads=pst(2 * par), writes=[("sg", par)])
                            P.add("dve", (lambda e, U=U, par=par, fl=fl, tb=tb: e.tensor_tensor(
                                out=A[:, fl, tbs(tb)], in0=U, in1=sg[par][:], op=ALU.mult)),
                                reads=pst(2 * par + 1) + [("sg", par)], writes=[("A", fl, tb)])
                    if DBG["ffn_stage"] <= 2:
                        return
                for m0 in (0, 4):
                    s = next_slot()
                    dv = slots[s][:, 0:5632].rearrange("p (f m) -> p f m", f=11)
                    P.add("pool", (lambda e, dv=dv, hf=hf, m0=m0: e.dma_start(
                        out=dv, in_=Wd[:, hf * 11:(hf + 1) * 11, m0 * 128:(m0 + 4) * 128])),
                        writes=stoks(s, 0, 6), lane=("w", s, 0))
                    for mi in range(4):
                        m = m0 + mi
                        for tb in range(2):
                            par = ctr["u"] % 2
                            ctr["u"] += 1
                            Dp = PS[:, 4 + par, :]
                            for fl in range(11):
                                mm(Dp, dv[:, fl, mi * 128:(mi + 1) * 128], A[:, fl, tbs(tb)], fl == 0, fl == 10,
                                   stoks(s, 0, 6) + [("A", fl, tb)], pst(4 + par))
                            blk = 2 * m + tb
                            if hf == 0:
                                P.add("act", (lambda e, Dp=Dp, blk=blk: e.copy(out=Fr[:, blk, :], in_=Dp)),
                                      reads=pst(4 + par), writes=[("F", blk)])
                            else:
                                P.add("dve", (lambda e, Dp=Dp, blk=blk: e.tensor_tensor(
                                    out=Fr[:, blk, :], in0=Dp, in1=Fr[:, blk, :], op=ALU.add)),
                                    reads=pst(4 + par) + [("F", blk)], writes=[("F", blk)])
                if DBG["ffn_stage"] <= 3:
                    return
            for tb in range(2):
                postnorm(gpost, tb)

        def conv_mixer(l, c):
            for tb in range(2):
                prenorm(gidx(l, 2), tb)
            Win = wview(wci_d[l])
            for jp in range(4):
                s = next_slot()
                parts = [slots[s][:, pt * 2048:(pt + 1) * 2048].rearrange("p (k f) -> p k f", k=8) for pt in range(3)]
                for pt in range(3):
                    P.add("pool", (lambda e, pt=pt, jp=jp, parts=parts: e.dma_start(
                        out=parts[pt], in_=Win[:, :, pt * 1024 + jp * 256:pt * 1024 + (jp + 1) * 256])),
                        writes=stoks(s, 2 * pt, 2 * pt + 2), lane=("w", s, 2 * pt))
                for jj in range(2):
                    j = 2 * jp + jj
                    cub = j % 2
                    cu = Fr[:, cub * 3:cub * 3 + 3, :].rearrange("p a b -> p (a b)")
                    cutoks = [("F", cub * 3 + i) for i in range(3)]
                    if c == 0:
                        P.add("dve", (lambda e, cu=cu: e.memset(cu[:, 0:2], 0.0)), writes=cutoks)
                    else:
                        P.add("dve", (lambda e, cu=cu, j=j: e.tensor_copy(out=cu[:, 0:2], in_=halo[:, l, j, :])),
                              reads=[("halo", l, j)], writes=cutoks)
                    w0 = 200 + (l * 3 + 0) * 8 + j
                    w1 = 200 + (l * 3 + 1) * 8 + j
                    w2 = 200 + (l * 3 + 2) * 8 + j
                    for tb in range(2):
                        par = ctr["u"] % 2
                        ctr["u"] += 1
                        Pb, Pc, Pu = (PS[:, 3 * par + i, :] for i in range(3))
                        for pt in range(3):
                            for k in range(8):
                                mm(PS[:, 3 * par + pt, :], parts[pt][:, k, jj * 128:(jj + 1) * 128], H[:, k, tbs(tb)],
                                   k == 0, k == 7, stoks(s, 2 * pt, 2 * pt + 2) + [("H", k, tb)], pst(3 * par + pt))
                        ucp = Fr[:, 6 + par, :]
                        z = Fr[:, 8 + par, :]
                        o2 = 2 + tb * 512
                        P.add("act", (lambda e, ucp=ucp, Pu=Pu: e.copy(out=ucp, in_=Pu)),
                              reads=pst(3 * par + 2), writes=[("F", 6 + par)])
                        P.add("dve", (lambda e, cu=cu, o2=o2, Pc=Pc, ucp=ucp: e.tensor_tensor(
                            out=cu[:, o2:o2 + 512], in0=Pc, in1=ucp, op=ALU.mult)),
                            reads=pst(3 * par + 1) + [("F", 6 + par)], writes=cutoks)
                        P.add("act", (lambda e, z=z, cu=cu, o2=o2, w2=w2: e.mul(out=z, in_=cu[:, o2:o2 + 512],
                                                                               mul=cols[:, w2:w2 + 1])),
                              reads=cutoks + ["cols"], writes=[("F", 8 + par)])
                        P.add("dve", (lambda e, z=z, cu=cu, o2=o2, w1=w1: e.scalar_tensor_tensor(
                            out=z, in0=cu[:, o2 - 1:o2 + 511], scalar=cols[:, w1:w1 + 1], in1=z,
                            op0=ALU.mult, op1=ALU.add)), reads=cutoks + ["cols", ("F", 8 + par)],
                            writes=[("F", 8 + par)])
                        P.add("dve", (lambda e, z=z, cu=cu, o2=o2, w0=w0: e.scalar_tensor_tensor(
                            out=z, in0=cu[:, o2 - 2:o2 + 510], scalar=cols[:, w0:w0 + 1], in1=z,
                            op0=ALU.mult, op1=ALU.add)), reads=cutoks + ["cols", ("F", 8 + par)],
                            writes=[("F", 8 + par)])
                        P.add("dve", (lambda e, z=z, Pb=Pb, j=j, tb=tb: e.tensor_tensor(
                            out=A[:, j, tbs(tb)], in0=Pb, in1=z, op=ALU.mult)),
                            reads=pst(3 * par) + [("F", 8 + par)], writes=[("A", j, tb)])
                    if c == 0:
                        P.add("dve", (lambda e, cu=cu, j=j: e.tensor_copy(out=halo[:, l, j, :], in_=cu[:, 1024:1026])),
                              reads=cutoks, writes=[("halo", l, j)])
            proj(wview(wco_d[l]), 0, A, "A", evac_to_F)
            for tb in range(2):
                postnorm(gidx(l, 3), tb)

        def kv_proj(c):
            for tb in range(2):
                prenorm(192, tb)
            Wkv = wview(wkv_d)

            def evac_k(m, tb, Pp, bank):
                P.add("act", lambda e: e.copy(out=KT[:, m, c * TCH + tb * 512:c * TCH + (tb + 1) * 512], in_=Pp),
                      reads=pst(bank), writes=[("K", m, c)])

            proj(Wkv, 0, H, "H", evac_k)
            for eh in range(2):
                s = next_slot()
                wv = slots[s][:, 0:4096].rearrange("p (k f) -> p k f", k=8)
                P.add("pool", (lambda e, wv=wv, eh=eh: e.dma_start(out=wv, in_=Wkv[:, :, D + eh * 512:D + (eh + 1) * 512])),
                      writes=stoks(s, 0, 4), lane=("w", s, 0))
                for tt in range(8):
                    bank = ctr["u"] % 2
                    ctr["u"] += 1
                    Pp = PS[:, bank, :]
                    for k in range(8):
                        mm(Pp, H[:, k, tt * 128:(tt + 1) * 128], wv[:, k, :], k == 0, k == 7,
                           stoks(s, 0, 4) + [("H", k, tt // 4)], pst(bank))
                    kt = c * 8 + tt
                    if tt % 2 == 0:
                        P.add("act", (lambda e, kt=kt, eh=eh, Pp=Pp: e.copy(out=V[:, kt, eh * 512:(eh + 1) * 512], in_=Pp)),
                              reads=pst(bank), writes=[("V", kt)])
                    else:
                        P.add("dve", (lambda e, kt=kt, eh=eh, Pp=Pp: e.tensor_copy(out=V[:, kt, eh * 512:(eh + 1) * 512], in_=Pp)),
                              reads=pst(bank), writes=[("V", kt)])

        def attn(l, c):
            jl = l - 2
            for tb in range(2):
                prenorm(gidx(l, 2), tb)

            def evac_q(m, tb, Pp, bank):
                P.add("act", lambda e: e.mul(out=A[:, m, tbs(tb)], in_=Pp, mul=0.125),
                      reads=pst(bank), writes=[("A", m, tb)])

            proj(wview(wq_d[jl]), 0, H, "H", evac_q)

            steps = []
            for hd in range(8):
                for u in range(4 * c, 4 * c + 4):
                    for kt in range(2 * u + 2):
                        steps.append((hd, u, kt))
            Eb = [Fr[:, r, :].bitcast(BF16)[:, 0:512].rearrange("p (c q) -> p c q", c=2) for r in range(3)]
            neglam = small[:, 8 + jl:9 + jl]
            gs = small[:, 10 + jl:11 + jl]
            tri_b = tri[:].unsqueeze(1).to_broadcast([128, 2, 128])

            def geom(i):
                hd, u, kt = steps[i]
                d = kt - 2 * u
                q0 = 128 if d == 1 else 0
                ql = (u - 4 * c) * 256
                return hd, u, kt, d, q0, ql

            it_of = []
            _it = 0
            for (hd_, u_, kt_) in steps:
                it_of.append(_it)
                if kt_ == 2 * u_ + 1:
                    _it += 1
            qz = [Fr[:, 9 + p_, :].bitcast(BF16)[:, 0:512].rearrange("p (c q) -> p c q", c=2) for p_ in range(2)]
            for p_ in range(2):
                P.add("dve", (lambda e, p_=p_: e.memset(Fr[:, 9 + p_, :].bitcast(BF16)[:, 0:512], 0.0)),
                      writes=[("F", 9 + p_)])

            def emit_S(i):
                hd, u, kt, d, q0, ql = geom(i)
                par = i % 2
                ipar = it_of[i] % 2
                if kt == 0:
                    P.add("dve", lambda e: e.tensor_copy(out=qz[ipar][0:64, 0, :], in_=A[0:64, hd, ql:ql + 256]),
                          reads=[("A", hd, ql // 512)], writes=[("F", 9 + ipar)])
                    P.add("dve", lambda e: e.tensor_copy(out=qz[ipar][64:128, 1, :], in_=A[64:128, hd, ql:ql + 256]),
                          reads=[("A", hd, ql // 512)], writes=[("F", 9 + ipar)])
                for cm in range(2):
                    mm(PS[:, par, cm * 256 + q0:(cm + 1) * 256],
                       KT[:, hd, kt * 128:(kt + 1) * 128],
                       qz[ipar][:, cm, q0:256], True, True,
                       [("K", hd, kt // 8), ("F", 9 + ipar)], pst(par))

            def emit_exp(i):
                hd, u, kt, d, q0, ql = geom(i)
                par = i % 2
                r = i % 3
                P.add("act", lambda e: e.activation(
                    out=Eb[r][:, :, q0:256],
                    in_=PS[:, par, :].rearrange("p (c q) -> p c q", c=2)[:, :, q0:256], func=AF.Exp),
                    reads=pst(par), writes=[("F", r)])
                if d >= 0:
                    P.add("dve", lambda e: e.tensor_tensor(out=Eb[r][:, :, q0:q0 + 128], in0=Eb[r][:, :, q0:q0 + 128],
                                                           in1=tri_b, op=ALU.mult),
                          reads=[("F", r), "tri"], writes=[("F", r)])

            pending = []

            def emit_PV(i, it):
                hd, u, kt, d, q0, ql = geom(i)
                r = i % 3
                ipar = it % 2
                bA = 2 + ipar
                bS = 4 + ipar
                Ab = PS[:, bA, :].rearrange("p (c q) -> p c q", c=2)
                Sb = PS[:, bS, :].rearrange("p (c q) -> p c q", c=2)
                first = kt == 0
                last = kt == 2 * u + 1
                if q0 == 0:
                    E2 = Fr[:, r, :].bitcast(BF16)[:, 0:512]
                    mm(PS[:, bA, :], V[:, kt, hd * 128:(hd + 1) * 128], E2, first, last,
                       [("V", kt), ("F", r)], pst(bA))
                    mm(PS[:, bS, :], ones[:], E2, first, last, [("F", r), "ones"], pst(bS))
                else:
                    for cm in range(2):
                        mm(Ab[:, cm, q0:256], V[:, kt, hd * 128:(hd + 1) * 128], Eb[r][:, cm, q0:256], first,
                           last and cm == 1, [("V", kt), ("F", r)], pst(bA))
                    for cm in range(2):
                        mm(Sb[:, cm, q0:256], ones[:], Eb[r][:, cm, q0:256], first, last and cm == 1,
                           [("F", r), "ones"], pst(bS))
                if last:
                    b0 = 3 + 3 * ipar
                    rr = Fr[:, b0, :]
                    tt_ = Fr[:, b0 + 1, :]
                    ob = Fr[:, b0 + 2, 0:256]
                    P.add("dve", lambda e: e.reciprocal(out=rr, in_=PS[:, bS, :]),
                          reads=pst(bS), writes=[("F", b0)])
                    P.add("dve", lambda e: e.tensor_tensor(out=tt_, in0=PS[:, bA, :], in1=rr, op=ALU.mult),
                          reads=pst(bA) + [("F", b0)], writes=[("F", b0 + 1)])
                    P.add("dve", lambda e: e.scalar_tensor_tensor(out=H[:, hd, ql:ql + 256], in0=tt_[:, 256:512],
                                                                  scalar=neglam, in1=tt_[:, 0:256],
                                                                  op0=ALU.mult, op1=ALU.add),
                          reads=[("F", b0 + 1), "small"], writes=[("H", hd, ql // 512)])

            n = len(steps)
            emit_S(0)
            it = 0
            for i in range(n):
                if i + 1 < n:
                    emit_S(i + 1)
                emit_exp(i)
                emit_PV(i, it)
                if steps[i][2] == 2 * steps[i][1] + 1:
                    it += 1
                for pnd in list(pending):
                    pnd[0] -= 1
                    if pnd[0] <= 0:
                        pending.remove(pnd)
                        pnd[1]()
            for pnd in pending:
                pnd[1]()
            for tb in range(2):
                for hd in range(8):
                    rs, rtok = rstd_of(lambda j, hd=hd, tb=tb: H[:, hd, tbs(tb)], lambda j, hd=hd, tb=tb: ("H", hd, tb),
                                       512, 1.0 / 128.0, 1)
                    P.add("dve", (lambda e, hd=hd, tb=tb, rs=rs: e.scalar_tensor_tensor(
                        out=H[:, hd, tbs(tb)], in0=H[:, hd, tbs(tb)], scalar=gs, in1=rs,
                        op0=ALU.mult, op1=ALU.mult)),
                        reads=[("H", hd, tb), rtok, "small"], writes=[("H", hd, tb)])
            proj(wview(wo_d[jl]), 0, H, "H", evac_to_F)
            for tb in range(2):
                postnorm(gidx(l, 3), tb)

        def load_x(c):
            for tt in range(8):
                P.add("sp", (lambda e, tt=tt: e.dma_start(
                    out=Fr[:, 2 * tt:2 * tt + 2, :].rearrange("p a b -> p (a b)"),
                    in_=x_d[c * TCH + tt * 128:c * TCH + (tt + 1) * 128, :])),
                    writes=[("F", 2 * tt), ("F", 2 * tt + 1)], lane=("x", tt % 4))
            for j in range(8):
                for g in range(2):
                    bank = ctr["u"] % 4
                    ctr["u"] += 1
                    for ti in range(4):
                        tt = g * 4 + ti
                        xs = Fr[:, 2 * tt:2 * tt + 2, :].rearrange("p a b -> p (a b)")
                        P.add("pe", (lambda e, bank=bank, ti=ti, xs=xs, j=j: e.transpose(
                            PS[:, bank, ti * 128:(ti + 1) * 128], xs[:, j * 128:(j + 1) * 128], ident)),
                            reads=[("F", 2 * tt), ("F", 2 * tt + 1), "cst"], writes=pst(bank))
                    if (j + g) % 2 == 0:
                        P.add("act", (lambda e, bank=bank, j=j, g=g: e.copy(out=X[:, j, tbs(g)], in_=PS[:, bank, :])),
                              reads=pst(bank), writes=[("X", j, g)])
                    else:
                        P.add("dve", (lambda e, bank=bank, j=j, g=g: e.tensor_copy(out=X[:, j, tbs(g)], in_=PS[:, bank, :])),
                              reads=pst(bank), writes=[("X", j, g)])

        def store_x(c):
            for tt in range(8):
                for g in range(2):
                    bank = ctr["u"] % 4
                    ctr["u"] += 1
                    for ji in range(4):
                        j = g * 4 + ji
                        P.add("pe", (lambda e, bank=bank, ji=ji, j=j, tt=tt: e.transpose(
                            PS[:, bank, ji * 128:(ji + 1) * 128], X[:, j, tt * 128:(tt + 1) * 128], ident)),
                            reads=[("X", j, tt // 4), "cst"], writes=pst(bank))
                    blk = 2 * tt + g
                    if g == 0:
                        P.add("act", (lambda e, bank=bank, blk=blk: e.copy(out=Fr[:, blk, :], in_=PS[:, bank, :])),
                              reads=pst(bank), writes=[("F", blk)])
                    else:
                        P.add("dve", (lambda e, bank=bank, blk=blk: e.tensor_copy(out=Fr[:, blk, :], in_=PS[:, bank, :])),
                              reads=pst(bank), writes=[("F", blk)])
                P.add("sp", (lambda e, tt=tt: e.dma_start(
                    out=out_d[c * TCH + tt * 128:c * TCH + (tt + 1) * 128, :],
                    in_=Fr[:, 2 * tt:2 * tt + 2, :].rearrange("p a b -> p (a b)"))),
                    reads=[("F", 2 * tt), ("F", 2 * tt + 1)], writes=[("OUT", c, tt)], lane=("o", tt % 4))

        for c in range(NCH):
            load_x(c)
            nsub = 0

            def go():
                return nsub < nstop

            for l in range(4):
                if go():
                    ffn(l, 0)
                nsub += 1
                if go():
                    if l < 2:
                        conv_mixer(l, c)
                    else:
                        attn(l, c)
                nsub += 1
                if go():
                    ffn(l, 1)
                nsub += 1
                if l == 1 and go():
                    kv_proj(c)
            store_x(c)
        P.add("sp", None, reads=[("OUT", c, tt) for c in range(NCH) for tt in range(8)])
        P.emit(nc, st)
    nc._prog_stats = P.stats
    return nc


_CACHE = {}


def _consts():
    c = np.zeros((128, 256), np.float32)
    c[:, 0:128] = np.eye(128, dtype=np.float32)
    c[:, 128:256] = np.triu(np.ones((128, 128), np.float32))
    return c


def kernel(**inputs):
    nstop = int(inputs.pop("_nstop", 999))
    ncores = int(inputs.pop("_ncores", NB))
    if nstop not in _CACHE:
        _CACHE[nstop] = build_nc(nstop)
    nc = _CACHE[nstop]
    names = ["g_norm", "w_ffn_gate", "w_ffn_up", "w_ffn_down", "w_conv_in", "w_conv", "w_conv_out", "g_kv",
             "w_kv", "w_q", "lambda_q1", "lambda_k1", "lambda_q2", "lambda_k2", "g_subln", "w_o"]
    shared = {k: np.asarray(inputs[k], dtype=np.float32) for k in names}
    if DBG["lite"]:
        for k in ("w_ffn_gate", "w_ffn_up", "w_ffn_down"):
            shared[k] = shared[k][:1, :1]
        for k in ("w_conv_in", "w_conv_out", "w_q", "w_o"):
            shared[k] = shared[k][:1]
    shared = {k: np.ascontiguousarray(v) for k, v in shared.items()}
    shared["consts"] = _consts()
    x = np.asarray(inputs["x"], dtype=np.float32)
    in_maps = []
    for b in range(ncores):
        m = dict(shared)
        m["x"] = np.ascontiguousarray(x[b])
        in_maps.append(m)
    res = run_bass_kernel_spmd(nc, in_maps, core_ids=list(range(ncores)))
    out = np.stack([np.asarray(res.results[b]["out"], dtype=np.float32) for b in range(ncores)], axis=0)
    return out
```

```python
import math
from contextlib import ExitStack

import numpy as np
import concourse.bass as bass
import concourse.mybir as mybir
from concourse.bass_utils import run_bass_kernel_spmd

F32 = mybir.dt.float32
BF16 = mybir.dt.bfloat16
AF = mybir.ActivationFunctionType
ALU = mybir.AluOpType
AX = mybir.AxisListType

D = 1024
SEQ = 2048
NB = 8
DFF = 2816
EPS = 1e-6
TCH = 1024
NCH = SEQ // TCH
SAME_ENG_SYNC = True


class _Op:
    __slots__ = ("eng", "fn", "dma", "lane", "deps", "idx", "ms", "cnt", "waits")


class Prog:
    def __init__(self):
        self.ops = []
        self.last_w = {}
        self.readers = {}
        self.lane_last = {}

    def add(self, eng, fn, reads=(), writes=(), lane=None):
        op = _Op()
        op.eng = eng
        op.fn = fn
        op.dma = lane is not None
        op.lane = lane
        op.idx = len(self.ops)
        op.ms = False
        op.cnt = 0
        deps = set()
        for t in reads:
            w = self.last_w.get(t)
            if w is not None:
                deps.add(w)
        for t in writes:
            w = self.last_w.get(t)
            if w is not None:
                deps.add(w)
            rs = self.readers.get(t)
            if rs:
                deps.update(rs)
        if lane is not None:
            p = self.lane_last.get(lane)
            if p is not None:
                deps.add(p)
            self.lane_last[lane] = op.idx
        for t in reads:
            self.readers.setdefault(t, []).append(op.idx)
        for t in writes:
            self.last_w[t] = op.idx
            self.readers[t] = []
        deps.discard(op.idx)
        op.deps = deps
        self.ops.append(op)
        return op

    @staticmethod
    def _needs_sync(p, op):
        if p.dma:
            return True
        if p.eng != op.eng:
            return True
        if p.eng == "pe":
            return False
        return SAME_ENG_SYNC

    def emit(self, nc, stack):
        ops = self.ops
        for op in ops:
            best = {}
            for d in op.deps:
                p = ops[d]
                if (not p.dma) and self._needs_sync(p, op):
                    if best.get(p.eng, -1) < d:
                        best[p.eng] = d
            for d in best.values():
                ops[d].ms = True
        eng_cnt = {}
        lane_cnt = {}
        for op in ops:
            if op.dma:
                lane_cnt[op.lane] = lane_cnt.get(op.lane, 0) + 1
                op.cnt = 16 * lane_cnt[op.lane]
            elif op.ms:
                eng_cnt[op.eng] = eng_cnt.get(op.eng, 0) + 1
                op.cnt = eng_cnt[op.eng]
        waited = {}
        for op in ops:
            need = {}
            for d in op.deps:
                p = ops[d]
                if not self._needs_sync(p, op):
                    continue
                key = ("lane", p.lane) if p.dma else ("eng", p.eng)
                if need.get(key, 0) < p.cnt:
                    need[key] = p.cnt
                assert p.dma or p.cnt > 0 or any(
                    (not ops[d2].dma) and ops[d2].eng == p.eng and d2 > d for d2 in op.deps)
            wd = waited.setdefault(op.eng, {})
            op.waits = []
            for k, v in need.items():
                if wd.get(k, 0) < v:
                    op.waits.append((k, v))
                    wd[k] = v
        sems = {}
        for e in sorted(eng_cnt):
            sems[("eng", e)] = stack.enter_context(nc.semaphore("s_" + e))
        for i, l in enumerate(lane_cnt):
            sems[("lane", l)] = stack.enter_context(nc.semaphore("l_%d" % i))
        per = {}
        for op in ops:
            per.setdefault(op.eng, []).append(op)
        self.stats = {e: len(v) for e, v in per.items()}
        self.stats["sems"] = len(sems)

        def mk(name):
            lst = per.get(name, [])

            def body(e):
                for op in lst:
                    for k, v in op.waits:
                        e.wait_ge(sems[k], v)
                    if op.fn is None:
                        continue
                    ins = op.fn(e)
                    if op.dma:
                        ins.then_inc(sems[("lane", op.lane)], 16)
                    elif op.ms:
                        ins.then_inc(sems[("eng", op.eng)], 1)

            return body

        with nc.Block() as block:
            block.tensor(mk("pe"))
            block.scalar(mk("act"))
            block.vector(mk("dve"))
            block.gpsimd(mk("pool"))
            block.sync(mk("sp"))


def lambda_init(layer):
    return 0.8 - 0.6 * math.exp(-0.3 * layer)


DBG = {"ffn_stage": 99, "lite": False}


def build_nc(nstop=999):
    nc = bass.Bass("TRN2", target_bir_lowering=False)
    nl4 = 1 if DBG["lite"] else 4
    nl2 = 1 if DBG["lite"] else 2

    def din(name, shape):
        return nc.dram_tensor(name, list(shape), F32, kind="ExternalInput").ap()

    x_d = din("x", (SEQ, D))
    gnorm_d = din("g_norm", (4, 6, D))
    wg_d = din("w_ffn_gate", (nl4, nl2, D, DFF))
    wu_d = din("w_ffn_up", (nl4, nl2, D, DFF))
    wd_d = din("w_ffn_down", (nl4, nl2, DFF, D))
    wci_d = din("w_conv_in", (nl2, D, 3 * D))
    wc_d = din("w_conv", (2, 3, D))
    wco_d = din("w_conv_out", (nl2, D, D))
    gkv_d = din("g_kv", (D,))
    wkv_d = din("w_kv", (D, 2 * D))
    wq_d = din("w_q", (nl2, D, D))
    lq1_d = din("lambda_q1", (2, 64))
    lk1_d = din("lambda_k1", (2, 64))
    lq2_d = din("lambda_q2", (2, 64))
    lk2_d = din("lambda_k2", (2, 64))
    gsub_d = din("g_subln", (2, 128))
    wo_d = din("w_o", (nl2, D, D))
    cst_d = din("consts", (128, 256))
    out_d = nc.dram_tensor("out", [SEQ, D], F32, kind="ExternalOutput").ap()

    P = Prog()
    with ExitStack() as st:
        def sb(name, shape, dt):
            return st.enter_context(nc.sbuf_tensor(name, list(shape), dt))

        X = sb("X", (128, 8, TCH), F32)
        H = sb("H", (128, 8, TCH), BF16)
        Fr = sb("Fr", (128, 16, 512), F32)
        A = sb("A", (128, 11, TCH), BF16)
        KT = sb("KT", (128, 8, SEQ), BF16)
        V = sb("V", (128, 16, D), BF16)
        slots = [sb("slot%d" % i, (128, 6144), BF16) for i in range(2)]
        cst = sb("cst", (128, 256), F32)
        tri = sb("tri", (128, 128), BF16)
        ones = sb("ones", (128, 128), BF16)
        cols = sb("cols", (128, 256), F32)
        eps_t = sb("eps", (128, 1), F32)
        sq = [sb("sq%d" % i, (128, 512), BF16) for i in range(3)]
        sd = sb("sd", (128, 512), F32)
        rstd = [sb("rstd%d" % i, (128, 512), F32) for i in range(2)]
        sg = [sb("sg%d" % i, (128, 512), F32) for i in range(2)]
        halo = sb("halo", (128, 2, 8, 2), F32)
        small = sb("small", (128, 16), F32)
        PS = st.enter_context(nc.psum_tensor("PS", [128, 8, 512], F32))
        ident = cst[:, 0:128]

        def pst(b):
            return [("ps", b, 0), ("ps", b, 1)]

        def tbs(tb):
            return slice(tb * 512, (tb + 1) * 512)

        def mm(out, lhsT, rhs, start, stop, reads, writes):
            P.add("pe", lambda e: e.matmul(out, lhsT, rhs, start=start, stop=stop), reads, writes)

        ctr = {"slot": 0, "stat": 0, "sq": 0, "rstd": 0, "u": 0}

        def next_slot():
            s = ctr["slot"] % 2
            ctr["slot"] += 1
            return s

        def stoks(s, a, b):
            return [("slot", s, i) for i in range(a, b)]

        P.add("dve", lambda e: e.memset(ones[:], 1.0), writes=["ones"])
        P.add("dve", lambda e: e.memset(eps_t[:], EPS), writes=["eps"])
        P.add("dve", lambda e: e.memset(small[:, 12:13], -0.5), writes=["mhalf"])
        P.add("sp", lambda e: e.dma_start(out=cst[:], in_=cst_d), writes=["cst"], lane="cst")
        P.add("dve", lambda e: e.tensor_copy(out=tri[:], in_=cst[:, 128:256]), reads=["cst"], writes=["tri"])
        gview = gnorm_d.rearrange("l n (j p) -> (l n j) p", p=128)
        P.add("sp", lambda e: e.dma_start(out=Fr[0:128, 0, 0:128], in_=gview[0:128, :]),
              writes=[("F", 0)], lane="p0")
        P.add("sp", lambda e: e.dma_start(out=Fr[0:64, 1, 0:128], in_=gview[128:192, :]),
              writes=[("F", 1)], lane="p1")
        P.add("sp", lambda e: e.dma_start(out=Fr[64:72, 1, 0:128], in_=gkv_d.rearrange("(j p) -> j p", p=128)),
              writes=[("F", 1)], lane="p2")
        P.add("sp", lambda e: e.dma_start(out=Fr[72:120, 1, 0:128],
                                          in_=wc_d.rearrange("l w (j p) -> (l w j) p", p=128)),
              writes=[("F", 1)], lane="p3")
        P.add("sp", lambda e: e.dma_start(out=Fr[120:122, 1, 0:128], in_=gsub_d),
              writes=[("F", 1)], lane="p4")
        P.add("pe", lambda e: e.transpose(PS[:, 0, 0:128], Fr[0:128, 0, 0:128], ident),
              reads=[("F", 0), "cst"], writes=pst(0))
        P.add("pe", lambda e: e.transpose(PS[:, 0, 128:250], Fr[0:122, 1, 0:128], cst[0:122, 0:122]),
              reads=[("F", 1), "cst"], writes=pst(0))
        P.add("dve", lambda e: e.tensor_copy(out=cols[:, 0:250], in_=PS[:, 0, 0:250]), reads=pst(0), writes=["cols"])
        gv4 = cols[:, 0:192].rearrange("p (l n j) -> p l n j", l=4, n=6)
        for n in (1, 5):
            P.add("dve", (lambda e, n=n: e.tensor_scalar(out=gv4[:, :, n, :], in0=gv4[:, :, n, :], scalar1=0.5,
                                                         scalar2=None, op0=ALU.mult)),
                  reads=["cols"], writes=["cols"])
        lv = Fr[:, 2, :].rearrange("p (a b) -> p a b", a=4)
        for i, ld in enumerate((lq1_d, lk1_d, lq2_d, lk2_d)):
            P.add("sp", (lambda e, i=i, ld=ld: e.dma_start(
                out=lv[:, i, :], in_=ld.rearrange("a b -> (a b)").partition_broadcast(128))),
                writes=[("F", 2)], lane="l%d" % i)
        pr = Fr[:, 3, 0:256].rearrange("p (a b) -> p a b", a=2)
        P.add("dve", lambda e: e.tensor_tensor(out=pr[:, 0, :], in0=lv[:, 0, :], in1=lv[:, 1, :], op=ALU.mult),
              reads=[("F", 2)], writes=[("F", 3)])
        P.add("dve", lambda e: e.tensor_tensor(out=pr[:, 1, :], in0=lv[:, 2, :], in1=lv[:, 3, :], op=ALU.mult),
              reads=[("F", 2)], writes=[("F", 3)])
        P.add("dve", lambda e: e.reduce_sum(out=small[:, 0:4],
                                            in_=Fr[:, 3, 0:256].rearrange("p (a b) -> p a b", a=4), axis=AX.X),
              reads=[("F", 3)], writes=["small"])
        P.add("act", lambda e: e.activation(out=small[:, 4:8], in_=small[:, 0:4], func=AF.Exp),
              reads=["small"], writes=["small"])
        P.add("dve", lambda e: e.tensor_tensor(out=small[:, 8:10], in0=small[:, 6:8], in1=small[:, 4:6],
                                               op=ALU.subtract), reads=["small"], writes=["small"])
        for jl in range(2):
            li = lambda_init(jl + 2)
            P.add("dve", (lambda e, jl=jl, li=li: e.tensor_scalar(out=small[:, 8 + jl:9 + jl], in0=small[:, 8 + jl:9 + jl],
                                                                  scalar1=-li, scalar2=None, op0=ALU.add)),
                  reads=["small"], writes=["small"])
            P.add("dve", (lambda e, jl=jl, li=li: e.tensor_scalar(out=small[:, 10 + jl:11 + jl],
                                                                  in0=cols[:, 248 + jl:249 + jl],
                                                                  scalar1=1.0 - li, scalar2=None, op0=ALU.mult)),
                  reads=["small", "cols"], writes=["small"])

        def gidx(l, n):
            return (l * 6 + n) * 8

        def rstd_of(src_fn, src_tok, ncols, invd, nj):
            b = 6 + ctr["stat"] % 2
            ctr["stat"] += 1
            ssb = PS[:, b, 0:ncols]
            for j in range(nj):
                r = ctr["sq"] % 3
                ctr["sq"] += 1
                P.add("act", (lambda e, j=j, r=r: e.activation(out=sq[r][:, 0:ncols], in_=src_fn(j), func=AF.Square)),
                      reads=[src_tok(j)], writes=[("sq", r)])
                mm(ssb, ones[:], sq[r][:, 0:ncols], j == 0, j == nj - 1, [("sq", r), "ones"], pst(b))
            P.add("act", lambda e: e.activation(out=sd[:, 0:ncols], in_=ssb, func=AF.Sqrt, bias=eps_t[:, 0:1],
                                                scale=invd), reads=pst(b) + ["eps"], writes=["sd"])
            rb = ctr["rstd"] % 2
            ctr["rstd"] += 1
            P.add("dve", lambda e: e.reciprocal(out=rstd[rb][:, 0:ncols], in_=sd[:, 0:ncols]),
                  reads=["sd"], writes=[("rstd", rb)])
            return rstd[rb][:, 0:ncols], ("rstd", rb)

        def prenorm(gbase, tb):
            rs, rtok = rstd_of(lambda j: X[:, j, tbs(tb)], lambda j: ("X", j, tb), 512, 1.0 / D, 8)
            for j in range(8):
                P.add("dve", (lambda e, j=j: e.scalar_tensor_tensor(
                    out=H[:, j, tbs(tb)], in0=X[:, j, tbs(tb)], scalar=cols[:, gbase + j:gbase + j + 1], in1=rs,
                    op0=ALU.mult, op1=ALU.mult)),
                    reads=[("X", j, tb), rtok, "cols"], writes=[("H", j, tb)])

        def postnorm(gbase, tb):
            rs, rtok = rstd_of(lambda j: Fr[:, 2 * j + tb, :], lambda j: ("F", 2 * j + tb), 512, 1.0 / D, 8)
            for j in range(8):
                blk = 2 * j + tb
                P.add("dve", (lambda e, j=j, blk=blk: e.scalar_tensor_tensor(
                    out=Fr[:, blk, :], in0=Fr[:, blk, :], scalar=cols[:, gbase + j:gbase + j + 1], in1=rs,
                    op0=ALU.mult, op1=ALU.mult)),
                    reads=[("F", blk), rtok, "cols"], writes=[("F", blk)])
                P.add("dve", (lambda e, j=j, blk=blk: e.tensor_tensor(
                    out=X[:, j, tbs(tb)], in0=X[:, j, tbs(tb)], in1=Fr[:, blk, :], op=ALU.add)),
                    reads=[("F", blk), ("X", j, tb)], writes=[("X", j, tb)])

        def wview(w2d):
            return w2d.rearrange("(k p) f -> p k f", p=128)

        def proj(Wv, c0, src, srctok, evac):
            pcs = []
            for mh in range(2):
                s = next_slot()
                wv = slots[s][:, 0:4096].rearrange("p (k f) -> p k f", k=8)
                P.add("pool", (lambda e, wv=wv, mh=mh: e.dma_start(out=wv, in_=Wv[:, :, c0 + mh * 512:c0 + (mh + 1) * 512])),
                      writes=stoks(s, 0, 4), lane=("w", s, 0))
                pcs.append((s, wv))
            for tb in range(2):
                for m in range(8):
                    s, wv = pcs[m // 4]
                    mi = m % 4
                    bank = ctr["u"] % 2
                    ctr["u"] += 1
                    Pp = PS[:, bank, :]
                    for k in range(8):
                        mm(Pp, wv[:, k, mi * 128:(mi + 1) * 128], src[:, k, tbs(tb)], k == 0, k == 7,
                           stoks(s, 0, 4) + [(srctok, k, tb)], pst(bank))
                    evac(m, tb, Pp, bank)

        def evac_to_F(m, tb, Pp, bank):
            blk = 2 * m + tb
            P.add("act", lambda e: e.copy(out=Fr[:, blk, :], in_=Pp), reads=pst(bank), writes=[("F", blk)])

        def ffn(l, w):
            gpre = gidx(l, 0 if w == 0 else 4)
            gpost = gidx(l, 1 if w == 0 else 5)
            for tb in range(2):
                prenorm(gpre, tb)
            if DBG["ffn_stage"] <= 1:
                return
            Wg = wview(wg_d[l, w])
            Wu = wview(wu_d[l, w])
            Wd = wd_d[l, w].rearrange("(f p) m -> p f m", p=128)
            for hf in range(2):
                for (f0, n) in ((0, 3), (3, 3), (6, 3), (9, 2)):
                    s = next_slot()
                    gv = slots[s][:, 0:8 * n * 128].rearrange("p (k f) -> p k f", k=8)
                    uv = slots[s][:, 3072:3072 + 8 * n * 128].rearrange("p (k f) -> p k f", k=8)
                    fa = hf * 11 + f0
                    P.add("pool", (lambda e, gv=gv, fa=fa, n=n: e.dma_start(out=gv, in_=Wg[:, :, fa * 128:(fa + n) * 128])),
                          writes=stoks(s, 0, 3), lane=("w", s, 0))
                    P.add("pool", (lambda e, uv=uv, fa=fa, n=n: e.dma_start(out=uv, in_=Wu[:, :, fa * 128:(fa + n) * 128])),
                          writes=stoks(s, 3, 6), lane=("w", s, 3))
                    order = ([(fi, tb) for tb in range(2) for fi in range(n)] if (hf == 0 and f0 == 0)
                             else [(fi, tb) for fi in range(n) for tb in range(2)])
                    for (fi, tb) in order:
                        fl = f0 + fi
                        if True:
                            par = ctr["u"] % 2
                            ctr["u"] += 1
                            G = PS[:, 2 * par, :]
                            U = PS[:, 2 * par + 1, :]
                            for k in range(8):
                                mm(G, gv[:, k, fi * 128:(fi + 1) * 128], H[:, k, tbs(tb)], k == 0, k == 7,
                                   stoks(s, 0, 3) + [("H", k, tb)], pst(2 * par))
                            for k in range(8):
                                mm(U, uv[:, k, fi * 128:(fi + 1) * 128], H[:, k, tbs(tb)], k == 0, k == 7,
                                   stoks(s, 3, 6) + [("H", k, tb)], pst(2 * par + 1))
                            P.add("act", (lambda e, G=G, par=par: e.activation(out=sg[par][:], in_=G, func=AF.Silu)),
                                  reads=pst(2 * par), writes=[("sg", par)])
                            P.add("dve", (lambda e, U=U, par=par, fl=fl, tb=tb: e.tensor_tensor(
                                out=A[:, fl, tbs(tb)], in0=U, in1=sg[par][:], op=ALU.mult)),
                                reads=pst(2 * par + 1) + [("sg", par)], writes=[("A", fl, tb)])
                    if DBG["ffn_stage"] <= 2:
                        return
                pieces = []
                for m0 in (0, 4):
                    s = next_slot()
                    dv = slots[s][:, 0:5632].rearrange("p (f m) -> p f m", f=11)
                    P.add("pool", (lambda e, dv=dv, hf=hf, m0=m0: e.dma_start(
                        out=dv, in_=Wd[:, hf * 11:(hf + 1) * 11, m0 * 128:(m0 + 4) * 128])),
                        writes=stoks(s, 0, 6), lane=("w", s, 0))
                    pieces.append((s, dv))
                if hf == 0:
                    dorder = [(m, tb) for m in range(8) for tb in range(2)]
                else:
                    dorder = [(m, tb) for tb in range(2) for m in range(8)]
                for (m, tb) in dorder:
                    s, dv = pieces[m // 4]
                    mi = m % 4
                    par = ctr["u"] % 2
                    ctr["u"] += 1
                    Dp = PS[:, 4 + par, :]
                    for fl in range(11):
                        mm(Dp, dv[:, fl, mi * 128:(mi + 1) * 128], A[:, fl, tbs(tb)], fl == 0, fl == 10,
                           stoks(s, 0, 6) + [("A", fl, tb)], pst(4 + par))
                    blk = 2 * m + tb
                    if hf == 0:
                        P.add("act", (lambda e, Dp=Dp, blk=blk: e.copy(out=Fr[:, blk, :], in_=Dp)),
                              reads=pst(4 + par), writes=[("F", blk)])
                    else:
                        P.add("dve", (lambda e, Dp=Dp, blk=blk: e.tensor_tensor(
                            out=Fr[:, blk, :], in0=Dp, in1=Fr[:, blk, :], op=ALU.add)),
                            reads=pst(4 + par) + [("F", blk)], writes=[("F", blk)])
                if DBG["ffn_stage"] <= 3:
                    return
            for tb in range(2):
                postnorm(gpost, tb)

        def conv_mixer(l, c):
            for tb in range(2):
                prenorm(gidx(l, 2), tb)
            Win = wview(wci_d[l])
            for jp in range(4):
                s = next_slot()
                parts = [slots[s][:, pt * 2048:(pt + 1) * 2048].rearrange("p (k f) -> p k f", k=8) for pt in range(3)]
                for pt in range(3):
                    P.add("pool", (lambda e, pt=pt, jp=jp, parts=parts: e.dma_start(
                        out=parts[pt], in_=Win[:, :, pt * 1024 + jp * 256:pt * 1024 + (jp + 1) * 256])),
                        writes=stoks(s, 2 * pt, 2 * pt + 2), lane=("w", s, 2 * pt))
                cus = []
                for jj in range(2):
                    j = 2 * jp + jj
                    cub = j % 2
                    cu = Fr[:, cub * 3:cub * 3 + 3, :].rearrange("p a b -> p (a b)")
                    cutoks = [("F", cub * 3 + i) for i in range(3)]
                    if c == 0:
                        P.add("dve", (lambda e, cu=cu: e.memset(cu[:, 0:2], 0.0)), writes=cutoks)
                    else:
                        P.add("dve", (lambda e, cu=cu, j=j: e.tensor_copy(out=cu[:, 0:2], in_=halo[:, l, j, :])),
                              reads=[("halo", l, j)], writes=cutoks)
                    cus.append((cu, cutoks))
                for tb in range(2):
                    for jj in range(2):
                        j = 2 * jp + jj
                        cu, cutoks = cus[jj]
                        w0 = 200 + (l * 3 + 0) * 8 + j
                        w1 = 200 + (l * 3 + 1) * 8 + j
                        w2 = 200 + (l * 3 + 2) * 8 + j
                        par = ctr["u"] % 2
                        ctr["u"] += 1
                        Pb, Pc, Pu = (PS[:, 3 * par + i, :] for i in range(3))
                        for pt in range(3):
                            for k in range(8):
                                mm(PS[:, 3 * par + pt, :], parts[pt][:, k, jj * 128:(jj + 1) * 128], H[:, k, tbs(tb)],
                                   k == 0, k == 7, stoks(s, 2 * pt, 2 * pt + 2) + [("H", k, tb)], pst(3 * par + pt))
                        ucp = Fr[:, 6 + par, :]
                        z = Fr[:, 8 + par, :]
                        o2 = 2 + tb * 512
                        P.add("act", (lambda e, ucp=ucp, Pu=Pu: e.copy(out=ucp, in_=Pu)),
                              reads=pst(3 * par + 2), writes=[("F", 6 + par)])
                        P.add("dve", (lambda e, cu=cu, o2=o2, Pc=Pc, ucp=ucp: e.tensor_tensor(
                            out=cu[:, o2:o2 + 512], in0=Pc, in1=ucp, op=ALU.mult)),
                            reads=pst(3 * par + 1) + [("F", 6 + par)], writes=cutoks)
                        P.add("act", (lambda e, z=z, cu=cu, o2=o2, w2=w2: e.mul(out=z, in_=cu[:, o2:o2 + 512],
                                                                               mul=cols[:, w2:w2 + 1])),
                              reads=cutoks + ["cols"], writes=[("F", 8 + par)])
                        P.add("dve", (lambda e, z=z, cu=cu, o2=o2, w1=w1: e.scalar_tensor_tensor(
                            out=z, in0=cu[:, o2 - 1:o2 + 511], scalar=cols[:, w1:w1 + 1], in1=z,
                            op0=ALU.mult, op1=ALU.add)), reads=cutoks + ["cols", ("F", 8 + par)],
                            writes=[("F", 8 + par)])
                        P.add("dve", (lambda e, z=z, cu=cu, o2=o2, w0=w0: e.scalar_tensor_tensor(
                            out=z, in0=cu[:, o2 - 2:o2 + 510], scalar=cols[:, w0:w0 + 1], in1=z,
                            op0=ALU.mult, op1=ALU.add)), reads=cutoks + ["cols", ("F", 8 + par)],
                            writes=[("F", 8 + par)])
                        P.add("dve", (lambda e, z=z, Pb=Pb, j=j, tb=tb: e.tensor_tensor(
                            out=A[:, j, tbs(tb)], in0=Pb, in1=z, op=ALU.mult)),
                            reads=pst(3 * par) + [("F", 8 + par)], writes=[("A", j, tb)])
                if c == 0:
                    for jj in range(2):
                        j = 2 * jp + jj
                        cu, cutoks = cus[jj]
                        P.add("dve", (lambda e, cu=cu, j=j: e.tensor_copy(out=halo[:, l, j, :], in_=cu[:, 1024:1026])),
                              reads=cutoks, writes=[("halo", l, j)])
            proj(wview(wco_d[l]), 0, A, "A", evac_to_F)
            for tb in range(2):
                postnorm(gidx(l, 3), tb)

        def kv_proj(c):
            for tb in range(2):
                prenorm(192, tb)
            Wkv = wview(wkv_d)

            def evac_k(m, tb, Pp, bank):
                P.add("act", lambda e: e.copy(out=KT[:, m, c * TCH + tb * 512:c * TCH + (tb + 1) * 512], in_=Pp),
                      reads=pst(bank), writes=[("K", m, c)])

            proj(Wkv, 0, H, "H", evac_k)
            for eh in range(2):
                s = next_slot()
                wv = slots[s][:, 0:4096].rearrange("p (k f) -> p k f", k=8)
                P.add("pool", (lambda e, wv=wv, eh=eh: e.dma_start(out=wv, in_=Wkv[:, :, D + eh * 512:D + (eh + 1) * 512])),
                      writes=stoks(s, 0, 4), lane=("w", s, 0))
                for tt in range(8):
                    bank = ctr["u"] % 2
                    ctr["u"] += 1
                    Pp = PS[:, bank, :]
                    for k in range(8):
                        mm(Pp, H[:, k, tt * 128:(tt + 1) * 128], wv[:, k, :], k == 0, k == 7,
                           stoks(s, 0, 4) + [("H", k, tt // 4)], pst(bank))
                    kt = c * 8 + tt
                    if tt % 2 == 0:
                        P.add("act", (lambda e, kt=kt, eh=eh, Pp=Pp: e.copy(out=V[:, kt, eh * 512:(eh + 1) * 512], in_=Pp)),
                              reads=pst(bank), writes=[("V", kt)])
                    else:
                        P.add("dve", (lambda e, kt=kt, eh=eh, Pp=Pp: e.tensor_copy(out=V[:, kt, eh * 512:(eh + 1) * 512], in_=Pp)),
                              reads=pst(bank), writes=[("V", kt)])

        def attn(l, c):
            jl = l - 2
            for tb in range(2):
                prenorm(gidx(l, 2), tb)

            def evac_q(m, tb, Pp, bank):
                P.add("act", lambda e: e.mul(out=A[:, m, tbs(tb)], in_=Pp, mul=0.125),
                      reads=pst(bank), writes=[("A", m, tb)])

            proj(wview(wq_d[jl]), 0, H, "H", evac_q)

            steps = []
            for hd in range(8):
                for u in range(4 * c, 4 * c + 4):
                    for kt in range(2 * u + 2):
                        steps.append((hd, u, kt))
            Eb = [Fr[:, r, :].bitcast(BF16)[:, 0:512].rearrange("p (c q) -> p c q", c=2) for r in range(3)]
            neglam = small[:, 8 + jl:9 + jl]
            gs = small[:, 10 + jl:11 + jl]
            tri_b = tri[:].unsqueeze(1).to_broadcast([128, 2, 128])

            def geom(i):
                hd, u, kt = steps[i]
                d = kt - 2 * u
                q0 = 128 if d == 1 else 0
                ql = (u - 4 * c) * 256
                return hd, u, kt, d, q0, ql

            it_of = []
            _it = 0
            for (hd_, u_, kt_) in steps:
                it_of.append(_it)
                if kt_ == 2 * u_ + 1:
                    _it += 1
            qz = [Fr[:, 9 + p_, :].bitcast(BF16)[:, 0:512].rearrange("p (c q) -> p c q", c=2) for p_ in range(2)]
            for p_ in range(2):
                P.add("dve", (lambda e, p_=p_: e.memset(Fr[:, 9 + p_, :].bitcast(BF16)[:, 0:512], 0.0)),
                      writes=[("F", 9 + p_)])

            def emit_S(i):
                hd, u, kt, d, q0, ql = geom(i)
                par = i % 2
                ipar = it_of[i] % 2
                if kt == 0:
                    P.add("dve", lambda e: e.tensor_copy(out=qz[ipar][0:64, 0, :], in_=A[0:64, hd, ql:ql + 256]),
                          reads=[("A", hd, ql // 512)], writes=[("F", 9 + ipar)])
                    P.add("dve", lambda e: e.tensor_copy(out=qz[ipar][64:128, 1, :], in_=A[64:128, hd, ql:ql + 256]),
                          reads=[("A", hd, ql // 512)], writes=[("F", 9 + ipar)])
                for cm in range(2):
                    mm(PS[:, par, cm * 256 + q0:(cm + 1) * 256],
                       KT[:, hd, kt * 128:(kt + 1) * 128],
                       qz[ipar][:, cm, q0:256], True, True,
                       [("K", hd, kt // 8), ("F", 9 + ipar)], pst(par))

            def emit_exp(i):
                hd, u, kt, d, q0, ql = geom(i)
                par = i % 2
                r = i % 3
                P.add("act", lambda e: e.activation(
                    out=Eb[r][:, :, q0:256],
                    in_=PS[:, par, :].rearrange("p (c q) -> p c q", c=2)[:, :, q0:256], func=AF.Exp),
                    reads=pst(par), writes=[("F", r)])
                if d >= 0:
                    P.add("dve", lambda e: e.tensor_tensor(out=Eb[r][:, :, q0:q0 + 128], in0=Eb[r][:, :, q0:q0 + 128],
                                                           in1=tri_b, op=ALU.mult),
                          reads=[("F", r), "tri"], writes=[("F", r)])

            pending = []

            def emit_PV(i, it):
                hd, u, kt, d, q0, ql = geom(i)
                r = i % 3
                ipar = it % 2
                bA = 2 + ipar
                bS = 4 + ipar
                Ab = PS[:, bA, :].rearrange("p (c q) -> p c q", c=2)
                Sb = PS[:, bS, :].rearrange("p (c q) -> p c q", c=2)
                first = kt == 0
                last = kt == 2 * u + 1
                if q0 == 0:
                    E2 = Fr[:, r, :].bitcast(BF16)[:, 0:512]
                    mm(PS[:, bA, :], V[:, kt, hd * 128:(hd + 1) * 128], E2, first, last,
                       [("V", kt), ("F", r)], pst(bA))
                    mm(PS[:, bS, :], ones[:], E2, first, last, [("F", r), "ones"], pst(bS))
                else:
                    for cm in range(2):
                        mm(Ab[:, cm, q0:256], V[:, kt, hd * 128:(hd + 1) * 128], Eb[r][:, cm, q0:256], first,
                           last and cm == 1, [("V", kt), ("F", r)], pst(bA))
                    for cm in range(2):
                        mm(Sb[:, cm, q0:256], ones[:], Eb[r][:, cm, q0:256], first, last and cm == 1,
                           [("F", r), "ones"], pst(bS))
                if last:
                    b0 = 3 + 3 * ipar
                    rr = Fr[:, b0, :]
                    tt_ = Fr[:, b0 + 1, :]
                    ob = Fr[:, b0 + 2, 0:256]
                    P.add("dve", lambda e: e.reciprocal(out=rr, in_=PS[:, bS, :]),
                          reads=pst(bS), writes=[("F", b0)])
                    P.add("dve", lambda e: e.tensor_tensor(out=tt_, in0=PS[:, bA, :], in1=rr, op=ALU.mult),
                          reads=pst(bA) + [("F", b0)], writes=[("F", b0 + 1)])
                    P.add("dve", lambda e: e.scalar_tensor_tensor(out=H[:, hd, ql:ql + 256], in0=tt_[:, 256:512],
                                                                  scalar=neglam, in1=tt_[:, 0:256],
                                                                  op0=ALU.mult, op1=ALU.add),
                          reads=[("F", b0 + 1), "small"], writes=[("H", hd, ql // 512)])

            n = len(steps)
            emit_S(0)
            it = 0
            for i in range(n):
                if i + 1 < n:
                    emit_S(i + 1)
                emit_exp(i)
                emit_PV(i, it)
                if steps[i][2] == 2 * steps[i][1] + 1:
                    it += 1
                for pnd in list(pending):
                    pnd[0] -= 1
                    if pnd[0] <= 0:
                        pending.remove(pnd)
                        pnd[1]()
            for pnd in pending:
                pnd[1]()
            for tb in range(2):
                for hd in range(8):
                    rs, rtok = rstd_of(lambda j, hd=hd, tb=tb: H[:, hd, tbs(tb)], lambda j, hd=hd, tb=tb: ("H", hd, tb),
                                       512, 1.0 / 128.0, 1)
                    P.add("dve", (lambda e, hd=hd, tb=tb, rs=rs: e.scalar_tensor_tensor(
                        out=H[:, hd, tbs(tb)], in0=H[:, hd, tbs(tb)], scalar=gs, in1=rs,
                        op0=ALU.mult, op1=ALU.mult)),
                        reads=[("H", hd, tb), rtok, "small"], writes=[("H", hd, tb)])
            proj(wview(wo_d[jl]), 0, H, "H", evac_to_F)
            for tb in range(2):
                postnorm(gidx(l, 3), tb)

        def load_x(c):
            for tt in range(8):
                P.add("sp", (lambda e, tt=tt: e.dma_start(
                    out=Fr[:, 2 * tt:2 * tt + 2, :].rearrange("p a b -> p (a b)"),
                    in_=x_d[c * TCH + tt * 128:c * TCH + (tt + 1) * 128, :])),
                    writes=[("F", 2 * tt), ("F", 2 * tt + 1)], lane=("x", tt % 4))
            for j in range(8):
                for g in range(2):
                    bank = ctr["u"] % 4
                    ctr["u"] += 1
                    for ti in range(4):
                        tt = g * 4 + ti
                        xs = Fr[:, 2 * tt:2 * tt + 2, :].rearrange("p a b -> p (a b)")
                        P.add("pe", (lambda e, bank=bank, ti=ti, xs=xs, j=j: e.transpose(
                            PS[:, bank, ti * 128:(ti + 1) * 128], xs[:, j * 128:(j + 1) * 128], ident)),
                            reads=[("F", 2 * tt), ("F", 2 * tt + 1), "cst"], writes=pst(bank))
                    if (j + g) % 2 == 0:
                        P.add("act", (lambda e, bank=bank, j=j, g=g: e.copy(out=X[:, j, tbs(g)], in_=PS[:, bank, :])),
                              reads=pst(bank), writes=[("X", j, g)])
                    else:
                        P.add("dve", (lambda e, bank=bank, j=j, g=g: e.tensor_copy(out=X[:, j, tbs(g)], in_=PS[:, bank, :])),
                              reads=pst(bank), writes=[("X", j, g)])

        def store_x(c):
            for tt in range(8):
                for g in range(2):
                    bank = ctr["u"] % 4
                    ctr["u"] += 1
                    for ji in range(4):
                        j = g * 4 + ji
                        P.add("pe", (lambda e, bank=bank, ji=ji, j=j, tt=tt: e.transpose(
                            PS[:, bank, ji * 128:(ji + 1) * 128], X[:, j, tt * 128:(tt + 1) * 128], ident)),
                            reads=[("X", j, tt // 4), "cst"], writes=pst(bank))
                    blk = 2 * tt + g
                    if g == 0:
                        P.add("act", (lambda e, bank=bank, blk=blk: e.copy(out=Fr[:, blk, :], in_=PS[:, bank, :])),
                              reads=pst(bank), writes=[("F", blk)])
                    else:
                        P.add("dve", (lambda e, bank=bank, blk=blk: e.tensor_copy(out=Fr[:, blk, :], in_=PS[:, bank, :])),
                              reads=pst(bank), writes=[("F", blk)])
                P.add("sp", (lambda e, tt=tt: e.dma_start(
                    out=out_d[c * TCH + tt * 128:c * TCH + (tt + 1) * 128, :],
                    in_=Fr[:, 2 * tt:2 * tt + 2, :].rearrange("p a b -> p (a b)"))),
                    reads=[("F", 2 * tt), ("F", 2 * tt + 1)], writes=[("OUT", c, tt)], lane=("o", tt % 4))

        for c in range(NCH):
            load_x(c)
            nsub = 0

            def go():
                return nsub < nstop

            for l in range(4):
                if go():
                    ffn(l, 0)
                nsub += 1
                if go():
                    if l < 2:
                        conv_mixer(l, c)
                    else:
                        attn(l, c)
                nsub += 1
                if go():
                    ffn(l, 1)
                nsub += 1
                if l == 1 and go():
                    kv_proj(c)
            store_x(c)
        P.add("sp", None, reads=[("OUT", c, tt) for c in range(NCH) for tt in range(8)])
        P.emit(nc, st)
    nc._prog_stats = P.stats
    return nc


_CACHE = {}


def _consts():
    c = np.zeros((128, 256), np.float32)
    c[:, 0:128] = np.eye(128, dtype=np.float32)
    c[:, 128:256] = np.triu(np.ones((128, 128), np.float32))
    return c


def kernel(**inputs):
    nstop = int(inputs.pop("_nstop", 999))
    ncores = int(inputs.pop("_ncores", NB))
    if nstop not in _CACHE:
        _CACHE[nstop] = build_nc(nstop)
    nc = _CACHE[nstop]
    names = ["g_norm", "w_ffn_gate", "w_ffn_up", "w_ffn_down", "w_conv_in", "w_conv", "w_conv_out", "g_kv",
             "w_kv", "w_q", "lambda_q1", "lambda_k1", "lambda_q2", "lambda_k2", "g_subln", "w_o"]
    shared = {k: np.asarray(inputs[k], dtype=np.float32) for k in names}
    if DBG["lite"]:
        for k in ("w_ffn_gate", "w_ffn_up", "w_ffn_down"):
            shared[k] = shared[k][:1, :1]
        for k in ("w_conv_in", "w_conv_out", "w_q", "w_o"):
            shared[k] = shared[k][:1]
    shared = {k: np.ascontiguousarray(v) for k, v in shared.items()}
    shared["consts"] = _consts()
    x = np.asarray(inputs["x"], dtype=np.float32)
    in_maps = []
    for b in range(ncores):
        m = dict(shared)
        m["x"] = np.ascontiguousarray(x[b])
        in_maps.append(m)
    res = run_bass_kernel_spmd(nc, in_maps, core_ids=list(range(ncores)))
    out = np.stack([np.asarray(res.results[b]["out"], dtype=np.float32) for b in range(ncores)], axis=0)
    return out
```

```python
import math
from contextlib import ExitStack

import numpy as np
import concourse.bass as bass
import concourse.mybir as mybir
from concourse.bass_utils import run_bass_kernel_spmd

F32 = mybir.dt.float32
BF16 = mybir.dt.bfloat16
AF = mybir.ActivationFunctionType
ALU = mybir.AluOpType
AX = mybir.AxisListType

D = 1024
SEQ = 2048
NB = 8
DFF = 2816
EPS = 1e-6
TCH = 1024
NCH = SEQ // TCH
SAME_ENG_SYNC = True


class _Op:
    __slots__ = ("eng", "fn", "dma", "lane", "deps", "idx", "ms", "cnt", "waits")


class Prog:
    def __init__(self):
        self.ops = []
        self.last_w = {}
        self.readers = {}
        self.lane_last = {}

    def add(self, eng, fn, reads=(), writes=(), lane=None):
        op = _Op()
        op.eng = eng
        op.fn = fn
        op.dma = lane is not None
        op.lane = lane
        op.idx = len(self.ops)
        op.ms = False
        op.cnt = 0
        deps = set()
        for t in reads:
            w = self.last_w.get(t)
            if w is not None:
                deps.add(w)
        for t in writes:
            w = self.last_w.get(t)
            if w is not None:
                deps.add(w)
            rs = self.readers.get(t)
            if rs:
                deps.update(rs)
        if lane is not None:
            p = self.lane_last.get(lane)
            if p is not None:
                deps.add(p)
            self.lane_last[lane] = op.idx
        for t in reads:
            self.readers.setdefault(t, []).append(op.idx)
        for t in writes:
            self.last_w[t] = op.idx
            self.readers[t] = []
        deps.discard(op.idx)
        op.deps = deps
        self.ops.append(op)
        return op

    @staticmethod
    def _needs_sync(p, op):
        if p.dma:
            return True
        if p.eng != op.eng:
            return True
        if p.eng == "pe":
            return False
        return SAME_ENG_SYNC

    def emit(self, nc, stack):
        ops = self.ops
        for op in ops:
            best = {}
            for d in op.deps:
                p = ops[d]
                if (not p.dma) and self._needs_sync(p, op):
                    if best.get(p.eng, -1) < d:
                        best[p.eng] = d
            for d in best.values():
                ops[d].ms = True
        eng_cnt = {}
        lane_cnt = {}
        for op in ops:
            if op.dma:
                lane_cnt[op.lane] = lane_cnt.get(op.lane, 0) + 1
                op.cnt = 16 * lane_cnt[op.lane]
            elif op.ms:
                eng_cnt[op.eng] = eng_cnt.get(op.eng, 0) + 1
                op.cnt = eng_cnt[op.eng]
        waited = {}
        for op in ops:
            need = {}
            for d in op.deps:
                p = ops[d]
                if not self._needs_sync(p, op):
                    continue
                key = ("lane", p.lane) if p.dma else ("eng", p.eng)
                if need.get(key, 0) < p.cnt:
                    need[key] = p.cnt
                assert p.dma or p.cnt > 0 or any(
                    (not ops[d2].dma) and ops[d2].eng == p.eng and d2 > d for d2 in op.deps)
            wd = waited.setdefault(op.eng, {})
            op.waits = []
            for k, v in need.items():
                if wd.get(k, 0) < v:
                    op.waits.append((k, v))
                    wd[k] = v
        sems = {}
        for e in sorted(eng_cnt):
            sems[("eng", e)] = stack.enter_context(nc.semaphore("s_" + e))
        for i, l in enumerate(lane_cnt):
            sems[("lane", l)] = stack.enter_context(nc.semaphore("l_%d" % i))
        per = {}
        for op in ops:
            per.setdefault(op.eng, []).append(op)
        self.stats = {e: len(v) for e, v in per.items()}
        self.stats["sems"] = len(sems)

        def mk(name):
            lst = per.get(name, [])

            def body(e):
                for op in lst:
                    for k, v in op.waits:
                        e.wait_ge(sems[k], v)
                    if op.fn is None:
                        continue
                    ins = op.fn(e)
                    if op.dma:
                        ins.then_inc(sems[("lane", op.lane)], 16)
                    elif op.ms:
                        ins.then_inc(sems[("eng", op.eng)], 1)

            return body

        with nc.Block() as block:
            block.tensor(mk("pe"))
            block.scalar(mk("act"))
            block.vector(mk("dve"))
            block.gpsimd(mk("pool"))
            block.sync(mk("sp"))


def lambda_init(layer):
    return 0.8 - 0.6 * math.exp(-0.3 * layer)


DBG = {"ffn_stage": 99, "lite": False}


def build_nc(nstop=999):
    nc = bass.Bass("TRN2", target_bir_lowering=False)
    nl4 = 1 if DBG["lite"] else 4
    nl2 = 1 if DBG["lite"] else 2

    def din(name, shape):
        return nc.dram_tensor(name, list(shape), F32, kind="ExternalInput").ap()

    x_d = din("x", (SEQ, D))
    gnorm_d = din("g_norm", (4, 6, D))
    wg_d = din("w_ffn_gate", (nl4, nl2, D, DFF))
    wu_d = din("w_ffn_up", (nl4, nl2, D, DFF))
    wd_d = din("w_ffn_down", (nl4, nl2, DFF, D))
    wci_d = din("w_conv_in", (nl2, D, 3 * D))
    wc_d = din("w_conv", (2, 3, D))
    wco_d = din("w_conv_out", (nl2, D, D))
    gkv_d = din("g_kv", (D,))
    wkv_d = din("w_kv", (D, 2 * D))
    wq_d = din("w_q", (nl2, D, D))
    lq1_d = din("lambda_q1", (2, 64))
    lk1_d = din("lambda_k1", (2, 64))
    lq2_d = din("lambda_q2", (2, 64))
    lk2_d = din("lambda_k2", (2, 64))
    gsub_d = din("g_subln", (2, 128))
    wo_d = din("w_o", (nl2, D, D))
    cst_d = din("consts", (128, 256))
    out_d = nc.dram_tensor("out", [SEQ, D], F32, kind="ExternalOutput").ap()

    P = Prog()
    with ExitStack() as st:
        def sb(name, shape, dt):
            return st.enter_context(nc.sbuf_tensor(name, list(shape), dt))

        X = sb("X", (128, 8, TCH), F32)
        H = sb("H", (128, 8, TCH), BF16)
        Fr = sb("Fr", (128, 16, 512), F32)
        A = sb("A", (128, 11, TCH), BF16)
        KT = sb("KT", (128, 8, SEQ), BF16)
        V = sb("V", (128, 16, D), BF16)
        slots = [sb("slot%d" % i, (128, 6144), BF16) for i in range(2)]
        cst = sb("cst", (128, 256), F32)
        tri = sb("tri", (128, 128), BF16)
        ones = sb("ones", (128, 128), BF16)
        cols = sb("cols", (128, 256), F32)
        eps_t = sb("eps", (128, 1), F32)
        sq = [sb("sq%d" % i, (128, 512), BF16) for i in range(3)]
        sd = sb("sd", (128, 512), F32)
        rstd = [sb("rstd%d" % i, (128, 512), F32) for i in range(2)]
        sg = [sb("sg%d" % i, (128, 512), F32) for i in range(2)]
        halo = sb("halo", (128, 2, 8, 2), F32)
        small = sb("small", (128, 16), F32)
        PS = st.enter_context(nc.psum_tensor("PS", [128, 8, 512], F32))
        ident = cst[:, 0:128]

        def pst(b):
            return [("ps", b, 0), ("ps", b, 1)]

        def tbs(tb):
            return slice(tb * 512, (tb + 1) * 512)

        def mm(out, lhsT, rhs, start, stop, reads, writes):
            P.add("pe", lambda e: e.matmul(out, lhsT, rhs, start=start, stop=stop), reads, writes)

        ctr = {"slot": 0, "stat": 0, "sq": 0, "rstd": 0, "u": 0}

        def next_slot():
            s = ctr["slot"] % 2
            ctr["slot"] += 1
            return s

        def stoks(s, a, b):
            return [("slot", s, i) for i in range(a, b)]

        P.add("dve", lambda e: e.memset(ones[:], 1.0), writes=["ones"])
        P.add("dve", lambda e: e.memset(eps_t[:], EPS), writes=["eps"])
        P.add("dve", lambda e: e.memset(small[:, 12:13], -0.5), writes=["mhalf"])
        P.add("sp", lambda e: e.dma_start(out=cst[:], in_=cst_d), writes=["cst"], lane="cst")
        P.add("dve", lambda e: e.tensor_copy(out=tri[:], in_=cst[:, 128:256]), reads=["cst"], writes=["tri"])
        gview = gnorm_d.rearrange("l n (j p) -> (l n j) p", p=128)
        P.add("sp", lambda e: e.dma_start(out=Fr[0:128, 0, 0:128], in_=gview[0:128, :]),
              writes=[("F", 0)], lane="p0")
        P.add("sp", lambda e: e.dma_start(out=Fr[0:64, 1, 0:128], in_=gview[128:192, :]),
              writes=[("F", 1)], lane="p1")
        P.add("sp", lambda e: e.dma_start(out=Fr[64:72, 1, 0:128], in_=gkv_d.rearrange("(j p) -> j p", p=128)),
              writes=[("F", 1)], lane="p2")
        P.add("sp", lambda e: e.dma_start(out=Fr[72:120, 1, 0:128],
                                          in_=wc_d.rearrange("l w (j p) -> (l w j) p", p=128)),
              writes=[("F", 1)], lane="p3")
        P.add("sp", lambda e: e.dma_start(out=Fr[120:122, 1, 0:128], in_=gsub_d),
              writes=[("F", 1)], lane="p4")
        P.add("pe", lambda e: e.transpose(PS[:, 0, 0:128], Fr[0:128, 0, 0:128], ident),
              reads=[("F", 0), "cst"], writes=pst(0))
        P.add("pe", lambda e: e.transpose(PS[:, 0, 128:250], Fr[0:122, 1, 0:128], cst[0:122, 0:122]),
              reads=[("F", 1), "cst"], writes=pst(0))
        P.add("dve", lambda e: e.tensor_copy(out=cols[:, 0:250], in_=PS[:, 0, 0:250]), reads=pst(0), writes=["cols"])
        gv4 = cols[:, 0:192].rearrange("p (l n j) -> p l n j", l=4, n=6)
        for n in (1, 5):
            P.add("dve", (lambda e, n=n: e.tensor_scalar(out=gv4[:, :, n, :], in0=gv4[:, :, n, :], scalar1=0.5,
                                                         scalar2=None, op0=ALU.mult)),
                  reads=["cols"], writes=["cols"])
        lv = Fr[:, 2, :].rearrange("p (a b) -> p a b", a=4)
        for i, ld in enumerate((lq1_d, lk1_d, lq2_d, lk2_d)):
            P.add("sp", (lambda e, i=i, ld=ld: e.dma_start(
                out=lv[:, i, :], in_=ld.rearrange("a b -> (a b)").partition_broadcast(128))),
                writes=[("F", 2)], lane="l%d" % i)
        pr = Fr[:, 3, 0:256].rearrange("p (a b) -> p a b", a=2)
        P.add("dve", lambda e: e.tensor_tensor(out=pr[:, 0, :], in0=lv[:, 0, :], in1=lv[:, 1, :], op=ALU.mult),
              reads=[("F", 2)], writes=[("F", 3)])
        P.add("dve", lambda e: e.tensor_tensor(out=pr[:, 1, :], in0=lv[:, 2, :], in1=lv[:, 3, :], op=ALU.mult),
              reads=[("F", 2)], writes=[("F", 3)])
        P.add("dve", lambda e: e.reduce_sum(out=small[:, 0:4],
                                            in_=Fr[:, 3, 0:256].rearrange("p (a b) -> p a b", a=4), axis=AX.X),
              reads=[("F", 3)], writes=["small"])
        P.add("act", lambda e: e.activation(out=small[:, 4:8], in_=small[:, 0:4], func=AF.Exp),
              reads=["small"], writes=["small"])
        P.add("dve", lambda e: e.tensor_tensor(out=small[:, 8:10], in0=small[:, 6:8], in1=small[:, 4:6],
                                               op=ALU.subtract), reads=["small"], writes=["small"])
        for jl in range(2):
            li = lambda_init(jl + 2)
            P.add("dve", (lambda e, jl=jl, li=li: e.tensor_scalar(out=small[:, 8 + jl:9 + jl], in0=small[:, 8 + jl:9 + jl],
                                                                  scalar1=-li, scalar2=None, op0=ALU.add)),
                  reads=["small"], writes=["small"])
            P.add("dve", (lambda e, jl=jl, li=li: e.tensor_scalar(out=small[:, 10 + jl:11 + jl],
                                                                  in0=cols[:, 248 + jl:249 + jl],
                                                                  scalar1=1.0 - li, scalar2=None, op0=ALU.mult)),
                  reads=["small", "cols"], writes=["small"])

        def gidx(l, n):
            return (l * 6 + n) * 8

        def rstd_of(src_fn, src_tok, ncols, invd, nj):
            b = 6 + ctr["stat"] % 2
            ctr["stat"] += 1
            ssb = PS[:, b, 0:ncols]
            for j in range(nj):
                r = ctr["sq"] % 3
                ctr["sq"] += 1
                P.add("act", (lambda e, j=j, r=r: e.activation(out=sq[r][:, 0:ncols], in_=src_fn(j), func=AF.Square)),
                      reads=[src_tok(j)], writes=[("sq", r)])
                mm(ssb, ones[:], sq[r][:, 0:ncols], j == 0, j == nj - 1, [("sq", r), "ones"], pst(b))
            P.add("act", lambda e: e.activation(out=sd[:, 0:ncols], in_=ssb, func=AF.Sqrt, bias=eps_t[:, 0:1],
                                                scale=invd), reads=pst(b) + ["eps"], writes=["sd"])
            rb = ctr["rstd"] % 2
            ctr["rstd"] += 1
            P.add("dve", lambda e: e.reciprocal(out=rstd[rb][:, 0:ncols], in_=sd[:, 0:ncols]),
                  reads=["sd"], writes=[("rstd", rb)])
            return rstd[rb][:, 0:ncols], ("rstd", rb)

        def prenorm(gbase, tb):
            rs, rtok = rstd_of(lambda j: X[:, j, tbs(tb)], lambda j: ("X", j, tb), 512, 1.0 / D, 8)
            for j in range(8):
                P.add("dve", (lambda e, j=j: e.scalar_tensor_tensor(
                    out=H[:, j, tbs(tb)], in0=X[:, j, tbs(tb)], scalar=cols[:, gbase + j:gbase + j + 1], in1=rs,
                    op0=ALU.mult, op1=ALU.mult)),
                    reads=[("X", j, tb), rtok, "cols"], writes=[("H", j, tb)])

        def post_stats(tb):
            return rstd_of(lambda j: Fr[:, 2 * j + tb, :], lambda j: ("F", 2 * j + tb), 512, 1.0 / D, 8)

        def post_apply(gbase, tb, rs, rtok):
            for j in range(8):
                blk = 2 * j + tb
                P.add("dve", (lambda e, j=j, blk=blk: e.scalar_tensor_tensor(
                    out=Fr[:, blk, :], in0=Fr[:, blk, :], scalar=cols[:, gbase + j:gbase + j + 1], in1=rs,
                    op0=ALU.mult, op1=ALU.mult)),
                    reads=[("F", blk), rtok, "cols"], writes=[("F", blk)])
                P.add("dve", (lambda e, j=j, blk=blk: e.tensor_tensor(
                    out=X[:, j, tbs(tb)], in0=X[:, j, tbs(tb)], in1=Fr[:, blk, :], op=ALU.add)),
                    reads=[("F", blk), ("X", j, tb)], writes=[("X", j, tb)])

        def wview(w2d):
            return w2d.rearrange("(k p) f -> p k f", p=128)

        def proj(Wv, c0, src, srctok, evac, hook=None):
            pcs = []
            for mh in range(2):
                s = next_slot()
                wv = slots[s][:, 0:4096].rearrange("p (k f) -> p k f", k=8)
                P.add("pool", (lambda e, wv=wv, mh=mh: e.dma_start(out=wv, in_=Wv[:, :, c0 + mh * 512:c0 + (mh + 1) * 512])),
                      writes=stoks(s, 0, 4), lane=("w", s, 0))
                pcs.append((s, wv))
            for tb in range(2):
                for m in range(8):
                    s, wv = pcs[m // 4]
                    mi = m % 4
                    bank = ctr["u"] % 2
                    ctr["u"] += 1
                    Pp = PS[:, bank, :]
                    for k in range(8):
                        mm(Pp, wv[:, k, mi * 128:(mi + 1) * 128], src[:, k, tbs(tb)], k == 0, k == 7,
                           stoks(s, 0, 4) + [(srctok, k, tb)], pst(bank))
                    evac(m, tb, Pp, bank)
                    if hook is not None and tb == 1 and m == 4:
                        hook()

        def evac_to_F(m, tb, Pp, bank):
            blk = 2 * m + tb
            P.add("act", lambda e: e.copy(out=Fr[:, blk, :], in_=Pp), reads=pst(bank), writes=[("F", blk)])

        def ffn(l, w, hook=None):
            Wg = wview(wg_d[l, w])
            Wu = wview(wu_d[l, w])
            Wd = wd_d[l, w].rearrange("(f p) m -> p f m", p=128)
            for hf in range(2):
                for (f0, n) in ((0, 3), (3, 3), (6, 3), (9, 2)):
                    s = next_slot()
                    gv = slots[s][:, 0:8 * n * 128].rearrange("p (k f) -> p k f", k=8)
                    uv = slots[s][:, 3072:3072 + 8 * n * 128].rearrange("p (k f) -> p k f", k=8)
                    fa = hf * 11 + f0
                    P.add("pool", (lambda e, gv=gv, fa=fa, n=n: e.dma_start(out=gv, in_=Wg[:, :, fa * 128:(fa + n) * 128])),
                          writes=stoks(s, 0, 3), lane=("w", s, 0))
                    P.add("pool", (lambda e, uv=uv, fa=fa, n=n: e.dma_start(out=uv, in_=Wu[:, :, fa * 128:(fa + n) * 128])),
                          writes=stoks(s, 3, 6), lane=("w", s, 3))
                    order = ([(fi, tb) for tb in range(2) for fi in range(n)] if (hf == 0 and f0 == 0)
                             else [(fi, tb) for fi in range(n) for tb in range(2)])
                    for (fi, tb) in order:
                        fl = f0 + fi
                        if True:
                            par = ctr["u"] % 2
                            ctr["u"] += 1
                            G = PS[:, 2 * par, :]
                            U = PS[:, 2 * par + 1, :]
                            for k in range(8):
                                mm(G, gv[:, k, fi * 128:(fi + 1) * 128], H[:, k, tbs(tb)], k == 0, k == 7,
                                   stoks(s, 0, 3) + [("H", k, tb)], pst(2 * par))
                            for k in range(8):
                                mm(U, uv[:, k, fi * 128:(fi + 1) * 128], H[:, k, tbs(tb)], k == 0, k == 7,
                                   stoks(s, 3, 6) + [("H", k, tb)], pst(2 * par + 1))
                            P.add("act", (lambda e, G=G, par=par: e.activation(out=sg[par][:], in_=G, func=AF.Silu)),
                                  reads=pst(2 * par), writes=[("sg", par)])
                            P.add("dve", (lambda e, U=U, par=par, fl=fl, tb=tb: e.tensor_tensor(
                                out=A[:, fl, tbs(tb)], in0=U, in1=sg[par][:], op=ALU.mult)),
                                reads=pst(2 * par + 1) + [("sg", par)], writes=[("A", fl, tb)])
                pieces = []
                for m0 in (0, 4):
                    s = next_slot()
                    dv = slots[s][:, 0:5632].rearrange("p (f m) -> p f m", f=11)
                    P.add("pool", (lambda e, dv=dv, hf=hf, m0=m0: e.dma_start(
                        out=dv, in_=Wd[:, hf * 11:(hf + 1) * 11, m0 * 128:(m0 + 4) * 128])),
                        writes=stoks(s, 0, 6), lane=("w", s, 0))
                    pieces.append((s, dv))
                if hf == 0:
                    dorder = [(m, tb) for m in range(8) for tb in range(2)]
                else:
                    dorder = [(m, tb) for tb in range(2) for m in range(8)]
                for (m, tb) in dorder:
                    s, dv = pieces[m // 4]
                    mi = m % 4
                    par = ctr["u"] % 2
                    ctr["u"] += 1
                    Dp = PS[:, 4 + par, :]
                    for fl in range(11):
                        mm(Dp, dv[:, fl, mi * 128:(mi + 1) * 128], A[:, fl, tbs(tb)], fl == 0, fl == 10,
                           stoks(s, 0, 6) + [("A", fl, tb)], pst(4 + par))
                    blk = 2 * m + tb
                    if hf == 0:
                        P.add("act", (lambda e, Dp=Dp, blk=blk: e.copy(out=Fr[:, blk, :], in_=Dp)),
                              reads=pst(4 + par), writes=[("F", blk)])
                    else:
                        P.add("dve", (lambda e, Dp=Dp, blk=blk: e.tensor_tensor(
                            out=Fr[:, blk, :], in0=Dp, in1=Fr[:, blk, :], op=ALU.add)),
                            reads=pst(4 + par) + [("F", blk)], writes=[("F", blk)])
                    if hook is not None and hf == 1 and tb == 1 and m == 4:
                        hook()

        def conv_mixer(l, c, hook=None):
            Win = wview(wci_d[l])
            for jp in range(4):
                s = next_slot()
                parts = [slots[s][:, pt * 2048:(pt + 1) * 2048].rearrange("p (k f) -> p k f", k=8) for pt in range(3)]
                for pt in range(3):
                    P.add("pool", (lambda e, pt=pt, jp=jp, parts=parts: e.dma_start(
                        out=parts[pt], in_=Win[:, :, pt * 1024 + jp * 256:pt * 1024 + (jp + 1) * 256])),
                        writes=stoks(s, 2 * pt, 2 * pt + 2), lane=("w", s, 2 * pt))
                cus = []
                for jj in range(2):
                    j = 2 * jp + jj
                    cub = j % 2
                    cu = Fr[:, cub * 3:cub * 3 + 3, :].rearrange("p a b -> p (a b)")
                    cutoks = [("F", cub * 3 + i) for i in range(3)]
                    if c == 0:
                        P.add("dve", (lambda e, cu=cu: e.memset(cu[:, 0:2], 0.0)), writes=cutoks)
                    else:
                        P.add("dve", (lambda e, cu=cu, j=j: e.tensor_copy(out=cu[:, 0:2], in_=halo[:, l, j, :])),
                              reads=[("halo", l, j)], writes=cutoks)
                    cus.append((cu, cutoks))
                for tb in range(2):
                    for jj in range(2):
                        j = 2 * jp + jj
                        cu, cutoks = cus[jj]
                        w0 = 200 + (l * 3 + 0) * 8 + j
                        w1 = 200 + (l * 3 + 1) * 8 + j
                        w2 = 200 + (l * 3 + 2) * 8 + j
                        par = ctr["u"] % 2
                        ctr["u"] += 1
                        Pb, Pc, Pu = (PS[:, 3 * par + i, :] for i in range(3))
                        for pt in range(3):
                            for k in range(8):
                                mm(PS[:, 3 * par + pt, :], parts[pt][:, k, jj * 128:(jj + 1) * 128], H[:, k, tbs(tb)],
                                   k == 0, k == 7, stoks(s, 2 * pt, 2 * pt + 2) + [("H", k, tb)], pst(3 * par + pt))
                        ucp = Fr[:, 6 + par, :]
                        z = Fr[:, 8 + par, :]
                        o2 = 2 + tb * 512
                        P.add("act", (lambda e, ucp=ucp, Pu=Pu: e.copy(out=ucp, in_=Pu)),
                              reads=pst(3 * par + 2), writes=[("F", 6 + par)])
                        P.add("dve", (lambda e, cu=cu, o2=o2, Pc=Pc, ucp=ucp: e.tensor_tensor(
                            out=cu[:, o2:o2 + 512], in0=Pc, in1=ucp, op=ALU.mult)),
                            reads=pst(3 * par + 1) + [("F", 6 + par)], writes=cutoks)
                        P.add("act", (lambda e, z=z, cu=cu, o2=o2, w2=w2: e.mul(out=z, in_=cu[:, o2:o2 + 512],
                                                                               mul=cols[:, w2:w2 + 1])),
                              reads=cutoks + ["cols"], writes=[("F", 8 + par)])
                        P.add("dve", (lambda e, z=z, cu=cu, o2=o2, w1=w1: e.scalar_tensor_tensor(
                            out=z, in0=cu[:, o2 - 1:o2 + 511], scalar=cols[:, w1:w1 + 1], in1=z,
                            op0=ALU.mult, op1=ALU.add)), reads=cutoks + ["cols", ("F", 8 + par)],
                            writes=[("F", 8 + par)])
                        P.add("dve", (lambda e, z=z, cu=cu, o2=o2, w0=w0: e.scalar_tensor_tensor(
                            out=z, in0=cu[:, o2 - 2:o2 + 510], scalar=cols[:, w0:w0 + 1], in1=z,
                            op0=ALU.mult, op1=ALU.add)), reads=cutoks + ["cols", ("F", 8 + par)],
                            writes=[("F", 8 + par)])
                        P.add("dve", (lambda e, z=z, Pb=Pb, j=j, tb=tb: e.tensor_tensor(
                            out=A[:, j, tbs(tb)], in0=Pb, in1=z, op=ALU.mult)),
                            reads=pst(3 * par) + [("F", 8 + par)], writes=[("A", j, tb)])
                if c == 0:
                    for jj in range(2):
                        j = 2 * jp + jj
                        cu, cutoks = cus[jj]
                        P.add("dve", (lambda e, cu=cu, j=j: e.tensor_copy(out=halo[:, l, j, :], in_=cu[:, 1024:1026])),
                              reads=cutoks, writes=[("halo", l, j)])
            proj(wview(wco_d[l]), 0, A, "A", evac_to_F, hook)

        def kv_proj(c):
            Wkv = wview(wkv_d)

            def evac_k(m, tb, Pp, bank):
                P.add("act", lambda e: e.copy(out=KT[:, m, c * TCH + tb * 512:c * TCH + (tb + 1) * 512], in_=Pp),
                      reads=pst(bank), writes=[("K", m, c)])

            proj(Wkv, 0, H, "H", evac_k)
            for eh in range(2):
                s = next_slot()
                wv = slots[s][:, 0:4096].rearrange("p (k f) -> p k f", k=8)
                P.add("pool", (lambda e, wv=wv, eh=eh: e.dma_start(out=wv, in_=Wkv[:, :, D + eh * 512:D + (eh + 1) * 512])),
                      writes=stoks(s, 0, 4), lane=("w", s, 0))
                for tt in range(8):
                    bank = ctr["u"] % 2
                    ctr["u"] += 1
                    Pp = PS[:, bank, :]
                    for k in range(8):
                        mm(Pp, H[:, k, tt * 128:(tt + 1) * 128], wv[:, k, :], k == 0, k == 7,
                           stoks(s, 0, 4) + [("H", k, tt // 4)], pst(bank))
                    kt = c * 8 + tt
                    if tt % 2 == 0:
                        P.add("act", (lambda e, kt=kt, eh=eh, Pp=Pp: e.copy(out=V[:, kt, eh * 512:(eh + 1) * 512], in_=Pp)),
                              reads=pst(bank), writes=[("V", kt)])
                    else:
                        P.add("dve", (lambda e, kt=kt, eh=eh, Pp=Pp: e.tensor_copy(out=V[:, kt, eh * 512:(eh + 1) * 512], in_=Pp)),
                              reads=pst(bank), writes=[("V", kt)])

        def attn(l, c, hook=None):
            jl = l - 2

            def evac_q(m, tb, Pp, bank):
                P.add("act", lambda e: e.mul(out=A[:, m, tbs(tb)], in_=Pp, mul=0.125),
                      reads=pst(bank), writes=[("A", m, tb)])

            proj(wview(wq_d[jl]), 0, H, "H", evac_q)

            steps = []
            for hd in range(8):
                for u in range(4 * c, 4 * c + 4):
                    for kt in range(2 * u + 2):
                        steps.append((hd, u, kt))
            Eb = [Fr[:, r, :].bitcast(BF16)[:, 0:512].rearrange("p (c q) -> p c q", c=2) for r in range(3)]
            neglam = small[:, 8 + jl:9 + jl]
            gs = small[:, 10 + jl:11 + jl]
            tri_b = tri[:].unsqueeze(1).to_broadcast([128, 2, 128])

            def geom(i):
                hd, u, kt = steps[i]
                d = kt - 2 * u
                q0 = 128 if d == 1 else 0
                ql = (u - 4 * c) * 256
                return hd, u, kt, d, q0, ql

            it_of = []
            _it = 0
            for (hd_, u_, kt_) in steps:
                it_of.append(_it)
                if kt_ == 2 * u_ + 1:
                    _it += 1
            qz = [Fr[:, 9 + p_, :].bitcast(BF16)[:, 0:512].rearrange("p (c q) -> p c q", c=2) for p_ in range(2)]
            for p_ in range(2):
                P.add("dve", (lambda e, p_=p_: e.memset(Fr[:, 9 + p_, :].bitcast(BF16)[:, 0:512], 0.0)),
                      writes=[("F", 9 + p_)])

            def emit_S(i):
                hd, u, kt, d, q0, ql = geom(i)
                par = i % 2
                ipar = it_of[i] % 2
                if kt == 0:
                    P.add("dve", lambda e: e.tensor_copy(out=qz[ipar][0:64, 0, :], in_=A[0:64, hd, ql:ql + 256]),
                          reads=[("A", hd, ql // 512)], writes=[("F", 9 + ipar)])
                    P.add("dve", lambda e: e.tensor_copy(out=qz[ipar][64:128, 1, :], in_=A[64:128, hd, ql:ql + 256]),
                          reads=[("A", hd, ql // 512)], writes=[("F", 9 + ipar)])
                for cm in range(2):
                    mm(PS[:, par, cm * 256 + q0:(cm + 1) * 256],
                       KT[:, hd, kt * 128:(kt + 1) * 128],
                       qz[ipar][:, cm, q0:256], True, True,
                       [("K", hd, kt // 8), ("F", 9 + ipar)], pst(par))

            def emit_exp(i):
                hd, u, kt, d, q0, ql = geom(i)
                par = i % 2
                r = i % 3
                P.add("act", lambda e: e.activation(
                    out=Eb[r][:, :, q0:256],
                    in_=PS[:, par, :].rearrange("p (c q) -> p c q", c=2)[:, :, q0:256], func=AF.Exp),
                    reads=pst(par), writes=[("F", r)])
                if d >= 0:
                    P.add("dve", lambda e: e.tensor_tensor(out=Eb[r][:, :, q0:q0 + 128], in0=Eb[r][:, :, q0:q0 + 128],
                                                           in1=tri_b, op=ALU.mult),
                          reads=[("F", r), "tri"], writes=[("F", r)])

            pending = []

            def emit_PV(i, it):
                hd, u, kt, d, q0, ql = geom(i)
                r = i % 3
                ipar = it % 2
                bA = 2 + ipar
                bS = 4 + ipar
                Ab = PS[:, bA, :].rearrange("p (c q) -> p c q", c=2)
                Sb = PS[:, bS, :].rearrange("p (c q) -> p c q", c=2)
                first = kt == 0
                last = kt == 2 * u + 1
                if q0 == 0:
                    E2 = Fr[:, r, :].bitcast(BF16)[:, 0:512]
                    mm(PS[:, bA, :], V[:, kt, hd * 128:(hd + 1) * 128], E2, first, last,
                       [("V", kt), ("F", r)], pst(bA))
                    mm(PS[:, bS, :], ones[:], E2, first, last, [("F", r), "ones"], pst(bS))
                else:
                    for cm in range(2):
                        mm(Ab[:, cm, q0:256], V[:, kt, hd * 128:(hd + 1) * 128], Eb[r][:, cm, q0:256], first,
                           last and cm == 1, [("V", kt), ("F", r)], pst(bA))
                    for cm in range(2):
                        mm(Sb[:, cm, q0:256], ones[:], Eb[r][:, cm, q0:256], first, last and cm == 1,
                           [("F", r), "ones"], pst(bS))
                if last:
                    b0 = 3 + 3 * ipar
                    rr = Fr[:, b0, :]
                    tt_ = Fr[:, b0 + 1, :]
                    ob = Fr[:, b0 + 2, 0:256]
                    P.add("dve", lambda e: e.reciprocal(out=rr, in_=PS[:, bS, :]),
                          reads=pst(bS), writes=[("F", b0)])
                    P.add("dve", lambda e: e.tensor_tensor(out=tt_, in0=PS[:, bA, :], in1=rr, op=ALU.mult),
                          reads=pst(bA) + [("F", b0)], writes=[("F", b0 + 1)])
                    P.add("dve", lambda e: e.scalar_tensor_tensor(out=H[:, hd, ql:ql + 256], in0=tt_[:, 256:512],
                                                                  scalar=neglam, in1=tt_[:, 0:256],
                                                                  op0=ALU.mult, op1=ALU.add),
                          reads=[("F", b0 + 1), "small"], writes=[("H", hd, ql // 512)])

            n = len(steps)
            emit_S(0)
            it = 0
            for i in range(n):
                if i + 1 < n:
                    emit_S(i + 1)
                emit_exp(i)
                emit_PV(i, it)
                if steps[i][2] == 2 * steps[i][1] + 1:
                    it += 1
                for pnd in list(pending):
                    pnd[0] -= 1
                    if pnd[0] <= 0:
                        pending.remove(pnd)
                        pnd[1]()
            for pnd in pending:
                pnd[1]()
            for tb in range(2):
                for hd in range(8):
                    rs, rtok = rstd_of(lambda j, hd=hd, tb=tb: H[:, hd, tbs(tb)], lambda j, hd=hd, tb=tb: ("H", hd, tb),
                                       512, 1.0 / 128.0, 1)
                    P.add("dve", (lambda e, hd=hd, tb=tb, rs=rs: e.scalar_tensor_tensor(
                        out=H[:, hd, tbs(tb)], in0=H[:, hd, tbs(tb)], scalar=gs, in1=rs,
                        op0=ALU.mult, op1=ALU.mult)),
                        reads=[("H", hd, tb), rtok, "small"], writes=[("H", hd, tb)])
            proj(wview(wo_d[jl]), 0, H, "H", evac_to_F, hook)

        def load_x(c):
            for tt in range(8):
                P.add("sp", (lambda e, tt=tt: e.dma_start(
                    out=Fr[:, 2 * tt:2 * tt + 2, :].rearrange("p a b -> p (a b)"),
                    in_=x_d[c * TCH + tt * 128:c * TCH + (tt + 1) * 128, :])),
                    writes=[("F", 2 * tt), ("F", 2 * tt + 1)], lane=("x", tt % 4))
            for j in range(8):
                for g in range(2):
                    bank = ctr["u"] % 4
                    ctr["u"] += 1
                    for ti in range(4):
                        tt = g * 4 + ti
                        xs = Fr[:, 2 * tt:2 * tt + 2, :].rearrange("p a b -> p (a b)")
                        P.add("pe", (lambda e, bank=bank, ti=ti, xs=xs, j=j: e.transpose(
                            PS[:, bank, ti * 128:(ti + 1) * 128], xs[:, j * 128:(j + 1) * 128], ident)),
                            reads=[("F", 2 * tt), ("F", 2 * tt + 1), "cst"], writes=pst(bank))
                    if (j + g) % 2 == 0:
                        P.add("act", (lambda e, bank=bank, j=j, g=g: e.copy(out=X[:, j, tbs(g)], in_=PS[:, bank, :])),
                              reads=pst(bank), writes=[("X", j, g)])
                    else:
                        P.add("dve", (lambda e, bank=bank, j=j, g=g: e.tensor_copy(out=X[:, j, tbs(g)], in_=PS[:, bank, :])),
                              reads=pst(bank), writes=[("X", j, g)])

        def store_x(c):
            for tt in range(8):
                for g in range(2):
                    bank = ctr["u"] % 4
                    ctr["u"] += 1
                    for ji in range(4):
                        j = g * 4 + ji
                        P.add("pe", (lambda e, bank=bank, ji=ji, j=j, tt=tt: e.transpose(
                            PS[:, bank, ji * 128:(ji + 1) * 128], X[:, j, tt * 128:(tt + 1) * 128], ident)),
                            reads=[("X", j, tt // 4), "cst"], writes=pst(bank))
                    blk = 2 * tt + g
                    if g == 0:
                        P.add("act", (lambda e, bank=bank, blk=blk: e.copy(out=Fr[:, blk, :], in_=PS[:, bank, :])),
                              reads=pst(bank), writes=[("F", blk)])
                    else:
                        P.add("dve", (lambda e, bank=bank, blk=blk: e.tensor_copy(out=Fr[:, blk, :], in_=PS[:, bank, :])),
                              reads=pst(bank), writes=[("F", blk)])
                P.add("sp", (lambda e, tt=tt: e.dma_start(
                    out=out_d[c * TCH + tt * 128:c * TCH + (tt + 1) * 128, :],
                    in_=Fr[:, 2 * tt:2 * tt + 2, :].rearrange("p a b -> p (a b)"))),
                    reads=[("F", 2 * tt), ("F", 2 * tt + 1)], writes=[("OUT", c, tt)], lane=("o", tt % 4))

        for c in range(NCH):
            load_x(c)
            subs = []
            for l in range(4):
                subs.append((lambda hook, l=l: ffn(l, 0, hook), gidx(l, 0), gidx(l, 1)))
                if l < 2:
                    subs.append((lambda hook, l=l, c=c: conv_mixer(l, c, hook), gidx(l, 2), gidx(l, 3)))
                else:
                    subs.append((lambda hook, l=l, c=c: attn(l, c, hook), gidx(l, 2), gidx(l, 3)))
                subs.append((lambda hook, l=l: ffn(l, 1, hook), gidx(l, 4), gidx(l, 5)))
                if l == 1:
                    subs.append((lambda hook, c=c: kv_proj(c), 192, None))
            nrun = nstop if nstop < 6 else (nstop if nstop > 6 else 6)
            if nstop > 6:
                nrun = nstop + 1
            subs = subs[:min(len(subs), nrun)]
            if subs:
                for tb in range(2):
                    prenorm(subs[0][1], tb)
            for i, (fn, gpre, gpost) in enumerate(subs):
                stash = {}

                def hook(stash=stash):
                    stash["s0"] = post_stats(0)

                fn(hook if gpost is not None else None)
                nxt = subs[i + 1][1] if i + 1 < len(subs) else None
                for tb in range(2):
                    if gpost is not None:
                        rs, rtok = stash["s0"] if (tb == 0 and "s0" in stash) else post_stats(tb)
                        post_apply(gpost, tb, rs, rtok)
                    if nxt is not None:
                        prenorm(nxt, tb)
            store_x(c)
        P.add("sp", None, reads=[("OUT", c, tt) for c in range(NCH) for tt in range(8)])
        P.emit(nc, st)
    nc._prog_stats = P.stats
    return nc


_CACHE = {}


def _consts():
    c = np.zeros((128, 256), np.float32)
    c[:, 0:128] = np.eye(128, dtype=np.float32)
    c[:, 128:256] = np.triu(np.ones((128, 128), np.float32))
    return c


def kernel(**inputs):
    nstop = int(inputs.pop("_nstop", 999))
    ncores = int(inputs.pop("_ncores", NB))
    if nstop not in _CACHE:
        _CACHE[nstop] = build_nc(nstop)
    nc = _CACHE[nstop]
    names = ["g_norm", "w_ffn_gate", "w_ffn_up", "w_ffn_down", "w_conv_in", "w_conv", "w_conv_out", "g_kv",
             "w_kv", "w_q", "lambda_q1", "lambda_k1", "lambda_q2", "lambda_k2", "g_subln", "w_o"]
    shared = {k: np.asarray(inputs[k], dtype=np.float32) for k in names}
    if DBG["lite"]:
        for k in ("w_ffn_gate", "w_ffn_up", "w_ffn_down"):
            shared[k] = shared[k][:1, :1]
        for k in ("w_conv_in", "w_conv_out", "w_q", "w_o"):
            shared[k] = shared[k][:1]
    shared = {k: np.ascontiguousarray(v) for k, v in shared.items()}
    shared["consts"] = _consts()
    x = np.asarray(inputs["x"], dtype=np.float32)
    in_maps = []
    for b in range(ncores):
        m = dict(shared)
        m["x"] = np.ascontiguousarray(x[b])
        in_maps.append(m)
    res = run_bass_kernel_spmd(nc, in_maps, core_ids=list(range(ncores)))
    out = np.stack([np.asarray(res.results[b]["out"], dtype=np.float32) for b in range(ncores)], axis=0)
    return out
```

```python
import math
from contextlib import ExitStack

import numpy as np
import concourse.bass as bass
import concourse.mybir as mybir
from concourse.bass_utils import run_bass_kernel_spmd

F32 = mybir.dt.float32
BF16 = mybir.dt.bfloat16
AF = mybir.ActivationFunctionType
ALU = mybir.AluOpType
AX = mybir.AxisListType

D = 1024
SEQ = 2048
NB = 8
DFF = 2816
EPS = 1e-6
TCH = 1024
NCH = SEQ // TCH
SAME_ENG_SYNC = True


class _Op:
    __slots__ = ("eng", "fn", "dma", "lane", "deps", "idx", "ms", "cnt", "waits")


class Prog:
    def __init__(self):
        self.ops = []
        self.last_w = {}
        self.readers = {}
        self.lane_last = {}

    def add(self, eng, fn, reads=(), writes=(), lane=None):
        op = _Op()
        op.eng = eng
        op.fn = fn
        op.dma = lane is not None
        op.lane = lane
        op.idx = len(self.ops)
        op.ms = False
        op.cnt = 0
        deps = set()
        for t in reads:
            w = self.last_w.get(t)
            if w is not None:
                deps.add(w)
        for t in writes:
            w = self.last_w.get(t)
            if w is not None:
                deps.add(w)
            rs = self.readers.get(t)
            if rs:
                deps.update(rs)
        if lane is not None:
            p = self.lane_last.get(lane)
            if p is not None:
                deps.add(p)
            self.lane_last[lane] = op.idx
        for t in reads:
            self.readers.setdefault(t, []).append(op.idx)
        for t in writes:
            self.last_w[t] = op.idx
            self.readers[t] = []
        deps.discard(op.idx)
        op.deps = deps
        self.ops.append(op)
        return op

    @staticmethod
    def _needs_sync(p, op):
        if p.dma:
            return True
        if p.eng != op.eng:
            return True
        if p.eng == "pe":
            return False
        return SAME_ENG_SYNC

    def emit(self, nc, stack):
        ops = self.ops
        for op in ops:
            best = {}
            for d in op.deps:
                p = ops[d]
                if (not p.dma) and self._needs_sync(p, op):
                    if best.get(p.eng, -1) < d:
                        best[p.eng] = d
            for d in best.values():
                ops[d].ms = True
        eng_cnt = {}
        lane_cnt = {}
        for op in ops:
            if op.dma:
                lane_cnt[op.lane] = lane_cnt.get(op.lane, 0) + 1
                op.cnt = 16 * lane_cnt[op.lane]
            elif op.ms:
                eng_cnt[op.eng] = eng_cnt.get(op.eng, 0) + 1
                op.cnt = eng_cnt[op.eng]
        waited = {}
        for op in ops:
            need = {}
            for d in op.deps:
                p = ops[d]
                if not self._needs_sync(p, op):
                    continue
                key = ("lane", p.lane) if p.dma else ("eng", p.eng)
                if need.get(key, 0) < p.cnt:
                    need[key] = p.cnt
                assert p.dma or p.cnt > 0 or any(
                    (not ops[d2].dma) and ops[d2].eng == p.eng and d2 > d for d2 in op.deps)
            wd = waited.setdefault(op.eng, {})
            op.waits = []
            for k, v in need.items():
                if wd.get(k, 0) < v:
                    op.waits.append((k, v))
                    wd[k] = v
        sems = {}
        for e in sorted(eng_cnt):
            sems[("eng", e)] = stack.enter_context(nc.semaphore("s_" + e))
        for i, l in enumerate(lane_cnt):
            sems[("lane", l)] = stack.enter_context(nc.semaphore("l_%d" % i))
        per = {}
        for op in ops:
            per.setdefault(op.eng, []).append(op)
        self.stats = {e: len(v) for e, v in per.items()}
        self.stats["sems"] = len(sems)

        def mk(name):
            lst = per.get(name, [])

            def body(e):
                for op in lst:
                    for k, v in op.waits:
                        e.wait_ge(sems[k], v)
                    if op.fn is None:
                        continue
                    ins = op.fn(e)
                    if op.dma:
                        ins.then_inc(sems[("lane", op.lane)], 16)
                    elif op.ms:
                        ins.then_inc(sems[("eng", op.eng)], 1)

            return body

        with nc.Block() as block:
            block.tensor(mk("pe"))
            block.scalar(mk("act"))
            block.vector(mk("dve"))
            block.gpsimd(mk("pool"))
            block.sync(mk("sp"))


def lambda_init(layer):
    return 0.8 - 0.6 * math.exp(-0.3 * layer)


DBG = {"ffn_stage": 99, "lite": False}


def build_nc(nstop=999):
    nc = bass.Bass("TRN2", target_bir_lowering=False)
    nl4 = 1 if DBG["lite"] else 4
    nl2 = 1 if DBG["lite"] else 2

    def din(name, shape):
        return nc.dram_tensor(name, list(shape), F32, kind="ExternalInput").ap()

    x_d = din("x", (SEQ, D))
    gnorm_d = din("g_norm", (4, 6, D))
    wg_d = din("w_ffn_gate", (nl4, nl2, D, DFF))
    wu_d = din("w_ffn_up", (nl4, nl2, D, DFF))
    wd_d = din("w_ffn_down", (nl4, nl2, DFF, D))
    wci_d = din("w_conv_in", (nl2, D, 3 * D))
    wc_d = din("w_conv", (2, 3, D))
    wco_d = din("w_conv_out", (nl2, D, D))
    gkv_d = din("g_kv", (D,))
    wkv_d = din("w_kv", (D, 2 * D))
    wq_d = din("w_q", (nl2, D, D))
    lq1_d = din("lambda_q1", (2, 64))
    lk1_d = din("lambda_k1", (2, 64))
    lq2_d = din("lambda_q2", (2, 64))
    lk2_d = din("lambda_k2", (2, 64))
    gsub_d = din("g_subln", (2, 128))
    wo_d = din("w_o", (nl2, D, D))
    cst_d = din("consts", (128, 256))
    out_d = nc.dram_tensor("out", [SEQ, D], F32, kind="ExternalOutput").ap()

    P = Prog()
    with ExitStack() as st:
        def sb(name, shape, dt):
            return st.enter_context(nc.sbuf_tensor(name, list(shape), dt))

        X = sb("X", (128, 8, TCH), F32)
        H = sb("H", (128, 8, TCH), BF16)
        Fr = sb("Fr", (128, 16, 512), F32)
        A = sb("A", (128, 11, TCH), BF16)
        KT = sb("KT", (128, 8, SEQ), BF16)
        V = sb("V", (128, 16, D), BF16)
        slots = [sb("slot%d" % i, (128, 6144), BF16) for i in range(2)]
        cst = sb("cst", (128, 256), F32)
        tri = sb("tri", (128, 128), BF16)
        ones = sb("ones", (128, 128), BF16)
        cols = sb("cols", (128, 256), F32)
        eps_t = sb("eps", (128, 1), F32)
        sq = [sb("sq%d" % i, (128, 512), BF16) for i in range(3)]
        sd = sb("sd", (128, 512), F32)
        rstd = [sb("rstd%d" % i, (128, 512), F32) for i in range(2)]
        sg = [sb("sg%d" % i, (128, 512), F32) for i in range(2)]
        halo = sb("halo", (128, 2, 8, 2), F32)
        small = sb("small", (128, 16), F32)
        PS = st.enter_context(nc.psum_tensor("PS", [128, 8, 512], F32))
        ident = cst[:, 0:128]

        def pst(b):
            return [("ps", b, 0), ("ps", b, 1)]

        def tbs(tb):
            return slice(tb * 512, (tb + 1) * 512)

        def mm(out, lhsT, rhs, start, stop, reads, writes):
            P.add("pe", lambda e: e.matmul(out, lhsT, rhs, start=start, stop=stop), reads, writes)

        ctr = {"slot": 0, "stat": 0, "sq": 0, "rstd": 0, "u": 0}

        def next_slot():
            s = ctr["slot"] % 2
            ctr["slot"] += 1
            return s

        def stoks(s, a, b):
            return [("slot", s, i) for i in range(a, b)]

        P.add("dve", lambda e: e.memset(ones[:], 1.0), writes=["ones"])
        P.add("dve", lambda e: e.memset(eps_t[:], EPS), writes=["eps"])
        P.add("dve", lambda e: e.memset(small[:, 12:13], -0.5), writes=["mhalf"])
        P.add("sp", lambda e: e.dma_start(out=cst[:], in_=cst_d), writes=["cst"], lane="cst")
        P.add("dve", lambda e: e.tensor_copy(out=tri[:], in_=cst[:, 128:256]), reads=["cst"], writes=["tri"])
        gview = gnorm_d.rearrange("l n (j p) -> (l n j) p", p=128)
        P.add("sp", lambda e: e.dma_start(out=Fr[0:128, 0, 0:128], in_=gview[0:128, :]),
              writes=[("F", 0)], lane="p0")
        P.add("sp", lambda e: e.dma_start(out=Fr[0:64, 1, 0:128], in_=gview[128:192, :]),
              writes=[("F", 1)], lane="p1")
        P.add("sp", lambda e: e.dma_start(out=Fr[64:72, 1, 0:128], in_=gkv_d.rearrange("(j p) -> j p", p=128)),
              writes=[("F", 1)], lane="p2")
        P.add("sp", lambda e: e.dma_start(out=Fr[72:120, 1, 0:128],
                                          in_=wc_d.rearrange("l w (j p) -> (l w j) p", p=128)),
              writes=[("F", 1)], lane="p3")
        P.add("sp", lambda e: e.dma_start(out=Fr[120:122, 1, 0:128], in_=gsub_d),
              writes=[("F", 1)], lane="p4")
        P.add("pe", lambda e: e.transpose(PS[:, 0, 0:128], Fr[0:128, 0, 0:128], ident),
              reads=[("F", 0), "cst"], writes=pst(0))
        P.add("pe", lambda e: e.transpose(PS[:, 0, 128:250], Fr[0:122, 1, 0:128], cst[0:122, 0:122]),
              reads=[("F", 1), "cst"], writes=pst(0))
        P.add("dve", lambda e: e.tensor_copy(out=cols[:, 0:250], in_=PS[:, 0, 0:250]), reads=pst(0), writes=["cols"])
        gv4 = cols[:, 0:192].rearrange("p (l n j) -> p l n j", l=4, n=6)
        for n in (1, 5):
            P.add("dve", (lambda e, n=n: e.tensor_scalar(out=gv4[:, :, n, :], in0=gv4[:, :, n, :], scalar1=0.5,
                                                         scalar2=None, op0=ALU.mult)),
                  reads=["cols"], writes=["cols"])
        lv = Fr[:, 2, :].rearrange("p (a b) -> p a b", a=4)
        for i, ld in enumerate((lq1_d, lk1_d, lq2_d, lk2_d)):
            P.add("sp", (lambda e, i=i, ld=ld: e.dma_start(
                out=lv[:, i, :], in_=ld.rearrange("a b -> (a b)").partition_broadcast(128))),
                writes=[("F", 2)], lane="l%d" % i)
        pr = Fr[:, 3, 0:256].rearrange("p (a b) -> p a b", a=2)
        P.add("dve", lambda e: e.tensor_tensor(out=pr[:, 0, :], in0=lv[:, 0, :], in1=lv[:, 1, :], op=ALU.mult),
              reads=[("F", 2)], writes=[("F", 3)])
        P.add("dve", lambda e: e.tensor_tensor(out=pr[:, 1, :], in0=lv[:, 2, :], in1=lv[:, 3, :], op=ALU.mult),
              reads=[("F", 2)], writes=[("F", 3)])
        P.add("dve", lambda e: e.reduce_sum(out=small[:, 0:4],
                                            in_=Fr[:, 3, 0:256].rearrange("p (a b) -> p a b", a=4), axis=AX.X),
              reads=[("F", 3)], writes=["small"])
        P.add("act", lambda e: e.activation(out=small[:, 4:8], in_=small[:, 0:4], func=AF.Exp),
              reads=["small"], writes=["small"])
        P.add("dve", lambda e: e.tensor_tensor(out=small[:, 8:10], in0=small[:, 6:8], in1=small[:, 4:6],
                                               op=ALU.subtract), reads=["small"], writes=["small"])
        for jl in range(2):
            li = lambda_init(jl + 2)
            P.add("dve", (lambda e, jl=jl, li=li: e.tensor_scalar(out=small[:, 8 + jl:9 + jl], in0=small[:, 8 + jl:9 + jl],
                                                                  scalar1=-li, scalar2=None, op0=ALU.add)),
                  reads=["small"], writes=["small"])
            P.add("dve", (lambda e, jl=jl, li=li: e.tensor_scalar(out=small[:, 10 + jl:11 + jl],
                                                                  in0=cols[:, 248 + jl:249 + jl],
                                                                  scalar1=1.0 - li, scalar2=None, op0=ALU.mult)),
                  reads=["small", "cols"], writes=["small"])

        def gidx(l, n):
            return (l * 6 + n) * 8

        def rstd_of(src_fn, src_tok, ncols, invd, nj, lnexp=True):
            b = 6 + ctr["stat"] % 2
            ctr["stat"] += 1
            ssb = PS[:, b, 0:ncols]
            for j in range(nj):
                r = ctr["sq"] % 3
                ctr["sq"] += 1
                P.add("act", (lambda e, j=j, r=r: e.activation(out=sq[r][:, 0:ncols], in_=src_fn(j), func=AF.Square)),
                      reads=[src_tok(j)], writes=[("sq", r)])
                mm(ssb, ones[:], sq[r][:, 0:ncols], j == 0, j == nj - 1, [("sq", r), "ones"], pst(b))
            rb = ctr["rstd"] % 2
            ctr["rstd"] += 1
            if lnexp:
                P.add("act", lambda e: e.activation(out=sd[:, 0:ncols], in_=ssb, func=AF.Ln, bias=eps_t[:, 0:1],
                                                    scale=invd), reads=pst(b) + ["eps"], writes=["sd"])
                P.add("act", lambda e: e.activation(out=rstd[rb][:, 0:ncols], in_=sd[:, 0:ncols], func=AF.Exp,
                                                    scale=-0.5), reads=["sd"], writes=[("rstd", rb)])
            else:
                P.add("act", lambda e: e.activation(out=sd[:, 0:ncols], in_=ssb, func=AF.Sqrt, bias=eps_t[:, 0:1],
                                                    scale=invd), reads=pst(b) + ["eps"], writes=["sd"])
                P.add("dve", lambda e: e.reciprocal(out=rstd[rb][:, 0:ncols], in_=sd[:, 0:ncols]),
                      reads=["sd"], writes=[("rstd", rb)])
            return rstd[rb][:, 0:ncols], ("rstd", rb)

        def prenorm(gbase, tb):
            rs, rtok = rstd_of(lambda j: X[:, j, tbs(tb)], lambda j: ("X", j, tb), 512, 1.0 / D, 8)
            for j in range(8):
                P.add("dve", (lambda e, j=j: e.scalar_tensor_tensor(
                    out=H[:, j, tbs(tb)], in0=X[:, j, tbs(tb)], scalar=cols[:, gbase + j:gbase + j + 1], in1=rs,
                    op0=ALU.mult, op1=ALU.mult)),
                    reads=[("X", j, tb), rtok, "cols"], writes=[("H", j, tb)])

        def post_stats(tb):
            return rstd_of(lambda j: Fr[:, 2 * j + tb, :], lambda j: ("F", 2 * j + tb), 512, 1.0 / D, 8)

        def post_apply(gbase, tb, rs, rtok):
            for j in range(8):
                blk = 2 * j + tb
                P.add("dve", (lambda e, j=j, blk=blk: e.scalar_tensor_tensor(
                    out=Fr[:, blk, :], in0=Fr[:, blk, :], scalar=cols[:, gbase + j:gbase + j + 1], in1=rs,
                    op0=ALU.mult, op1=ALU.mult)),
                    reads=[("F", blk), rtok, "cols"], writes=[("F", blk)])
                P.add("dve", (lambda e, j=j, blk=blk: e.tensor_tensor(
                    out=X[:, j, tbs(tb)], in0=X[:, j, tbs(tb)], in1=Fr[:, blk, :], op=ALU.add)),
                    reads=[("F", blk), ("X", j, tb)], writes=[("X", j, tb)])

        def wview(w2d):
            return w2d.rearrange("(k p) f -> p k f", p=128)

        def proj(Wv, c0, src, srctok, evac, hook=None):
            pcs = []
            for mh in range(2):
                s = next_slot()
                wv = slots[s][:, 0:4096].rearrange("p (k f) -> p k f", k=8)
                P.add("pool", (lambda e, wv=wv, mh=mh: e.dma_start(out=wv, in_=Wv[:, :, c0 + mh * 512:c0 + (mh + 1) * 512])),
                      writes=stoks(s, 0, 4), lane=("w", s, 0))
                pcs.append((s, wv))
            for tb in range(2):
                for m in range(8):
                    s, wv = pcs[m // 4]
                    mi = m % 4
                    bank = ctr["u"] % 2
                    ctr["u"] += 1
                    Pp = PS[:, bank, :]
                    for k in range(8):
                        mm(Pp, wv[:, k, mi * 128:(mi + 1) * 128], src[:, k, tbs(tb)], k == 0, k == 7,
                           stoks(s, 0, 4) + [(srctok, k, tb)], pst(bank))
                    evac(m, tb, Pp, bank)
                    if hook is not None and tb == 1 and m == 4:
                        hook()

        def evac_to_F(m, tb, Pp, bank):
            blk = 2 * m + tb
            P.add("act", lambda e: e.copy(out=Fr[:, blk, :], in_=Pp), reads=pst(bank), writes=[("F", blk)])

        def ffn(l, w, hook=None):
            Wg = wview(wg_d[l, w])
            Wu = wview(wu_d[l, w])
            Wd = wd_d[l, w].rearrange("(f p) m -> p f m", p=128)
            for hf in range(2):
                for (f0, n) in ((0, 3), (3, 3), (6, 3), (9, 2)):
                    s = next_slot()
                    gv = slots[s][:, 0:8 * n * 128].rearrange("p (k f) -> p k f", k=8)
                    uv = slots[s][:, 3072:3072 + 8 * n * 128].rearrange("p (k f) -> p k f", k=8)
                    fa = hf * 11 + f0
                    P.add("pool", (lambda e, gv=gv, fa=fa, n=n: e.dma_start(out=gv, in_=Wg[:, :, fa * 128:(fa + n) * 128])),
                          writes=stoks(s, 0, 3), lane=("w", s, 0))
                    P.add("pool", (lambda e, uv=uv, fa=fa, n=n: e.dma_start(out=uv, in_=Wu[:, :, fa * 128:(fa + n) * 128])),
                          writes=stoks(s, 3, 6), lane=("w", s, 3))
                    order = ([(fi, tb) for tb in range(2) for fi in range(n)] if (hf == 0 and f0 == 0)
                             else [(fi, tb) for fi in range(n) for tb in range(2)])
                    for (fi, tb) in order:
                        fl = f0 + fi
                        if True:
                            par = ctr["u"] % 2
                            ctr["u"] += 1
                            G = PS[:, 2 * par, :]
                            U = PS[:, 2 * par + 1, :]
                            for k in range(8):
                                mm(G, gv[:, k, fi * 128:(fi + 1) * 128], H[:, k, tbs(tb)], k == 0, k == 7,
                                   stoks(s, 0, 3) + [("H", k, tb)], pst(2 * par))
                            for k in range(8):
                                mm(U, uv[:, k, fi * 128:(fi + 1) * 128], H[:, k, tbs(tb)], k == 0, k == 7,
                                   stoks(s, 3, 6) + [("H", k, tb)], pst(2 * par + 1))
                            P.add("act", (lambda e, G=G, par=par: e.activation(out=sg[par][:], in_=G, func=AF.Silu)),
                                  reads=pst(2 * par), writes=[("sg", par)])
                            P.add("dve", (lambda e, U=U, par=par, fl=fl, tb=tb: e.tensor_tensor(
                                out=A[:, fl, tbs(tb)], in0=U, in1=sg[par][:], op=ALU.mult)),
                                reads=pst(2 * par + 1) + [("sg", par)], writes=[("A", fl, tb)])
                pieces = []
                for m0 in (0, 4):
                    s = next_slot()
                    dv = slots[s][:, 0:5632].rearrange("p (f m) -> p f m", f=11)
                    P.add("pool", (lambda e, dv=dv, hf=hf, m0=m0: e.dma_start(
                        out=dv, in_=Wd[:, hf * 11:(hf + 1) * 11, m0 * 128:(m0 + 4) * 128])),
                        writes=stoks(s, 0, 6), lane=("w", s, 0))
                    pieces.append((s, dv))
                if hf == 0:
                    dorder = [(m, tb) for m in range(8) for tb in range(2)]
                else:
                    dorder = [(m, tb) for tb in range(2) for m in range(8)]
                for (m, tb) in dorder:
                    s, dv = pieces[m // 4]
                    mi = m % 4
                    par = ctr["u"] % 2
                    ctr["u"] += 1
                    Dp = PS[:, 4 + par, :]
                    for fl in range(11):
                        mm(Dp, dv[:, fl, mi * 128:(mi + 1) * 128], A[:, fl, tbs(tb)], fl == 0, fl == 10,
                           stoks(s, 0, 6) + [("A", fl, tb)], pst(4 + par))
                    blk = 2 * m + tb
                    if hf == 0:
                        P.add("act", (lambda e, Dp=Dp, blk=blk: e.copy(out=Fr[:, blk, :], in_=Dp)),
                              reads=pst(4 + par), writes=[("F", blk)])
                    else:
                        P.add("dve", (lambda e, Dp=Dp, blk=blk: e.tensor_tensor(
                            out=Fr[:, blk, :], in0=Dp, in1=Fr[:, blk, :], op=ALU.add)),
                            reads=pst(4 + par) + [("F", blk)], writes=[("F", blk)])
                    if hook is not None and hf == 1 and tb == 1 and m == 4:
                        hook()

        def conv_mixer(l, c, hook=None):
            Win = wview(wci_d[l])
            for jp in range(4):
                s = next_slot()
                parts = [slots[s][:, pt * 2048:(pt + 1) * 2048].rearrange("p (k f) -> p k f", k=8) for pt in range(3)]
                for pt in range(3):
                    P.add("pool", (lambda e, pt=pt, jp=jp, parts=parts: e.dma_start(
                        out=parts[pt], in_=Win[:, :, pt * 1024 + jp * 256:pt * 1024 + (jp + 1) * 256])),
                        writes=stoks(s, 2 * pt, 2 * pt + 2), lane=("w", s, 2 * pt))
                cus = []
                for jj in range(2):
                    j = 2 * jp + jj
                    cub = j % 2
                    cu = Fr[:, cub * 3:cub * 3 + 3, :].rearrange("p a b -> p (a b)")
                    cutoks = [("F", cub * 3 + i) for i in range(3)]
                    if c == 0:
                        P.add("dve", (lambda e, cu=cu: e.memset(cu[:, 0:2], 0.0)), writes=cutoks)
                    else:
                        P.add("dve", (lambda e, cu=cu, j=j: e.tensor_copy(out=cu[:, 0:2], in_=halo[:, l, j, :])),
                              reads=[("halo", l, j)], writes=cutoks)
                    cus.append((cu, cutoks))
                for tb in range(2):
                    for jj in range(2):
                        j = 2 * jp + jj
                        cu, cutoks = cus[jj]
                        w0 = 200 + (l * 3 + 0) * 8 + j
                        w1 = 200 + (l * 3 + 1) * 8 + j
                        w2 = 200 + (l * 3 + 2) * 8 + j
                        par = ctr["u"] % 2
                        ctr["u"] += 1
                        Pb, Pc, Pu = (PS[:, 3 * par + i, :] for i in range(3))
                        for pt in range(3):
                            for k in range(8):
                                mm(PS[:, 3 * par + pt, :], parts[pt][:, k, jj * 128:(jj + 1) * 128], H[:, k, tbs(tb)],
                                   k == 0, k == 7, stoks(s, 2 * pt, 2 * pt + 2) + [("H", k, tb)], pst(3 * par + pt))
                        ucp = Fr[:, 6 + par, :]
                        z = Fr[:, 8 + par, :]
                        o2 = 2 + tb * 512
                        P.add("act", (lambda e, ucp=ucp, Pu=Pu: e.copy(out=ucp, in_=Pu)),
                              reads=pst(3 * par + 2), writes=[("F", 6 + par)])
                        P.add("dve", (lambda e, cu=cu, o2=o2, Pc=Pc, ucp=ucp: e.tensor_tensor(
                            out=cu[:, o2:o2 + 512], in0=Pc, in1=ucp, op=ALU.mult)),
                            reads=pst(3 * par + 1) + [("F", 6 + par)], writes=cutoks)
                        P.add("act", (lambda e, z=z, cu=cu, o2=o2, w2=w2: e.mul(out=z, in_=cu[:, o2:o2 + 512],
                                                                               mul=cols[:, w2:w2 + 1])),
                              reads=cutoks + ["cols"], writes=[("F", 8 + par)])
                        P.add("dve", (lambda e, z=z, cu=cu, o2=o2, w1=w1: e.scalar_tensor_tensor(
                            out=z, in0=cu[:, o2 - 1:o2 + 511], scalar=cols[:, w1:w1 + 1], in1=z,
                            op0=ALU.mult, op1=ALU.add)), reads=cutoks + ["cols", ("F", 8 + par)],
                            writes=[("F", 8 + par)])
                        P.add("dve", (lambda e, z=z, cu=cu, o2=o2, w0=w0: e.scalar_tensor_tensor(
                            out=z, in0=cu[:, o2 - 2:o2 + 510], scalar=cols[:, w0:w0 + 1], in1=z,
                            op0=ALU.mult, op1=ALU.add)), reads=cutoks + ["cols", ("F", 8 + par)],
                            writes=[("F", 8 + par)])
                        P.add("dve", (lambda e, z=z, Pb=Pb, j=j, tb=tb: e.tensor_tensor(
                            out=A[:, j, tbs(tb)], in0=Pb, in1=z, op=ALU.mult)),
                            reads=pst(3 * par) + [("F", 8 + par)], writes=[("A", j, tb)])
                if c == 0:
                    for jj in range(2):
                        j = 2 * jp + jj
                        cu, cutoks = cus[jj]
                        P.add("dve", (lambda e, cu=cu, j=j: e.tensor_copy(out=halo[:, l, j, :], in_=cu[:, 1024:1026])),
                              reads=cutoks, writes=[("halo", l, j)])
            proj(wview(wco_d[l]), 0, A, "A", evac_to_F, hook)

        def kv_proj(c):
            Wkv = wview(wkv_d)

            def evac_k(m, tb, Pp, bank):
                P.add("act", lambda e: e.copy(out=KT[:, m, c * TCH + tb * 512:c * TCH + (tb + 1) * 512], in_=Pp),
                      reads=pst(bank), writes=[("K", m, c)])

            proj(Wkv, 0, H, "H", evac_k)
            for eh in range(2):
                s = next_slot()
                wv = slots[s][:, 0:4096].rearrange("p (k f) -> p k f", k=8)
                P.add("pool", (lambda e, wv=wv, eh=eh: e.dma_start(out=wv, in_=Wkv[:, :, D + eh * 512:D + (eh + 1) * 512])),
                      writes=stoks(s, 0, 4), lane=("w", s, 0))
                for tt in range(8):
                    bank = ctr["u"] % 2
                    ctr["u"] += 1
                    Pp = PS[:, bank, :]
                    for k in range(8):
                        mm(Pp, H[:, k, tt * 128:(tt + 1) * 128], wv[:, k, :], k == 0, k == 7,
                           stoks(s, 0, 4) + [("H", k, tt // 4)], pst(bank))
                    kt = c * 8 + tt
                    if tt % 2 == 0:
                        P.add("act", (lambda e, kt=kt, eh=eh, Pp=Pp: e.copy(out=V[:, kt, eh * 512:(eh + 1) * 512], in_=Pp)),
                              reads=pst(bank), writes=[("V", kt)])
                    else:
                        P.add("dve", (lambda e, kt=kt, eh=eh, Pp=Pp: e.tensor_copy(out=V[:, kt, eh * 512:(eh + 1) * 512], in_=Pp)),
                              reads=pst(bank), writes=[("V", kt)])

        def attn(l, c, hook=None):
            jl = l - 2

            def evac_q(m, tb, Pp, bank):
                P.add("act", lambda e: e.mul(out=A[:, m, tbs(tb)], in_=Pp, mul=0.125),
                      reads=pst(bank), writes=[("A", m, tb)])

            proj(wview(wq_d[jl]), 0, H, "H", evac_q)

            steps = []
            for hd in range(8):
                for u in range(4 * c, 4 * c + 4):
                    for kt in range(2 * u + 2):
                        steps.append((hd, u, kt))
            Eb = [Fr[:, r, :].bitcast(BF16)[:, 0:512].rearrange("p (c q) -> p c q", c=2) for r in range(3)]
            neglam = small[:, 8 + jl:9 + jl]
            gs = small[:, 10 + jl:11 + jl]
            tri_b = tri[:].unsqueeze(1).to_broadcast([128, 2, 128])

            def geom(i):
                hd, u, kt = steps[i]
                d = kt - 2 * u
                q0 = 128 if d == 1 else 0
                ql = (u - 4 * c) * 256
                return hd, u, kt, d, q0, ql

            it_of = []
            _it = 0
            for (hd_, u_, kt_) in steps:
                it_of.append(_it)
                if kt_ == 2 * u_ + 1:
                    _it += 1
            qz = [Fr[:, 9 + p_, :].bitcast(BF16)[:, 0:512].rearrange("p (c q) -> p c q", c=2) for p_ in range(2)]
            for p_ in range(2):
                P.add("dve", (lambda e, p_=p_: e.memset(Fr[:, 9 + p_, :].bitcast(BF16)[:, 0:512], 0.0)),
                      writes=[("F", 9 + p_)])

            def emit_S(i):
                hd, u, kt, d, q0, ql = geom(i)
                par = i % 2
                ipar = it_of[i] % 2
                if kt == 0:
                    P.add("dve", lambda e: e.tensor_copy(out=qz[ipar][0:64, 0, :], in_=A[0:64, hd, ql:ql + 256]),
                          reads=[("A", hd, ql // 512)], writes=[("F", 9 + ipar)])
                    P.add("dve", lambda e: e.tensor_copy(out=qz[ipar][64:128, 1, :], in_=A[64:128, hd, ql:ql + 256]),
                          reads=[("A", hd, ql // 512)], writes=[("F", 9 + ipar)])
                for cm in range(2):
                    mm(PS[:, par, cm * 256 + q0:(cm + 1) * 256],
                       KT[:, hd, kt * 128:(kt + 1) * 128],
                       qz[ipar][:, cm, q0:256], True, True,
                       [("K", hd, kt // 8), ("F", 9 + ipar)], pst(par))

            def emit_exp(i):
                hd, u, kt, d, q0, ql = geom(i)
                par = i % 2
                r = i % 3
                P.add("act", lambda e: e.activation(
                    out=Eb[r][:, :, q0:256],
                    in_=PS[:, par, :].rearrange("p (c q) -> p c q", c=2)[:, :, q0:256], func=AF.Exp),
                    reads=pst(par), writes=[("F", r)])
                if d >= 0:
                    P.add("dve", lambda e: e.tensor_tensor(out=Eb[r][:, :, q0:q0 + 128], in0=Eb[r][:, :, q0:q0 + 128],
                                                           in1=tri_b, op=ALU.mult),
                          reads=[("F", r), "tri"], writes=[("F", r)])

            pending = []

            def emit_PV(i, it):
                hd, u, kt, d, q0, ql = geom(i)
                r = i % 3
                ipar = it % 2
                bA = 2 + ipar
                bS = 4 + ipar
                Ab = PS[:, bA, :].rearrange("p (c q) -> p c q", c=2)
                Sb = PS[:, bS, :].rearrange("p (c q) -> p c q", c=2)
                first = kt == 0
                last = kt == 2 * u + 1
                if q0 == 0:
                    E2 = Fr[:, r, :].bitcast(BF16)[:, 0:512]
                    mm(PS[:, bA, :], V[:, kt, hd * 128:(hd + 1) * 128], E2, first, last,
                       [("V", kt), ("F", r)], pst(bA))
                    mm(PS[:, bS, :], ones[:], E2, first, last, [("F", r), "ones"], pst(bS))
                else:
                    for cm in range(2):
                        mm(Ab[:, cm, q0:256], V[:, kt, hd * 128:(hd + 1) * 128], Eb[r][:, cm, q0:256], first,
                           last and cm == 1, [("V", kt), ("F", r)], pst(bA))
                    for cm in range(2):
                        mm(Sb[:, cm, q0:256], ones[:], Eb[r][:, cm, q0:256], first, last and cm == 1,
                           [("F", r), "ones"], pst(bS))
                if last:
                    b0 = 3 + 3 * ipar
                    rr = Fr[:, b0, :]
                    tt_ = Fr[:, b0 + 1, :]
                    ob = Fr[:, b0 + 2, 0:256]
                    P.add("dve", lambda e: e.reciprocal(out=rr, in_=PS[:, bS, :]),
                          reads=pst(bS), writes=[("F", b0)])
                    P.add("dve", lambda e: e.tensor_tensor(out=tt_, in0=PS[:, bA, :], in1=rr, op=ALU.mult),
                          reads=pst(bA) + [("F", b0)], writes=[("F", b0 + 1)])
                    P.add("dve", lambda e: e.scalar_tensor_tensor(out=H[:, hd, ql:ql + 256], in0=tt_[:, 256:512],
                                                                  scalar=neglam, in1=tt_[:, 0:256],
                                                                  op0=ALU.mult, op1=ALU.add),
                          reads=[("F", b0 + 1), "small"], writes=[("H", hd, ql // 512)])

            n = len(steps)
            emit_S(0)
            it = 0
            for i in range(n):
                if i + 1 < n:
                    emit_S(i + 1)
                emit_exp(i)
                emit_PV(i, it)
                if steps[i][2] == 2 * steps[i][1] + 1:
                    it += 1
                for pnd in list(pending):
                    pnd[0] -= 1
                    if pnd[0] <= 0:
                        pending.remove(pnd)
                        pnd[1]()
            for pnd in pending:
                pnd[1]()
            for tb in range(2):
                for hd in range(8):
                    rs, rtok = rstd_of(lambda j, hd=hd, tb=tb: H[:, hd, tbs(tb)], lambda j, hd=hd, tb=tb: ("H", hd, tb),
                                       512, 1.0 / 128.0, 1, lnexp=False)
                    P.add("dve", (lambda e, hd=hd, tb=tb, rs=rs: e.scalar_tensor_tensor(
                        out=H[:, hd, tbs(tb)], in0=H[:, hd, tbs(tb)], scalar=gs, in1=rs,
                        op0=ALU.mult, op1=ALU.mult)),
                        reads=[("H", hd, tb), rtok, "small"], writes=[("H", hd, tb)])
            proj(wview(wo_d[jl]), 0, H, "H", evac_to_F, hook)

        def load_x(c):
            for tt in range(8):
                P.add("sp", (lambda e, tt=tt: e.dma_start(
                    out=Fr[:, 2 * tt:2 * tt + 2, :].rearrange("p a b -> p (a b)"),
                    in_=x_d[c * TCH + tt * 128:c * TCH + (tt + 1) * 128, :])),
                    writes=[("F", 2 * tt), ("F", 2 * tt + 1)], lane=("x", tt % 4))
            for j in range(8):
                for g in range(2):
                    bank = ctr["u"] % 4
                    ctr["u"] += 1
                    for ti in range(4):
                        tt = g * 4 + ti
                        xs = Fr[:, 2 * tt:2 * tt + 2, :].rearrange("p a b -> p (a b)")
                        P.add("pe", (lambda e, bank=bank, ti=ti, xs=xs, j=j: e.transpose(
                            PS[:, bank, ti * 128:(ti + 1) * 128], xs[:, j * 128:(j + 1) * 128], ident)),
                            reads=[("F", 2 * tt), ("F", 2 * tt + 1), "cst"], writes=pst(bank))
                    if (j + g) % 2 == 0:
                        P.add("act", (lambda e, bank=bank, j=j, g=g: e.copy(out=X[:, j, tbs(g)], in_=PS[:, bank, :])),
                              reads=pst(bank), writes=[("X", j, g)])
                    else:
                        P.add("dve", (lambda e, bank=bank, j=j, g=g: e.tensor_copy(out=X[:, j, tbs(g)], in_=PS[:, bank, :])),
                              reads=pst(bank), writes=[("X", j, g)])

        def store_x(c):
            for tt in range(8):
                for g in range(2):
                    bank = ctr["u"] % 4
                    ctr["u"] += 1
                    for ji in range(4):
                        j = g * 4 + ji
                        P.add("pe", (lambda e, bank=bank, ji=ji, j=j, tt=tt: e.transpose(
                            PS[:, bank, ji * 128:(ji + 1) * 128], X[:, j, tt * 128:(tt + 1) * 128], ident)),
                            reads=[("X", j, tt // 4), "cst"], writes=pst(bank))
                    blk = 2 * tt + g
                    if g == 0:
                        P.add("act", (lambda e, bank=bank, blk=blk: e.copy(out=Fr[:, blk, :], in_=PS[:, bank, :])),
                              reads=pst(bank), writes=[("F", blk)])
                    else:
                        P.add("dve", (lambda e, bank=bank, blk=blk: e.tensor_copy(out=Fr[:, blk, :], in_=PS[:, bank, :])),
                              reads=pst(bank), writes=[("F", blk)])
                P.add("sp", (lambda e, tt=tt: e.dma_start(
                    out=out_d[c * TCH + tt * 128:c * TCH + (tt + 1) * 128, :],
                    in_=Fr[:, 2 * tt:2 * tt + 2, :].rearrange("p a b -> p (a b)"))),
                    reads=[("F", 2 * tt), ("F", 2 * tt + 1)], writes=[("OUT", c, tt)], lane=("o", tt % 4))

        for c in range(NCH):
            load_x(c)
            subs = []
            for l in range(4):
                subs.append((lambda hook, l=l: ffn(l, 0, hook), gidx(l, 0), gidx(l, 1)))
                if l < 2:
                    subs.append((lambda hook, l=l, c=c: conv_mixer(l, c, hook), gidx(l, 2), gidx(l, 3)))
                else:
                    subs.append((lambda hook, l=l, c=c: attn(l, c, hook), gidx(l, 2), gidx(l, 3)))
                subs.append((lambda hook, l=l: ffn(l, 1, hook), gidx(l, 4), gidx(l, 5)))
                if l == 1:
                    subs.append((lambda hook, c=c: kv_proj(c), 192, None))
            nrun = nstop if nstop < 6 else (nstop if nstop > 6 else 6)
            if nstop > 6:
                nrun = nstop + 1
            subs = subs[:min(len(subs), nrun)]
            if subs:
                for tb in range(2):
                    prenorm(subs[0][1], tb)
            for i, (fn, gpre, gpost) in enumerate(subs):
                stash = {}

                def hook(stash=stash):
                    stash["s0"] = post_stats(0)

                fn(hook if gpost is not None else None)
                nxt = subs[i + 1][1] if i + 1 < len(subs) else None
                for tb in range(2):
                    if gpost is not None:
                        rs, rtok = stash["s0"] if (tb == 0 and "s0" in stash) else post_stats(tb)
                        post_apply(gpost, tb, rs, rtok)
                    if nxt is not None:
                        prenorm(nxt, tb)
            store_x(c)
        P.add("sp", None, reads=[("OUT", c, tt) for c in range(NCH) for tt in range(8)])
        P.emit(nc, st)
    nc._prog_stats = P.stats
    return nc


_CACHE = {}


def _consts():
    c = np.zeros((128, 256), np.float32)
    c[:, 0:128] = np.eye(128, dtype=np.float32)
    c[:, 128:256] = np.triu(np.ones((128, 128), np.float32))
    return c


def kernel(**inputs):
    nstop = int(inputs.pop("_nstop", 999))
    ncores = int(inputs.pop("_ncores", NB))
    if nstop not in _CACHE:
        _CACHE[nstop] = build_nc(nstop)
    nc = _CACHE[nstop]
    names = ["g_norm", "w_ffn_gate", "w_ffn_up", "w_ffn_down", "w_conv_in", "w_conv", "w_conv_out", "g_kv",
             "w_kv", "w_q", "lambda_q1", "lambda_k1", "lambda_q2", "lambda_k2", "g_subln", "w_o"]
    shared = {k: np.asarray(inputs[k], dtype=np.float32) for k in names}
    if DBG["lite"]:
        for k in ("w_ffn_gate", "w_ffn_up", "w_ffn_down"):
            shared[k] = shared[k][:1, :1]
        for k in ("w_conv_in", "w_conv_out", "w_q", "w_o"):
            shared[k] = shared[k][:1]
    shared = {k: np.ascontiguousarray(v) for k, v in shared.items()}
    shared["consts"] = _consts()
    x = np.asarray(inputs["x"], dtype=np.float32)
    in_maps = []
    for b in range(ncores):
        m = dict(shared)
        m["x"] = np.ascontiguousarray(x[b])
        in_maps.append(m)
    res = run_bass_kernel_spmd(nc, in_maps, core_ids=list(range(ncores)))
    out = np.stack([np.asarray(res.results[b]["out"], dtype=np.float32) for b in range(ncores)], axis=0)
    return out
```

```python
import math
from contextlib import ExitStack

import numpy as np
import concourse.bass as bass
import concourse.mybir as mybir
from concourse.bass_utils import run_bass_kernel_spmd

F32 = mybir.dt.float32
BF16 = mybir.dt.bfloat16
AF = mybir.ActivationFunctionType
ALU = mybir.AluOpType
AX = mybir.AxisListType

D = 1024
SEQ = 2048
NB = 8
DFF = 2816
EPS = 1e-6
TCH = 1024
NCH = SEQ // TCH
SAME_ENG_SYNC = True


class _Op:
    __slots__ = ("eng", "fn", "dma", "lane", "deps", "idx", "ms", "cnt", "waits")


class Prog:
    def __init__(self):
        self.ops = []
        self.last_w = {}
        self.readers = {}
        self.lane_last = {}

    def add(self, eng, fn, reads=(), writes=(), lane=None):
        op = _Op()
        op.eng = eng
        op.fn = fn
        op.dma = lane is not None
        op.lane = lane
        op.idx = len(self.ops)
        op.ms = False
        op.cnt = 0
        deps = set()
        for t in reads:
            w = self.last_w.get(t)
            if w is not None:
                deps.add(w)
        for t in writes:
            w = self.last_w.get(t)
            if w is not None:
                deps.add(w)
            rs = self.readers.get(t)
            if rs:
                deps.update(rs)
        if lane is not None:
            p = self.lane_last.get(lane)
            if p is not None:
                deps.add(p)
            self.lane_last[lane] = op.idx
        for t in reads:
            self.readers.setdefault(t, []).append(op.idx)
        for t in writes:
            self.last_w[t] = op.idx
            self.readers[t] = []
        deps.discard(op.idx)
        op.deps = deps
        self.ops.append(op)
        return op

    @staticmethod
    def _needs_sync(p, op):
        if p.dma:
            return True
        if p.eng != op.eng:
            return True
        if p.eng == "pe":
            return False
        return SAME_ENG_SYNC

    def emit(self, nc, stack):
        ops = self.ops
        for op in ops:
            best = {}
            for d in op.deps:
                p = ops[d]
                if (not p.dma) and self._needs_sync(p, op):
                    if best.get(p.eng, -1) < d:
                        best[p.eng] = d
            for d in best.values():
                ops[d].ms = True
        eng_cnt = {}
        lane_cnt = {}
        for op in ops:
            if op.dma:
                lane_cnt[op.lane] = lane_cnt.get(op.lane, 0) + 1
                op.cnt = 16 * lane_cnt[op.lane]
            elif op.ms:
                eng_cnt[op.eng] = eng_cnt.get(op.eng, 0) + 1
                op.cnt = eng_cnt[op.eng]
        waited = {}
        for op in ops:
            need = {}
            for d in op.deps:
                p = ops[d]
                if not self._needs_sync(p, op):
                    continue
                key = ("lane", p.lane) if p.dma else ("eng", p.eng)
                if need.get(key, 0) < p.cnt:
                    need[key] = p.cnt
                assert p.dma or p.cnt > 0 or any(
                    (not ops[d2].dma) and ops[d2].eng == p.eng and d2 > d for d2 in op.deps)
            wd = waited.setdefault(op.eng, {})
            op.waits = []
            for k, v in need.items():
                if wd.get(k, 0) < v:
                    op.waits.append((k, v))
                    wd[k] = v
        sems = {}
        for e in sorted(eng_cnt):
            sems[("eng", e)] = stack.enter_context(nc.semaphore("s_" + e))
        for i, l in enumerate(lane_cnt):
            sems[("lane", l)] = stack.enter_context(nc.semaphore("l_%d" % i))
        per = {}
        for op in ops:
            per.setdefault(op.eng, []).append(op)
        self.stats = {e: len(v) for e, v in per.items()}
        self.stats["sems"] = len(sems)

        def mk(name):
            lst = per.get(name, [])

            def body(e):
                for op in lst:
                    for k, v in op.waits:
                        e.wait_ge(sems[k], v)
                    if op.fn is None:
                        continue
                    ins = op.fn(e)
                    if op.dma:
                        ins.then_inc(sems[("lane", op.lane)], 16)
                    elif op.ms:
                        ins.then_inc(sems[("eng", op.eng)], 1)

            return body

        with nc.Block() as block:
            block.tensor(mk("pe"))
            block.scalar(mk("act"))
            block.vector(mk("dve"))
            block.gpsimd(mk("pool"))
            block.sync(mk("sp"))


def lambda_init(layer):
    return 0.8 - 0.6 * math.exp(-0.3 * layer)


DBG = {"ffn_stage": 99, "lite": False}


def build_nc(nstop=999):
    nc = bass.Bass("TRN2", target_bir_lowering=False)
    nl4 = 1 if DBG["lite"] else 4
    nl2 = 1 if DBG["lite"] else 2

    def din(name, shape):
        return nc.dram_tensor(name, list(shape), F32, kind="ExternalInput").ap()

    x_d = din("x", (SEQ, D))
    gnorm_d = din("g_norm", (4, 6, D))
    wg_d = din("w_ffn_gate", (nl4, nl2, D, DFF))
    wu_d = din("w_ffn_up", (nl4, nl2, D, DFF))
    wd_d = din("w_ffn_down", (nl4, nl2, DFF, D))
    wci_d = din("w_conv_in", (nl2, D, 3 * D))
    wc_d = din("w_conv", (2, 3, D))
    wco_d = din("w_conv_out", (nl2, D, D))
    gkv_d = din("g_kv", (D,))
    wkv_d = din("w_kv", (D, 2 * D))
    wq_d = din("w_q", (nl2, D, D))
    lq1_d = din("lambda_q1", (2, 64))
    lk1_d = din("lambda_k1", (2, 64))
    lq2_d = din("lambda_q2", (2, 64))
    lk2_d = din("lambda_k2", (2, 64))
    gsub_d = din("g_subln", (2, 128))
    wo_d = din("w_o", (nl2, D, D))
    cst_d = din("consts", (128, 256))
    out_d = nc.dram_tensor("out", [SEQ, D], F32, kind="ExternalOutput").ap()

    P = Prog()
    with ExitStack() as st:
        def sb(name, shape, dt):
            return st.enter_context(nc.sbuf_tensor(name, list(shape), dt))

        X = sb("X", (128, 8, TCH), F32)
        H = sb("H", (128, 8, TCH), BF16)
        Fr = sb("Fr", (128, 16, 512), F32)
        A = sb("A", (128, 11, TCH), BF16)
        KT = sb("KT", (128, 8, SEQ), BF16)
        V = sb("V", (128, 16, D), BF16)
        slots = [sb("slot%d" % i, (128, 6144), BF16) for i in range(2)]
        cst = sb("cst", (128, 256), F32)
        tri = sb("tri", (128, 128), BF16)
        ones = sb("ones", (128, 128), BF16)
        cols = sb("cols", (128, 256), F32)
        eps_t = sb("eps", (128, 1), F32)
        sq = [sb("sq%d" % i, (128, 512), BF16) for i in range(3)]
        sd = sb("sd", (128, 512), F32)
        rstd = [sb("rstd%d" % i, (128, 512), F32) for i in range(2)]
        sg = [sb("sg%d" % i, (128, 512), F32) for i in range(2)]
        halo = sb("halo", (128, 2, 8, 2), F32)
        small = sb("small", (128, 16), F32)
        PS = st.enter_context(nc.psum_tensor("PS", [128, 8, 512], F32))
        ident = cst[:, 0:128]

        def pst(b):
            return [("ps", b, 0), ("ps", b, 1)]

        def tbs(tb):
            return slice(tb * 512, (tb + 1) * 512)

        def mm(out, lhsT, rhs, start, stop, reads, writes):
            P.add("pe", lambda e: e.matmul(out, lhsT, rhs, start=start, stop=stop), reads, writes)

        ctr = {"slot": 0, "stat": 0, "sq": 0, "rstd": 0, "u": 0}

        def next_slot():
            s = ctr["slot"] % 2
            ctr["slot"] += 1
            return s

        def stoks(s, a, b):
            return [("slot", s, i) for i in range(a, b)]

        P.add("dve", lambda e: e.memset(ones[:], 1.0), writes=["ones"])
        P.add("dve", lambda e: e.memset(eps_t[:], EPS), writes=["eps"])
        P.add("dve", lambda e: e.memset(small[:, 12:13], -0.5), writes=["mhalf"])
        P.add("sp", lambda e: e.dma_start(out=cst[:], in_=cst_d), writes=["cst"], lane="cst")
        P.add("dve", lambda e: e.tensor_copy(out=tri[:], in_=cst[:, 128:256]), reads=["cst"], writes=["tri"])
        gview = gnorm_d.rearrange("l n (j p) -> (l n j) p", p=128)
        P.add("sp", lambda e: e.dma_start(out=Fr[0:128, 0, 0:128], in_=gview[0:128, :]),
              writes=[("F", 0)], lane="p0")
        P.add("sp", lambda e: e.dma_start(out=Fr[0:64, 1, 0:128], in_=gview[128:192, :]),
              writes=[("F", 1)], lane="p1")
        P.add("sp", lambda e: e.dma_start(out=Fr[64:72, 1, 0:128], in_=gkv_d.rearrange("(j p) -> j p", p=128)),
              writes=[("F", 1)], lane="p2")
        P.add("sp", lambda e: e.dma_start(out=Fr[72:120, 1, 0:128],
                                          in_=wc_d.rearrange("l w (j p) -> (l w j) p", p=128)),
              writes=[("F", 1)], lane="p3")
        P.add("sp", lambda e: e.dma_start(out=Fr[120:122, 1, 0:128], in_=gsub_d),
              writes=[("F", 1)], lane="p4")
        P.add("pe", lambda e: e.transpose(PS[:, 0, 0:128], Fr[0:128, 0, 0:128], ident),
              reads=[("F", 0), "cst"], writes=pst(0))
        P.add("pe", lambda e: e.transpose(PS[:, 0, 128:250], Fr[0:122, 1, 0:128], cst[0:122, 0:122]),
              reads=[("F", 1), "cst"], writes=pst(0))
        P.add("dve", lambda e: e.tensor_copy(out=cols[:, 0:250], in_=PS[:, 0, 0:250]), reads=pst(0), writes=["cols"])
        gv4 = cols[:, 0:192].rearrange("p (l n j) -> p l n j", l=4, n=6)
        for n in (1, 5):
            P.add("dve", (lambda e, n=n: e.tensor_scalar(out=gv4[:, :, n, :], in0=gv4[:, :, n, :], scalar1=0.5,
                                                         scalar2=None, op0=ALU.mult)),
                  reads=["cols"], writes=["cols"])
        lv = Fr[:, 2, :].rearrange("p (a b) -> p a b", a=4)
        for i, ld in enumerate((lq1_d, lk1_d, lq2_d, lk2_d)):
            P.add("sp", (lambda e, i=i, ld=ld: e.dma_start(
                out=lv[:, i, :], in_=ld.rearrange("a b -> (a b)").partition_broadcast(128))),
                writes=[("F", 2)], lane="l%d" % i)
        pr = Fr[:, 3, 0:256].rearrange("p (a b) -> p a b", a=2)
        P.add("dve", lambda e: e.tensor_tensor(out=pr[:, 0, :], in0=lv[:, 0, :], in1=lv[:, 1, :], op=ALU.mult),
              reads=[("F", 2)], writes=[("F", 3)])
        P.add("dve", lambda e: e.tensor_tensor(out=pr[:, 1, :], in0=lv[:, 2, :], in1=lv[:, 3, :], op=ALU.mult),
              reads=[("F", 2)], writes=[("F", 3)])
        P.add("dve", lambda e: e.reduce_sum(out=small[:, 0:4],
                                            in_=Fr[:, 3, 0:256].rearrange("p (a b) -> p a b", a=4), axis=AX.X),
              reads=[("F", 3)], writes=["small"])
        P.add("act", lambda e: e.activation(out=small[:, 4:8], in_=small[:, 0:4], func=AF.Exp),
              reads=["small"], writes=["small"])
        P.add("dve", lambda e: e.tensor_tensor(out=small[:, 8:10], in0=small[:, 6:8], in1=small[:, 4:6],
                                               op=ALU.subtract), reads=["small"], writes=["small"])
        for jl in range(2):
            li = lambda_init(jl + 2)
            P.add("dve", (lambda e, jl=jl, li=li: e.tensor_scalar(out=small[:, 8 + jl:9 + jl], in0=small[:, 8 + jl:9 + jl],
                                                                  scalar1=-li, scalar2=None, op0=ALU.add)),
                  reads=["small"], writes=["small"])
            P.add("dve", (lambda e, jl=jl, li=li: e.tensor_scalar(out=small[:, 10 + jl:11 + jl],
                                                                  in0=cols[:, 248 + jl:249 + jl],
                                                                  scalar1=1.0 - li, scalar2=None, op0=ALU.mult)),
                  reads=["small", "cols"], writes=["small"])

        def gidx(l, n):
            return (l * 6 + n) * 8

        def rstd_of(src_fn, src_tok, ncols, invd, nj, lnexp=True):
            b = 6 + ctr["stat"] % 2
            ctr["stat"] += 1
            ssb = PS[:, b, 0:ncols]
            for j in range(nj):
                r = ctr["sq"] % 3
                ctr["sq"] += 1
                P.add("act", (lambda e, j=j, r=r: e.activation(out=sq[r][:, 0:ncols], in_=src_fn(j), func=AF.Square)),
                      reads=[src_tok(j)], writes=[("sq", r)])
                mm(ssb, ones[:], sq[r][:, 0:ncols], j == 0, j == nj - 1, [("sq", r), "ones"], pst(b))
            rb = ctr["rstd"] % 2
            ctr["rstd"] += 1
            if lnexp:
                P.add("act", lambda e: e.activation(out=sd[:, 0:ncols], in_=ssb, func=AF.Ln, bias=eps_t[:, 0:1],
                                                    scale=invd), reads=pst(b) + ["eps"], writes=["sd"])
                P.add("act", lambda e: e.activation(out=rstd[rb][:, 0:ncols], in_=sd[:, 0:ncols], func=AF.Exp,
                                                    scale=-0.5), reads=["sd"], writes=[("rstd", rb)])
            else:
                P.add("act", lambda e: e.activation(out=sd[:, 0:ncols], in_=ssb, func=AF.Sqrt, bias=eps_t[:, 0:1],
                                                    scale=invd), reads=pst(b) + ["eps"], writes=["sd"])
                P.add("dve", lambda e: e.reciprocal(out=rstd[rb][:, 0:ncols], in_=sd[:, 0:ncols]),
                      reads=["sd"], writes=[("rstd", rb)])
            return rstd[rb][:, 0:ncols], ("rstd", rb)

        def prenorm(gbase, tb):
            rs, rtok = rstd_of(lambda j: X[:, j, tbs(tb)], lambda j: ("X", j, tb), 512, 1.0 / D, 8)
            for j in range(8):
                P.add("dve", (lambda e, j=j: e.scalar_tensor_tensor(
                    out=H[:, j, tbs(tb)], in0=X[:, j, tbs(tb)], scalar=cols[:, gbase + j:gbase + j + 1], in1=rs,
                    op0=ALU.mult, op1=ALU.mult)),
                    reads=[("X", j, tb), rtok, "cols"], writes=[("H", j, tb)])

        def post_stats(tb):
            return rstd_of(lambda j: Fr[:, 2 * j + tb, :], lambda j: ("F", 2 * j + tb), 512, 1.0 / D, 8)

        def post_apply(gbase, tb, rs, rtok, js=range(8)):
            for j in js:
                blk = 2 * j + tb
                P.add("dve", (lambda e, j=j, blk=blk: e.scalar_tensor_tensor(
                    out=Fr[:, blk, :], in0=Fr[:, blk, :], scalar=cols[:, gbase + j:gbase + j + 1], in1=rs,
                    op0=ALU.mult, op1=ALU.mult)),
                    reads=[("F", blk), rtok, "cols"], writes=[("F", blk)])
                P.add("dve", (lambda e, j=j, blk=blk: e.tensor_tensor(
                    out=X[:, j, tbs(tb)], in0=X[:, j, tbs(tb)], in1=Fr[:, blk, :], op=ALU.add)),
                    reads=[("F", blk), ("X", j, tb)], writes=[("X", j, tb)])

        def wview(w2d):
            return w2d.rearrange("(k p) f -> p k f", p=128)

        def proj(Wv, c0, src, srctok, evac, hook=None):
            pcs = []
            for mh in range(2):
                s = next_slot()
                wv = slots[s][:, 0:4096].rearrange("p (k f) -> p k f", k=8)
                P.add("pool", (lambda e, wv=wv, mh=mh: e.dma_start(out=wv, in_=Wv[:, :, c0 + mh * 512:c0 + (mh + 1) * 512])),
                      writes=stoks(s, 0, 4), lane=("w", s, 0))
                pcs.append((s, wv))
            for tb in range(2):
                for m in range(8):
                    s, wv = pcs[m // 4]
                    mi = m % 4
                    bank = ctr["u"] % 2
                    ctr["u"] += 1
                    Pp = PS[:, bank, :]
                    for k in range(8):
                        mm(Pp, wv[:, k, mi * 128:(mi + 1) * 128], src[:, k, tbs(tb)], k == 0, k == 7,
                           stoks(s, 0, 4) + [(srctok, k, tb)], pst(bank))
                    evac(m, tb, Pp, bank)
                    if hook is not None and tb == 1:
                        hook(m)

        def evac_to_F(m, tb, Pp, bank):
            blk = 2 * m + tb
            P.add("act", lambda e: e.copy(out=Fr[:, blk, :], in_=Pp), reads=pst(bank), writes=[("F", blk)])

        def ffn(l, w, hook=None):
            Wg = wview(wg_d[l, w])
            Wu = wview(wu_d[l, w])
            Wd = wd_d[l, w].rearrange("(f p) m -> p f m", p=128)
            for hf in range(2):
                for (f0, n) in ((0, 3), (3, 3), (6, 3), (9, 2)):
                    s = next_slot()
                    gv = slots[s][:, 0:8 * n * 128].rearrange("p (k f) -> p k f", k=8)
                    uv = slots[s][:, 3072:3072 + 8 * n * 128].rearrange("p (k f) -> p k f", k=8)
                    fa = hf * 11 + f0
                    P.add("pool", (lambda e, gv=gv, fa=fa, n=n: e.dma_start(out=gv, in_=Wg[:, :, fa * 128:(fa + n) * 128])),
                          writes=stoks(s, 0, 3), lane=("w", s, 0))
                    P.add("pool", (lambda e, uv=uv, fa=fa, n=n: e.dma_start(out=uv, in_=Wu[:, :, fa * 128:(fa + n) * 128])),
                          writes=stoks(s, 3, 6), lane=("w", s, 3))
                    order = ([(fi, tb) for tb in range(2) for fi in range(n)] if (hf == 0 and f0 == 0)
                             else [(fi, tb) for fi in range(n) for tb in range(2)])
                    for (fi, tb) in order:
                        fl = f0 + fi
                        if True:
                            par = ctr["u"] % 2
                            ctr["u"] += 1
                            G = PS[:, 2 * par, :]
                            U = PS[:, 2 * par + 1, :]
                            for k in range(8):
                                mm(G, gv[:, k, fi * 128:(fi + 1) * 128], H[:, k, tbs(tb)], k == 0, k == 7,
                                   stoks(s, 0, 3) + [("H", k, tb)], pst(2 * par))
                            for k in range(8):
                                mm(U, uv[:, k, fi * 128:(fi + 1) * 128], H[:, k, tbs(tb)], k == 0, k == 7,
                                   stoks(s, 3, 6) + [("H", k, tb)], pst(2 * par + 1))
                            P.add("act", (lambda e, G=G, par=par: e.activation(out=sg[par][:], in_=G, func=AF.Silu)),
                                  reads=pst(2 * par), writes=[("sg", par)])
                            P.add("dve", (lambda e, U=U, par=par, fl=fl, tb=tb: e.tensor_tensor(
                                out=A[:, fl, tbs(tb)], in0=U, in1=sg[par][:], op=ALU.mult)),
                                reads=pst(2 * par + 1) + [("sg", par)], writes=[("A", fl, tb)])
                pieces = []
                for m0 in (0, 4):
                    s = next_slot()
                    dv = slots[s][:, 0:5632].rearrange("p (f m) -> p f m", f=11)
                    P.add("pool", (lambda e, dv=dv, hf=hf, m0=m0: e.dma_start(
                        out=dv, in_=Wd[:, hf * 11:(hf + 1) * 11, m0 * 128:(m0 + 4) * 128])),
                        writes=stoks(s, 0, 6), lane=("w", s, 0))
                    pieces.append((s, dv))
                if hf == 0:
                    dorder = [(m, tb) for m in range(8) for tb in range(2)]
                else:
                    dorder = [(m, tb) for tb in range(2) for m in range(8)]
                for (m, tb) in dorder:
                    s, dv = pieces[m // 4]
                    mi = m % 4
                    par = ctr["u"] % 2
                    ctr["u"] += 1
                    Dp = PS[:, 4 + par, :]
                    for fl in range(11):
                        mm(Dp, dv[:, fl, mi * 128:(mi + 1) * 128], A[:, fl, tbs(tb)], fl == 0, fl == 10,
                           stoks(s, 0, 6) + [("A", fl, tb)], pst(4 + par))
                    blk = 2 * m + tb
                    if hf == 0:
                        P.add("act", (lambda e, Dp=Dp, blk=blk: e.copy(out=Fr[:, blk, :], in_=Dp)),
                              reads=pst(4 + par), writes=[("F", blk)])
                    else:
                        P.add("dve", (lambda e, Dp=Dp, blk=blk: e.tensor_tensor(
                            out=Fr[:, blk, :], in0=Dp, in1=Fr[:, blk, :], op=ALU.add)),
                            reads=pst(4 + par) + [("F", blk)], writes=[("F", blk)])
                    if hook is not None and hf == 1 and tb == 1:
                        hook(m)

        def conv_mixer(l, c, hook=None):
            Win = wview(wci_d[l])
            for jp in range(4):
                s = next_slot()
                parts = [slots[s][:, pt * 2048:(pt + 1) * 2048].rearrange("p (k f) -> p k f", k=8) for pt in range(3)]
                for pt in range(3):
                    P.add("pool", (lambda e, pt=pt, jp=jp, parts=parts: e.dma_start(
                        out=parts[pt], in_=Win[:, :, pt * 1024 + jp * 256:pt * 1024 + (jp + 1) * 256])),
                        writes=stoks(s, 2 * pt, 2 * pt + 2), lane=("w", s, 2 * pt))
                cus = []
                for jj in range(2):
                    j = 2 * jp + jj
                    cub = j % 2
                    cu = Fr[:, cub * 3:cub * 3 + 3, :].rearrange("p a b -> p (a b)")
                    cutoks = [("F", cub * 3 + i) for i in range(3)]
                    if c == 0:
                        P.add("dve", (lambda e, cu=cu: e.memset(cu[:, 0:2], 0.0)), writes=cutoks)
                    else:
                        P.add("dve", (lambda e, cu=cu, j=j: e.tensor_copy(out=cu[:, 0:2], in_=halo[:, l, j, :])),
                              reads=[("halo", l, j)], writes=cutoks)
                    cus.append((cu, cutoks))
                for tb in range(2):
                    for jj in range(2):
                        j = 2 * jp + jj
                        cu, cutoks = cus[jj]
                        w0 = 200 + (l * 3 + 0) * 8 + j
                        w1 = 200 + (l * 3 + 1) * 8 + j
                        w2 = 200 + (l * 3 + 2) * 8 + j
                        par = ctr["u"] % 2
                        ctr["u"] += 1
                        Pb, Pc, Pu = (PS[:, 3 * par + i, :] for i in range(3))
                        for pt in range(3):
                            for k in range(8):
                                mm(PS[:, 3 * par + pt, :], parts[pt][:, k, jj * 128:(jj + 1) * 128], H[:, k, tbs(tb)],
                                   k == 0, k == 7, stoks(s, 2 * pt, 2 * pt + 2) + [("H", k, tb)], pst(3 * par + pt))
                        ucp = Fr[:, 6 + par, :]
                        z = Fr[:, 8 + par, :]
                        o2 = 2 + tb * 512
                        P.add("act", (lambda e, ucp=ucp, Pu=Pu: e.copy(out=ucp, in_=Pu)),
                              reads=pst(3 * par + 2), writes=[("F", 6 + par)])
                        P.add("dve", (lambda e, cu=cu, o2=o2, Pc=Pc, ucp=ucp: e.tensor_tensor(
                            out=cu[:, o2:o2 + 512], in0=Pc, in1=ucp, op=ALU.mult)),
                            reads=pst(3 * par + 1) + [("F", 6 + par)], writes=cutoks)
                        P.add("act", (lambda e, z=z, cu=cu, o2=o2, w2=w2: e.mul(out=z, in_=cu[:, o2:o2 + 512],
                                                                               mul=cols[:, w2:w2 + 1])),
                              reads=cutoks + ["cols"], writes=[("F", 8 + par)])
                        P.add("dve", (lambda e, z=z, cu=cu, o2=o2, w1=w1: e.scalar_tensor_tensor(
                            out=z, in0=cu[:, o2 - 1:o2 + 511], scalar=cols[:, w1:w1 + 1], in1=z,
                            op0=ALU.mult, op1=ALU.add)), reads=cutoks + ["cols", ("F", 8 + par)],
                            writes=[("F", 8 + par)])
                        P.add("dve", (lambda e, z=z, cu=cu, o2=o2, w0=w0: e.scalar_tensor_tensor(
                            out=z, in0=cu[:, o2 - 2:o2 + 510], scalar=cols[:, w0:w0 + 1], in1=z,
                            op0=ALU.mult, op1=ALU.add)), reads=cutoks + ["cols", ("F", 8 + par)],
                            writes=[("F", 8 + par)])
                        P.add("dve", (lambda e, z=z, Pb=Pb, j=j, tb=tb: e.tensor_tensor(
                            out=A[:, j, tbs(tb)], in0=Pb, in1=z, op=ALU.mult)),
                            reads=pst(3 * par) + [("F", 8 + par)], writes=[("A", j, tb)])
                if c == 0:
                    for jj in range(2):
                        j = 2 * jp + jj
                        cu, cutoks = cus[jj]
                        P.add("dve", (lambda e, cu=cu, j=j: e.tensor_copy(out=halo[:, l, j, :], in_=cu[:, 1024:1026])),
                              reads=cutoks, writes=[("halo", l, j)])
            proj(wview(wco_d[l]), 0, A, "A", evac_to_F, hook)

        def kv_proj(c):
            Wkv = wview(wkv_d)

            def evac_k(m, tb, Pp, bank):
                P.add("act", lambda e: e.copy(out=KT[:, m, c * TCH + tb * 512:c * TCH + (tb + 1) * 512], in_=Pp),
                      reads=pst(bank), writes=[("K", m, c)])

            proj(Wkv, 0, H, "H", evac_k)
            for eh in range(2):
                s = next_slot()
                wv = slots[s][:, 0:4096].rearrange("p (k f) -> p k f", k=8)
                P.add("pool", (lambda e, wv=wv, eh=eh: e.dma_start(out=wv, in_=Wkv[:, :, D + eh * 512:D + (eh + 1) * 512])),
                      writes=stoks(s, 0, 4), lane=("w", s, 0))
                for tt in range(8):
                    bank = ctr["u"] % 2
                    ctr["u"] += 1
                    Pp = PS[:, bank, :]
                    for k in range(8):
                        mm(Pp, H[:, k, tt * 128:(tt + 1) * 128], wv[:, k, :], k == 0, k == 7,
                           stoks(s, 0, 4) + [("H", k, tt // 4)], pst(bank))
                    kt = c * 8 + tt
                    if tt % 2 == 0:
                        P.add("act", (lambda e, kt=kt, eh=eh, Pp=Pp: e.copy(out=V[:, kt, eh * 512:(eh + 1) * 512], in_=Pp)),
                              reads=pst(bank), writes=[("V", kt)])
                    else:
                        P.add("dve", (lambda e, kt=kt, eh=eh, Pp=Pp: e.tensor_copy(out=V[:, kt, eh * 512:(eh + 1) * 512], in_=Pp)),
                              reads=pst(bank), writes=[("V", kt)])

        def attn(l, c, hook=None):
            jl = l - 2

            def evac_q(m, tb, Pp, bank):
                P.add("act", lambda e: e.mul(out=A[:, m, tbs(tb)], in_=Pp, mul=0.125),
                      reads=pst(bank), writes=[("A", m, tb)])

            proj(wview(wq_d[jl]), 0, H, "H", evac_q)

            steps = []
            for hd in range(8):
                for u in range(4 * c, 4 * c + 4):
                    for kt in range(2 * u + 2):
                        steps.append((hd, u, kt))
            Eb = [Fr[:, r, :].bitcast(BF16)[:, 0:512].rearrange("p (c q) -> p c q", c=2) for r in range(3)]
            neglam = small[:, 8 + jl:9 + jl]
            gs = small[:, 10 + jl:11 + jl]
            tri_b = tri[:].unsqueeze(1).to_broadcast([128, 2, 128])

            def geom(i):
                hd, u, kt = steps[i]
                d = kt - 2 * u
                q0 = 128 if d == 1 else 0
                ql = (u - 4 * c) * 256
                return hd, u, kt, d, q0, ql

            it_of = []
            _it = 0
            for (hd_, u_, kt_) in steps:
                it_of.append(_it)
                if kt_ == 2 * u_ + 1:
                    _it += 1
            qz = [Fr[:, 9 + p_, :].bitcast(BF16)[:, 0:512].rearrange("p (c q) -> p c q", c=2) for p_ in range(2)]
            for p_ in range(2):
                P.add("dve", (lambda e, p_=p_: e.memset(Fr[:, 9 + p_, :].bitcast(BF16)[:, 0:512], 0.0)),
                      writes=[("F", 9 + p_)])

            def emit_S(i):
                hd, u, kt, d, q0, ql = geom(i)
                par = i % 2
                ipar = it_of[i] % 2
                if kt == 0:
                    P.add("dve", lambda e: e.tensor_copy(out=qz[ipar][0:64, 0, :], in_=A[0:64, hd, ql:ql + 256]),
                          reads=[("A", hd, ql // 512)], writes=[("F", 9 + ipar)])
                    P.add("dve", lambda e: e.tensor_copy(out=qz[ipar][64:128, 1, :], in_=A[64:128, hd, ql:ql + 256]),
                          reads=[("A", hd, ql // 512)], writes=[("F", 9 + ipar)])
                for cm in range(2):
                    mm(PS[:, par, cm * 256 + q0:(cm + 1) * 256],
                       KT[:, hd, kt * 128:(kt + 1) * 128],
                       qz[ipar][:, cm, q0:256], True, True,
                       [("K", hd, kt // 8), ("F", 9 + ipar)], pst(par))

            def emit_exp(i):
                hd, u, kt, d, q0, ql = geom(i)
                par = i % 2
                r = i % 3
                P.add("act", lambda e: e.activation(
                    out=Eb[r][:, :, q0:256],
                    in_=PS[:, par, :].rearrange("p (c q) -> p c q", c=2)[:, :, q0:256], func=AF.Exp),
                    reads=pst(par), writes=[("F", r)])
                if d >= 0:
                    P.add("dve", lambda e: e.tensor_tensor(out=Eb[r][:, :, q0:q0 + 128], in0=Eb[r][:, :, q0:q0 + 128],
                                                           in1=tri_b, op=ALU.mult),
                          reads=[("F", r), "tri"], writes=[("F", r)])

            pending = []

            def emit_PV(i, it):
                hd, u, kt, d, q0, ql = geom(i)
                r = i % 3
                ipar = it % 2
                bA = 2 + ipar
                bS = 4 + ipar
                Ab = PS[:, bA, :].rearrange("p (c q) -> p c q", c=2)
                Sb = PS[:, bS, :].rearrange("p (c q) -> p c q", c=2)
                first = kt == 0
                last = kt == 2 * u + 1
                if q0 == 0:
                    E2 = Fr[:, r, :].bitcast(BF16)[:, 0:512]
                    mm(PS[:, bA, :], V[:, kt, hd * 128:(hd + 1) * 128], E2, first, last,
                       [("V", kt), ("F", r)], pst(bA))
                    mm(PS[:, bS, :], ones[:], E2, first, last, [("F", r), "ones"], pst(bS))
                else:
                    for cm in range(2):
                        mm(Ab[:, cm, q0:256], V[:, kt, hd * 128:(hd + 1) * 128], Eb[r][:, cm, q0:256], first,
                           last and cm == 1, [("V", kt), ("F", r)], pst(bA))
                    for cm in range(2):
                        mm(Sb[:, cm, q0:256], ones[:], Eb[r][:, cm, q0:256], first, last and cm == 1,
                           [("F", r), "ones"], pst(bS))
                if last:
                    b0 = 3 + 3 * ipar
                    rr = Fr[:, b0, :]
                    tt_ = Fr[:, b0 + 1, :]
                    ob = Fr[:, b0 + 2, 0:256]
                    P.add("dve", lambda e: e.reciprocal(out=rr, in_=PS[:, bS, :]),
                          reads=pst(bS), writes=[("F", b0)])
                    P.add("dve", lambda e: e.tensor_tensor(out=tt_, in0=PS[:, bA, :], in1=rr, op=ALU.mult),
                          reads=pst(bA) + [("F", b0)], writes=[("F", b0 + 1)])
                    P.add("dve", lambda e: e.scalar_tensor_tensor(out=H[:, hd, ql:ql + 256], in0=tt_[:, 256:512],
                                                                  scalar=neglam, in1=tt_[:, 0:256],
                                                                  op0=ALU.mult, op1=ALU.add),
                          reads=[("F", b0 + 1), "small"], writes=[("H", hd, ql // 512)])

            n = len(steps)
            emit_S(0)
            it = 0
            for i in range(n):
                if i + 1 < n:
                    emit_S(i + 1)
                emit_exp(i)
                emit_PV(i, it)
                if steps[i][2] == 2 * steps[i][1] + 1:
                    it += 1
                for pnd in list(pending):
                    pnd[0] -= 1
                    if pnd[0] <= 0:
                        pending.remove(pnd)
                        pnd[1]()
            for pnd in pending:
                pnd[1]()
            for tb in range(2):
                for hd in range(8):
                    rs, rtok = rstd_of(lambda j, hd=hd, tb=tb: H[:, hd, tbs(tb)], lambda j, hd=hd, tb=tb: ("H", hd, tb),
                                       512, 1.0 / 128.0, 1, lnexp=False)
                    P.add("dve", (lambda e, hd=hd, tb=tb, rs=rs: e.scalar_tensor_tensor(
                        out=H[:, hd, tbs(tb)], in0=H[:, hd, tbs(tb)], scalar=gs, in1=rs,
                        op0=ALU.mult, op1=ALU.mult)),
                        reads=[("H", hd, tb), rtok, "small"], writes=[("H", hd, tb)])
            proj(wview(wo_d[jl]), 0, H, "H", evac_to_F, hook)

        def load_x(c):
            for tt in range(8):
                P.add("sp", (lambda e, tt=tt: e.dma_start(
                    out=Fr[:, 2 * tt:2 * tt + 2, :].rearrange("p a b -> p (a b)"),
                    in_=x_d[c * TCH + tt * 128:c * TCH + (tt + 1) * 128, :])),
                    writes=[("F", 2 * tt), ("F", 2 * tt + 1)], lane=("x", tt % 4))
            for j in range(8):
                for g in range(2):
                    bank = ctr["u"] % 4
                    ctr["u"] += 1
                    for ti in range(4):
                        tt = g * 4 + ti
                        xs = Fr[:, 2 * tt:2 * tt + 2, :].rearrange("p a b -> p (a b)")
                        P.add("pe", (lambda e, bank=bank, ti=ti, xs=xs, j=j: e.transpose(
                            PS[:, bank, ti * 128:(ti + 1) * 128], xs[:, j * 128:(j + 1) * 128], ident)),
                            reads=[("F", 2 * tt), ("F", 2 * tt + 1), "cst"], writes=pst(bank))
                    if (j + g) % 2 == 0:
                        P.add("act", (lambda e, bank=bank, j=j, g=g: e.copy(out=X[:, j, tbs(g)], in_=PS[:, bank, :])),
                              reads=pst(bank), writes=[("X", j, g)])
                    else:
                        P.add("dve", (lambda e, bank=bank, j=j, g=g: e.tensor_copy(out=X[:, j, tbs(g)], in_=PS[:, bank, :])),
                              reads=pst(bank), writes=[("X", j, g)])

        def store_x(c):
            for tt in range(8):
                for g in range(2):
                    bank = ctr["u"] % 4
                    ctr["u"] += 1
                    for ji in range(4):
                        j = g * 4 + ji
                        P.add("pe", (lambda e, bank=bank, ji=ji, j=j, tt=tt: e.transpose(
                            PS[:, bank, ji * 128:(ji + 1) * 128], X[:, j, tt * 128:(tt + 1) * 128], ident)),
                            reads=[("X", j, tt // 4), "cst"], writes=pst(bank))
                    blk = 2 * tt + g
                    if g == 0:
                        P.add("act", (lambda e, bank=bank, blk=blk: e.copy(out=Fr[:, blk, :], in_=PS[:, bank, :])),
                              reads=pst(bank), writes=[("F", blk)])
                    else:
                        P.add("dve", (lambda e, bank=bank, blk=blk: e.tensor_copy(out=Fr[:, blk, :], in_=PS[:, bank, :])),
                              reads=pst(bank), writes=[("F", blk)])
                P.add("sp", (lambda e, tt=tt: e.dma_start(
                    out=out_d[c * TCH + tt * 128:c * TCH + (tt + 1) * 128, :],
                    in_=Fr[:, 2 * tt:2 * tt + 2, :].rearrange("p a b -> p (a b)"))),
                    reads=[("F", 2 * tt), ("F", 2 * tt + 1)], writes=[("OUT", c, tt)], lane=("o", tt % 4))

        for c in range(NCH):
            load_x(c)
            subs = []
            for l in range(4):
                subs.append((lambda hook, l=l: ffn(l, 0, hook), gidx(l, 0), gidx(l, 1)))
                if l < 2:
                    subs.append((lambda hook, l=l, c=c: conv_mixer(l, c, hook), gidx(l, 2), gidx(l, 3)))
                else:
                    subs.append((lambda hook, l=l, c=c: attn(l, c, hook), gidx(l, 2), gidx(l, 3)))
                subs.append((lambda hook, l=l: ffn(l, 1, hook), gidx(l, 4), gidx(l, 5)))
                if l == 1:
                    subs.append((lambda hook, c=c: kv_proj(c), 192, None))
            nrun = nstop if nstop < 6 else (nstop if nstop > 6 else 6)
            if nstop > 6:
                nrun = nstop + 1
            subs = subs[:min(len(subs), nrun)]
            if subs:
                for tb in range(2):
                    prenorm(subs[0][1], tb)
            for i, (fn, gpre, gpost) in enumerate(subs):
                stash = {}

                def hook(m, stash=stash, gpost=gpost):
                    if m == 1:
                        stash["s0"] = post_stats(0)
                    elif m >= 2:
                        js = {2: (0, 1), 3: (2, 3), 4: (4,), 5: (5,), 6: (6,), 7: (7,)}[m]
                        post_apply(gpost, 0, stash["s0"][0], stash["s0"][1], js)
                        stash["done0"] = True

                fn(hook if gpost is not None else None)
                nxt = subs[i + 1][1] if i + 1 < len(subs) else None
                for tb in range(2):
                    if gpost is not None and not (tb == 0 and stash.get("done0")):
                        rs, rtok = stash["s0"] if (tb == 0 and "s0" in stash) else post_stats(tb)
                        post_apply(gpost, tb, rs, rtok)
                    if nxt is not None:
                        prenorm(nxt, tb)
            store_x(c)
        P.add("sp", None, reads=[("OUT", c, tt) for c in range(NCH) for tt in range(8)])
        P.emit(nc, st)
    nc._prog_stats = P.stats
    return nc


_CACHE = {}


def _consts():
    c = np.zeros((128, 256), np.float32)
    c[:, 0:128] = np.eye(128, dtype=np.float32)
    c[:, 128:256] = np.triu(np.ones((128, 128), np.float32))
    return c


def kernel(**inputs):
    nstop = int(inputs.pop("_nstop", 999))
    ncores = int(inputs.pop("_ncores", NB))
    if nstop not in _CACHE:
        _CACHE[nstop] = build_nc(nstop)
    nc = _CACHE[nstop]
    names = ["g_norm", "w_ffn_gate", "w_ffn_up", "w_ffn_down", "w_conv_in", "w_conv", "w_conv_out", "g_kv",
             "w_kv", "w_q", "lambda_q1", "lambda_k1", "lambda_q2", "lambda_k2", "g_subln", "w_o"]
    shared = {k: np.asarray(inputs[k], dtype=np.float32) for k in names}
    if DBG["lite"]:
        for k in ("w_ffn_gate", "w_ffn_up", "w_ffn_down"):
            shared[k] = shared[k][:1, :1]
        for k in ("w_conv_in", "w_conv_out", "w_q", "w_o"):
            shared[k] = shared[k][:1]
    shared = {k: np.ascontiguousarray(v) for k, v in shared.items()}
    shared["consts"] = _consts()
    x = np.asarray(inputs["x"], dtype=np.float32)
    in_maps = []
    for b in range(ncores):
        m = dict(shared)
        m["x"] = np.ascontiguousarray(x[b])
        in_maps.append(m)
    res = run_bass_kernel_spmd(nc, in_maps, core_ids=list(range(ncores)))
    out = np.stack([np.asarray(res.results[b]["out"], dtype=np.float32) for b in range(ncores)], axis=0)
    return out
```

```python
import math
from contextlib import ExitStack

import numpy as np
import concourse.bass as bass
import concourse.mybir as mybir
from concourse.bass_utils import run_bass_kernel_spmd

F32 = mybir.dt.float32
BF16 = mybir.dt.bfloat16
AF = mybir.ActivationFunctionType
ALU = mybir.AluOpType
AX = mybir.AxisListType

D = 1024
SEQ = 2048
NB = 8
DFF = 2816
EPS = 1e-6
TCH = 1024
NCH = SEQ // TCH
SAME_ENG_SYNC = True


class _Op:
    __slots__ = ("eng", "fn", "dma", "lane", "deps", "idx", "ms", "cnt", "waits")


class Prog:
    def __init__(self):
        self.ops = []
        self.last_w = {}
        self.readers = {}
        self.lane_last = {}

    def add(self, eng, fn, reads=(), writes=(), lane=None):
        op = _Op()
        op.eng = eng
        op.fn = fn
        op.dma = lane is not None
        op.lane = lane
        op.idx = len(self.ops)
        op.ms = False
        op.cnt = 0
        deps = set()
        for t in reads:
            w = self.last_w.get(t)
            if w is not None:
                deps.add(w)
        for t in writes:
            w = self.last_w.get(t)
            if w is not None:
                deps.add(w)
            rs = self.readers.get(t)
            if rs:
                deps.update(rs)
        if lane is not None:
            p = self.lane_last.get(lane)
            if p is not None:
                deps.add(p)
            self.lane_last[lane] = op.idx
        for t in reads:
            self.readers.setdefault(t, []).append(op.idx)
        for t in writes:
            self.last_w[t] = op.idx
            self.readers[t] = []
        deps.discard(op.idx)
        op.deps = deps
        self.ops.append(op)
        return op

    @staticmethod
    def _needs_sync(p, op):
        if p.dma:
            return True
        if p.eng != op.eng:
            return True
        if p.eng == "pe":
            return False
        return SAME_ENG_SYNC

    def emit(self, nc, stack):
        ops = self.ops
        for op in ops:
            best = {}
            for d in op.deps:
                p = ops[d]
                if (not p.dma) and self._needs_sync(p, op):
                    if best.get(p.eng, -1) < d:
                        best[p.eng] = d
            for d in best.values():
                ops[d].ms = True
        eng_cnt = {}
        lane_cnt = {}
        for op in ops:
            if op.dma:
                lane_cnt[op.lane] = lane_cnt.get(op.lane, 0) + 1
                op.cnt = 16 * lane_cnt[op.lane]
            elif op.ms:
                eng_cnt[op.eng] = eng_cnt.get(op.eng, 0) + 1
                op.cnt = eng_cnt[op.eng]
        waited = {}
        for op in ops:
            need = {}
            for d in op.deps:
                p = ops[d]
                if not self._needs_sync(p, op):
                    continue
                key = ("lane", p.lane) if p.dma else ("eng", p.eng)
                if need.get(key, 0) < p.cnt:
                    need[key] = p.cnt
                assert p.dma or p.cnt > 0 or any(
                    (not ops[d2].dma) and ops[d2].eng == p.eng and d2 > d for d2 in op.deps)
            wd = waited.setdefault(op.eng, {})
            op.waits = []
            for k, v in need.items():
                if wd.get(k, 0) < v:
                    op.waits.append((k, v))
                    wd[k] = v
        sems = {}
        for e in sorted(eng_cnt):
            sems[("eng", e)] = stack.enter_context(nc.semaphore("s_" + e))
        for i, l in enumerate(lane_cnt):
            sems[("lane", l)] = stack.enter_context(nc.semaphore("l_%d" % i))
        per = {}
        for op in ops:
            per.setdefault(op.eng, []).append(op)
        self.stats = {e: len(v) for e, v in per.items()}
        self.stats["sems"] = len(sems)

        def mk(name):
            lst = per.get(name, [])

            def body(e):
                for op in lst:
                    for k, v in op.waits:
                        e.wait_ge(sems[k], v)
                    if op.fn is None:
                        continue
                    ins = op.fn(e)
                    if op.dma:
                        ins.then_inc(sems[("lane", op.lane)], 16)
                    elif op.ms:
                        ins.then_inc(sems[("eng", op.eng)], 1)

            return body

        with nc.Block() as block:
            block.tensor(mk("pe"))
            block.scalar(mk("act"))
            block.vector(mk("dve"))
            block.gpsimd(mk("pool"))
            block.sync(mk("sp"))


def lambda_init(layer):
    return 0.8 - 0.6 * math.exp(-0.3 * layer)


DBG = {"ffn_stage": 99, "lite": False}


def build_nc(nstop=999):
    nc = bass.Bass("TRN2", target_bir_lowering=False)
    nl4 = 1 if DBG["lite"] else 4
    nl2 = 1 if DBG["lite"] else 2

    def din(name, shape):
        return nc.dram_tensor(name, list(shape), F32, kind="ExternalInput").ap()

    x_d = din("x", (SEQ, D))
    gnorm_d = din("g_norm", (4, 6, D))
    wg_d = din("w_ffn_gate", (nl4, nl2, D, DFF))
    wu_d = din("w_ffn_up", (nl4, nl2, D, DFF))
    wd_d = din("w_ffn_down", (nl4, nl2, DFF, D))
    wci_d = din("w_conv_in", (nl2, D, 3 * D))
    wc_d = din("w_conv", (2, 3, D))
    wco_d = din("w_conv_out", (nl2, D, D))
    gkv_d = din("g_kv", (D,))
    wkv_d = din("w_kv", (D, 2 * D))
    wq_d = din("w_q", (nl2, D, D))
    lq1_d = din("lambda_q1", (2, 64))
    lk1_d = din("lambda_k1", (2, 64))
    lq2_d = din("lambda_q2", (2, 64))
    lk2_d = din("lambda_k2", (2, 64))
    gsub_d = din("g_subln", (2, 128))
    wo_d = din("w_o", (nl2, D, D))
    cst_d = din("consts", (128, 256))
    out_d = nc.dram_tensor("out", [SEQ, D], F32, kind="ExternalOutput").ap()

    P = Prog()
    with ExitStack() as st:
        def sb(name, shape, dt):
            return st.enter_context(nc.sbuf_tensor(name, list(shape), dt))

        X = sb("X", (128, 8, TCH), F32)
        H = sb("H", (128, 8, TCH), BF16)
        Fr = sb("Fr", (128, 16, 512), F32)
        A = sb("A", (128, 11, TCH), BF16)
        KT = sb("KT", (128, 8, SEQ), BF16)
        V = sb("V", (128, 16, D), BF16)
        slots = [sb("slot%d" % i, (128, 6144), BF16) for i in range(2)]
        cst = sb("cst", (128, 256), F32)
        tri = sb("tri", (128, 128), BF16)
        ones = sb("ones", (128, 128), BF16)
        cols = sb("cols", (128, 256), F32)
        eps_t = sb("eps", (128, 1), F32)
        sq = [sb("sq%d" % i, (128, 512), BF16) for i in range(3)]
        sd = sb("sd", (128, 512), F32)
        rstd = [sb("rstd%d" % i, (128, 512), F32) for i in range(2)]
        sg = [sb("sg%d" % i, (128, 512), F32) for i in range(2)]
        halo = sb("halo", (128, 2, 8, 2), F32)
        small = sb("small", (128, 16), F32)
        PS = st.enter_context(nc.psum_tensor("PS", [128, 8, 512], F32))
        ident = cst[:, 0:128]

        def pst(b):
            return [("ps", b, 0), ("ps", b, 1)]

        def tbs(tb):
            return slice(tb * 512, (tb + 1) * 512)

        def mm(out, lhsT, rhs, start, stop, reads, writes):
            P.add("pe", lambda e: e.matmul(out, lhsT, rhs, start=start, stop=stop), reads, writes)

        ctr = {"slot": 0, "stat": 0, "sq": 0, "rstd": 0, "u": 0}

        def next_slot():
            s = ctr["slot"] % 2
            ctr["slot"] += 1
            return s

        def stoks(s, a, b):
            return [("slot", s, i) for i in range(a, b)]

        P.add("dve", lambda e: e.memset(ones[:], 1.0), writes=["ones"])
        P.add("dve", lambda e: e.memset(eps_t[:], EPS), writes=["eps"])
        P.add("dve", lambda e: e.memset(small[:, 12:13], -0.5), writes=["mhalf"])
        P.add("sp", lambda e: e.dma_start(out=cst[:], in_=cst_d), writes=["cst"], lane="cst")
        P.add("dve", lambda e: e.tensor_copy(out=tri[:], in_=cst[:, 128:256]), reads=["cst"], writes=["tri"])
        gview = gnorm_d.rearrange("l n (j p) -> (l n j) p", p=128)
        P.add("sp", lambda e: e.dma_start(out=Fr[0:128, 0, 0:128], in_=gview[0:128, :]),
              writes=[("F", 0)], lane="p0")
        P.add("sp", lambda e: e.dma_start(out=Fr[0:64, 1, 0:128], in_=gview[128:192, :]),
              writes=[("F", 1)], lane="p1")
        P.add("sp", lambda e: e.dma_start(out=Fr[64:72, 1, 0:128], in_=gkv_d.rearrange("(j p) -> j p", p=128)),
              writes=[("F", 1)], lane="p2")
        P.add("sp", lambda e: e.dma_start(out=Fr[72:120, 1, 0:128],
                                          in_=wc_d.rearrange("l w (j p) -> (l w j) p", p=128)),
              writes=[("F", 1)], lane="p3")
        P.add("sp", lambda e: e.dma_start(out=Fr[120:122, 1, 0:128], in_=gsub_d),
              writes=[("F", 1)], lane="p4")
        P.add("pe", lambda e: e.transpose(PS[:, 0, 0:128], Fr[0:128, 0, 0:128], ident),
              reads=[("F", 0), "cst"], writes=pst(0))
        P.add("pe", lambda e: e.transpose(PS[:, 0, 128:250], Fr[0:122, 1, 0:128], cst[0:122, 0:122]),
              reads=[("F", 1), "cst"], writes=pst(0))
        P.add("dve", lambda e: e.tensor_copy(out=cols[:, 0:250], in_=PS[:, 0, 0:250]), reads=pst(0), writes=["cols"])
        gv4 = cols[:, 0:192].rearrange("p (l n j) -> p l n j", l=4, n=6)
        for n in (1, 5):
            P.add("dve", (lambda e, n=n: e.tensor_scalar(out=gv4[:, :, n, :], in0=gv4[:, :, n, :], scalar1=0.5,
                                                         scalar2=None, op0=ALU.mult)),
                  reads=["cols"], writes=["cols"])
        lv = Fr[:, 2, :].rearrange("p (a b) -> p a b", a=4)
        for i, ld in enumerate((lq1_d, lk1_d, lq2_d, lk2_d)):
            P.add("sp", (lambda e, i=i, ld=ld: e.dma_start(
                out=lv[:, i, :], in_=ld.rearrange("a b -> (a b)").partition_broadcast(128))),
                writes=[("F", 2)], lane="l%d" % i)
        pr = Fr[:, 3, 0:256].rearrange("p (a b) -> p a b", a=2)
        P.add("dve", lambda e: e.tensor_tensor(out=pr[:, 0, :], in0=lv[:, 0, :], in1=lv[:, 1, :], op=ALU.mult),
              reads=[("F", 2)], writes=[("F", 3)])
        P.add("dve", lambda e: e.tensor_tensor(out=pr[:, 1, :], in0=lv[:, 2, :], in1=lv[:, 3, :], op=ALU.mult),
              reads=[("F", 2)], writes=[("F", 3)])
        P.add("dve", lambda e: e.reduce_sum(out=small[:, 0:4],
                                            in_=Fr[:, 3, 0:256].rearrange("p (a b) -> p a b", a=4), axis=AX.X),
              reads=[("F", 3)], writes=["small"])
        P.add("act", lambda e: e.activation(out=small[:, 4:8], in_=small[:, 0:4], func=AF.Exp),
              reads=["small"], writes=["small"])
        P.add("dve", lambda e: e.tensor_tensor(out=small[:, 8:10], in0=small[:, 6:8], in1=small[:, 4:6],
                                               op=ALU.subtract), reads=["small"], writes=["small"])
        for jl in range(2):
            li = lambda_init(jl + 2)
            P.add("dve", (lambda e, jl=jl, li=li: e.tensor_scalar(out=small[:, 8 + jl:9 + jl], in0=small[:, 8 + jl:9 + jl],
                                                                  scalar1=-li, scalar2=None, op0=ALU.add)),
                  reads=["small"], writes=["small"])
            P.add("dve", (lambda e, jl=jl, li=li: e.tensor_scalar(out=small[:, 10 + jl:11 + jl],
                                                                  in0=cols[:, 248 + jl:249 + jl],
                                                                  scalar1=1.0 - li, scalar2=None, op0=ALU.mult)),
                  reads=["small", "cols"], writes=["small"])

        def gidx(l, n):
            return (l * 6 + n) * 8

        def rstd_of(src_fn, src_tok, ncols, invd, nj, lnexp=True):
            b = 6 + ctr["stat"] % 2
            ctr["stat"] += 1
            ssb = PS[:, b, 0:ncols]
            for j in range(nj):
                r = ctr["sq"] % 3
                ctr["sq"] += 1
                P.add("act", (lambda e, j=j, r=r: e.activation(out=sq[r][:, 0:ncols], in_=src_fn(j), func=AF.Square)),
                      reads=[src_tok(j)], writes=[("sq", r)])
                mm(ssb, ones[:], sq[r][:, 0:ncols], j == 0, j == nj - 1, [("sq", r), "ones"], pst(b))
            rb = ctr["rstd"] % 2
            ctr["rstd"] += 1
            if lnexp:
                P.add("act", lambda e: e.activation(out=sd[:, 0:ncols], in_=ssb, func=AF.Ln, bias=eps_t[:, 0:1],
                                                    scale=invd), reads=pst(b) + ["eps"], writes=["sd"])
                P.add("act", lambda e: e.activation(out=rstd[rb][:, 0:ncols], in_=sd[:, 0:ncols], func=AF.Exp,
                                                    scale=-0.5), reads=["sd"], writes=[("rstd", rb)])
            else:
                P.add("act", lambda e: e.activation(out=sd[:, 0:ncols], in_=ssb, func=AF.Sqrt, bias=eps_t[:, 0:1],
                                                    scale=invd), reads=pst(b) + ["eps"], writes=["sd"])
                P.add("dve", lambda e: e.reciprocal(out=rstd[rb][:, 0:ncols], in_=sd[:, 0:ncols]),
                      reads=["sd"], writes=[("rstd", rb)])
            return rstd[rb][:, 0:ncols], ("rstd", rb)

        def prenorm(gbase, tb):
            rs, rtok = rstd_of(lambda j: X[:, j, tbs(tb)], lambda j: ("X", j, tb), 512, 1.0 / D, 8)
            for j in range(8):
                P.add("dve", (lambda e, j=j: e.scalar_tensor_tensor(
                    out=H[:, j, tbs(tb)], in0=X[:, j, tbs(tb)], scalar=cols[:, gbase + j:gbase + j + 1], in1=rs,
                    op0=ALU.mult, op1=ALU.mult)),
                    reads=[("X", j, tb), rtok, "cols"], writes=[("H", j, tb)])

        def post_stats(tb):
            return rstd_of(lambda j: Fr[:, 2 * j + tb, :], lambda j: ("F", 2 * j + tb), 512, 1.0 / D, 8)

        def post_apply(gbase, tb, rs, rtok, js=range(8)):
            for j in js:
                blk = 2 * j + tb
                P.add("dve", (lambda e, j=j, blk=blk: e.scalar_tensor_tensor(
                    out=Fr[:, blk, :], in0=Fr[:, blk, :], scalar=cols[:, gbase + j:gbase + j + 1], in1=rs,
                    op0=ALU.mult, op1=ALU.mult)),
                    reads=[("F", blk), rtok, "cols"], writes=[("F", blk)])
                P.add("dve", (lambda e, j=j, blk=blk: e.tensor_tensor(
                    out=X[:, j, tbs(tb)], in0=X[:, j, tbs(tb)], in1=Fr[:, blk, :], op=ALU.add)),
                    reads=[("F", blk), ("X", j, tb)], writes=[("X", j, tb)])

        def wview(w2d):
            return w2d.rearrange("(k p) f -> p k f", p=128)

        def proj(Wv, c0, src, srctok, evac, hook=None):
            pcs = []
            for mh in range(2):
                s = next_slot()
                wv = slots[s][:, 0:4096].rearrange("p (k f) -> p k f", k=8)
                P.add("pool", (lambda e, wv=wv, mh=mh: e.dma_start(out=wv, in_=Wv[:, :, c0 + mh * 512:c0 + (mh + 1) * 512])),
                      writes=stoks(s, 0, 4), lane=("w", s, 0))
                pcs.append((s, wv))
            for tb in range(2):
                for m in range(8):
                    s, wv = pcs[m // 4]
                    mi = m % 4
                    bank = ctr["u"] % 2
                    ctr["u"] += 1
                    Pp = PS[:, bank, :]
                    for k in range(8):
                        mm(Pp, wv[:, k, mi * 128:(mi + 1) * 128], src[:, k, tbs(tb)], k == 0, k == 7,
                           stoks(s, 0, 4) + [(srctok, k, tb)], pst(bank))
                    evac(m, tb, Pp, bank)
                    if hook is not None and tb == 1:
                        hook(m)

        def evac_to_F(m, tb, Pp, bank):
            blk = 2 * m + tb
            P.add("act", lambda e: e.copy(out=Fr[:, blk, :], in_=Pp), reads=pst(bank), writes=[("F", blk)])

        def ffn(l, w, hook=None):
            Wg = wview(wg_d[l, w])
            Wu = wview(wu_d[l, w])
            Wd = wd_d[l, w].rearrange("(f p) m -> p f m", p=128)
            def gu_unit(s, gv, uv, hf, f0, fi, tb):
                fl = f0 + fi
                par = ctr["u"] % 2
                ctr["u"] += 1
                G = PS[:, 2 * par, :]
                U = PS[:, 2 * par + 1, :]
                for k in range(8):
                    mm(G, gv[:, k, fi * 128:(fi + 1) * 128], H[:, k, tbs(tb)], k == 0, k == 7,
                       stoks(s, 0, 3) + [("H", k, tb)], pst(2 * par))
                for k in range(8):
                    mm(U, uv[:, k, fi * 128:(fi + 1) * 128], H[:, k, tbs(tb)], k == 0, k == 7,
                       stoks(s, 3, 6) + [("H", k, tb)], pst(2 * par + 1))
                P.add("act", lambda e: e.activation(out=sg[par][:], in_=G, func=AF.Silu),
                      reads=pst(2 * par), writes=[("sg", par)])
                P.add("dve", lambda e: e.tensor_tensor(out=A[:, fl, tbs(tb)], in0=U, in1=sg[par][:], op=ALU.mult),
                      reads=pst(2 * par + 1) + [("sg", par)], writes=[("A", fl, tb)])

            def gu_load(hf, f0, n):
                s = next_slot()
                gv = slots[s][:, 0:8 * n * 128].rearrange("p (k f) -> p k f", k=8)
                uv = slots[s][:, 3072:3072 + 8 * n * 128].rearrange("p (k f) -> p k f", k=8)
                fa = hf * 11 + f0
                P.add("pool", lambda e: e.dma_start(out=gv, in_=Wg[:, :, fa * 128:(fa + n) * 128]),
                      writes=stoks(s, 0, 3), lane=("w", s, 0))
                P.add("pool", lambda e: e.dma_start(out=uv, in_=Wu[:, :, fa * 128:(fa + n) * 128]),
                      writes=stoks(s, 3, 6), lane=("w", s, 3))
                return s, gv, uv

            for hf in range(2):
                groups = ((0, 3), (3, 3), (6, 3), (9, 2))
                if hf == 0:
                    ld = [gu_load(hf, f0, n) for (f0, n) in groups[:2]]
                    for tb in range(2):
                        for gi, (f0, n) in enumerate(groups[:2]):
                            for fi in range(n):
                                gu_unit(ld[gi][0], ld[gi][1], ld[gi][2], hf, f0, fi, tb)
                    rest = groups[2:]
                else:
                    rest = groups
                for (f0, n) in rest:
                    s, gv, uv = gu_load(hf, f0, n)
                    for fi in range(n):
                        for tb in range(2):
                            gu_unit(s, gv, uv, hf, f0, fi, tb)
                pieces = []
                for m0 in (0, 4):
                    s = next_slot()
                    dv = slots[s][:, 0:5632].rearrange("p (f m) -> p f m", f=11)
                    P.add("pool", (lambda e, dv=dv, hf=hf, m0=m0: e.dma_start(
                        out=dv, in_=Wd[:, hf * 11:(hf + 1) * 11, m0 * 128:(m0 + 4) * 128])),
                        writes=stoks(s, 0, 6), lane=("w", s, 0))
                    pieces.append((s, dv))
                if hf == 0:
                    dorder = [(m, tb) for m in range(8) for tb in range(2)]
                else:
                    dorder = [(m, tb) for tb in range(2) for m in range(8)]
                for (m, tb) in dorder:
                    s, dv = pieces[m // 4]
                    mi = m % 4
                    par = ctr["u"] % 2
                    ctr["u"] += 1
                    Dp = PS[:, 4 + par, :]
                    for fl in range(11):
                        mm(Dp, dv[:, fl, mi * 128:(mi + 1) * 128], A[:, fl, tbs(tb)], fl == 0, fl == 10,
                           stoks(s, 0, 6) + [("A", fl, tb)], pst(4 + par))
                    blk = 2 * m + tb
                    if hf == 0:
                        P.add("act", (lambda e, Dp=Dp, blk=blk: e.copy(out=Fr[:, blk, :], in_=Dp)),
                              reads=pst(4 + par), writes=[("F", blk)])
                    else:
                        P.add("dve", (lambda e, Dp=Dp, blk=blk: e.tensor_tensor(
                            out=Fr[:, blk, :], in0=Dp, in1=Fr[:, blk, :], op=ALU.add)),
                            reads=pst(4 + par) + [("F", blk)], writes=[("F", blk)])
                    if hook is not None and hf == 1 and tb == 1:
                        hook(m)

        def conv_mixer(l, c, hook=None):
            Win = wview(wci_d[l])
            for jp in range(4):
                s = next_slot()
                parts = [slots[s][:, pt * 2048:(pt + 1) * 2048].rearrange("p (k f) -> p k f", k=8) for pt in range(3)]
                for pt in range(3):
                    P.add("pool", (lambda e, pt=pt, jp=jp, parts=parts: e.dma_start(
                        out=parts[pt], in_=Win[:, :, pt * 1024 + jp * 256:pt * 1024 + (jp + 1) * 256])),
                        writes=stoks(s, 2 * pt, 2 * pt + 2), lane=("w", s, 2 * pt))
                cus = []
                for jj in range(2):
                    j = 2 * jp + jj
                    cub = j % 2
                    cu = Fr[:, cub * 3:cub * 3 + 3, :].rearrange("p a b -> p (a b)")
                    cutoks = [("F", cub * 3 + i) for i in range(3)]
                    if c == 0:
                        P.add("dve", (lambda e, cu=cu: e.memset(cu[:, 0:2], 0.0)), writes=cutoks)
                    else:
                        P.add("dve", (lambda e, cu=cu, j=j: e.tensor_copy(out=cu[:, 0:2], in_=halo[:, l, j, :])),
                              reads=[("halo", l, j)], writes=cutoks)
                    cus.append((cu, cutoks))
                for tb in range(2):
                    for jj in range(2):
                        j = 2 * jp + jj
                        cu, cutoks = cus[jj]
                        w0 = 200 + (l * 3 + 0) * 8 + j
                        w1 = 200 + (l * 3 + 1) * 8 + j
                        w2 = 200 + (l * 3 + 2) * 8 + j
                        par = ctr["u"] % 2
                        ctr["u"] += 1
                        Pb, Pc, Pu = (PS[:, 3 * par + i, :] for i in range(3))
                        for pt in range(3):
                            for k in range(8):
                                mm(PS[:, 3 * par + pt, :], parts[pt][:, k, jj * 128:(jj + 1) * 128], H[:, k, tbs(tb)],
                                   k == 0, k == 7, stoks(s, 2 * pt, 2 * pt + 2) + [("H", k, tb)], pst(3 * par + pt))
                        ucp = Fr[:, 6 + par, :]
                        z = Fr[:, 8 + par, :]
                        o2 = 2 + tb * 512
                        P.add("act", (lambda e, ucp=ucp, Pu=Pu: e.copy(out=ucp, in_=Pu)),
                              reads=pst(3 * par + 2), writes=[("F", 6 + par)])
                        P.add("dve", (lambda e, cu=cu, o2=o2, Pc=Pc, ucp=ucp: e.tensor_tensor(
                            out=cu[:, o2:o2 + 512], in0=Pc, in1=ucp, op=ALU.mult)),
                            reads=pst(3 * par + 1) + [("F", 6 + par)], writes=cutoks)
                        P.add("act", (lambda e, z=z, cu=cu, o2=o2, w2=w2: e.mul(out=z, in_=cu[:, o2:o2 + 512],
                                                                               mul=cols[:, w2:w2 + 1])),
                              reads=cutoks + ["cols"], writes=[("F", 8 + par)])
                        P.add("dve", (lambda e, z=z, cu=cu, o2=o2, w1=w1: e.scalar_tensor_tensor(
                            out=z, in0=cu[:, o2 - 1:o2 + 511], scalar=cols[:, w1:w1 + 1], in1=z,
                            op0=ALU.mult, op1=ALU.add)), reads=cutoks + ["cols", ("F", 8 + par)],
                            writes=[("F", 8 + par)])
                        P.add("dve", (lambda e, z=z, cu=cu, o2=o2, w0=w0: e.scalar_tensor_tensor(
                            out=z, in0=cu[:, o2 - 2:o2 + 510], scalar=cols[:, w0:w0 + 1], in1=z,
                            op0=ALU.mult, op1=ALU.add)), reads=cutoks + ["cols", ("F", 8 + par)],
                            writes=[("F", 8 + par)])
                        P.add("dve", (lambda e, z=z, Pb=Pb, j=j, tb=tb: e.tensor_tensor(
                            out=A[:, j, tbs(tb)], in0=Pb, in1=z, op=ALU.mult)),
                            reads=pst(3 * par) + [("F", 8 + par)], writes=[("A", j, tb)])
                if c == 0:
                    for jj in range(2):
                        j = 2 * jp + jj
                        cu, cutoks = cus[jj]
                        P.add("dve", (lambda e, cu=cu, j=j: e.tensor_copy(out=halo[:, l, j, :], in_=cu[:, 1024:1026])),
                              reads=cutoks, writes=[("halo", l, j)])
            proj(wview(wco_d[l]), 0, A, "A", evac_to_F, hook)

        def kv_proj(c):
            Wkv = wview(wkv_d)

            def evac_k(m, tb, Pp, bank):
                P.add("act", lambda e: e.copy(out=KT[:, m, c * TCH + tb * 512:c * TCH + (tb + 1) * 512], in_=Pp),
                      reads=pst(bank), writes=[("K", m, c)])

            proj(Wkv, 0, H, "H", evac_k)
            for eh in range(2):
                s = next_slot()
                wv = slots[s][:, 0:4096].rearrange("p (k f) -> p k f", k=8)
                P.add("pool", (lambda e, wv=wv, eh=eh: e.dma_start(out=wv, in_=Wkv[:, :, D + eh * 512:D + (eh + 1) * 512])),
                      writes=stoks(s, 0, 4), lane=("w", s, 0))
                for tt in range(8):
                    bank = ctr["u"] % 2
                    ctr["u"] += 1
                    Pp = PS[:, bank, :]
                    for k in range(8):
                        mm(Pp, H[:, k, tt * 128:(tt + 1) * 128], wv[:, k, :], k == 0, k == 7,
                           stoks(s, 0, 4) + [("H", k, tt // 4)], pst(bank))
                    kt = c * 8 + tt
                    if tt % 2 == 0:
                        P.add("act", (lambda e, kt=kt, eh=eh, Pp=Pp: e.copy(out=V[:, kt, eh * 512:(eh + 1) * 512], in_=Pp)),
                              reads=pst(bank), writes=[("V", kt)])
                    else:
                        P.add("dve", (lambda e, kt=kt, eh=eh, Pp=Pp: e.tensor_copy(out=V[:, kt, eh * 512:(eh + 1) * 512], in_=Pp)),
                              reads=pst(bank), writes=[("V", kt)])

        def attn(l, c, hook=None):
            jl = l - 2

            def evac_q(m, tb, Pp, bank):
                P.add("act", lambda e: e.mul(out=A[:, m, tbs(tb)], in_=Pp, mul=0.125),
                      reads=pst(bank), writes=[("A", m, tb)])

            proj(wview(wq_d[jl]), 0, H, "H", evac_q)

            steps = []
            for hd in range(8):
                for u in range(4 * c, 4 * c + 4):
                    for kt in range(2 * u + 2):
                        steps.append((hd, u, kt))
            Eb = [Fr[:, r, :].bitcast(BF16)[:, 0:512].rearrange("p (c q) -> p c q", c=2) for r in range(3)]
            neglam = small[:, 8 + jl:9 + jl]
            gs = small[:, 10 + jl:11 + jl]
            tri_b = tri[:].unsqueeze(1).to_broadcast([128, 2, 128])

            def geom(i):
                hd, u, kt = steps[i]
                d = kt - 2 * u
                q0 = 128 if d == 1 else 0
                ql = (u - 4 * c) * 256
                return hd, u, kt, d, q0, ql

            it_of = []
            _it = 0
            for (hd_, u_, kt_) in steps:
                it_of.append(_it)
                if kt_ == 2 * u_ + 1:
                    _it += 1
            qz = [Fr[:, 9 + p_, :].bitcast(BF16)[:, 0:512].rearrange("p (c q) -> p c q", c=2) for p_ in range(2)]
            for p_ in range(2):
                P.add("dve", (lambda e, p_=p_: e.memset(Fr[:, 9 + p_, :].bitcast(BF16)[:, 0:512], 0.0)),
                      writes=[("F", 9 + p_)])

            def emit_S(i):
                hd, u, kt, d, q0, ql = geom(i)
                par = i % 2
                ipar = it_of[i] % 2
                if kt == 0:
                    P.add("dve", lambda e: e.tensor_copy(out=qz[ipar][0:64, 0, :], in_=A[0:64, hd, ql:ql + 256]),
                          reads=[("A", hd, ql // 512)], writes=[("F", 9 + ipar)])
                    P.add("dve", lambda e: e.tensor_copy(out=qz[ipar][64:128, 1, :], in_=A[64:128, hd, ql:ql + 256]),
                          reads=[("A", hd, ql // 512)], writes=[("F", 9 + ipar)])
                for cm in range(2):
                    mm(PS[:, par, cm * 256 + q0:(cm + 1) * 256],
                       KT[:, hd, kt * 128:(kt + 1) * 128],
                       qz[ipar][:, cm, q0:256], True, True,
                       [("K", hd, kt // 8), ("F", 9 + ipar)], pst(par))

            def emit_exp(i):
                hd, u, kt, d, q0, ql = geom(i)
                par = i % 2
                r = i % 3
                P.add("act", lambda e: e.activation(
                    out=Eb[r][:, :, q0:256],
                    in_=PS[:, par, :].rearrange("p (c q) -> p c q", c=2)[:, :, q0:256], func=AF.Exp),
                    reads=pst(par), writes=[("F", r)])
                if d >= 0:
                    P.add("dve", lambda e: e.tensor_tensor(out=Eb[r][:, :, q0:q0 + 128], in0=Eb[r][:, :, q0:q0 + 128],
                                                           in1=tri_b, op=ALU.mult),
                          reads=[("F", r), "tri"], writes=[("F", r)])

            pending = []

            def emit_PV(i, it):
                hd, u, kt, d, q0, ql = geom(i)
                r = i % 3
                ipar = it % 2
                bA = 2 + ipar
                bS = 4 + ipar
                Ab = PS[:, bA, :].rearrange("p (c q) -> p c q", c=2)
                Sb = PS[:, bS, :].rearrange("p (c q) -> p c q", c=2)
                first = kt == 0
                last = kt == 2 * u + 1
                if q0 == 0:
                    E2 = Fr[:, r, :].bitcast(BF16)[:, 0:512]
                    mm(PS[:, bA, :], V[:, kt, hd * 128:(hd + 1) * 128], E2, first, last,
                       [("V", kt), ("F", r)], pst(bA))
                    mm(PS[:, bS, :], ones[:], E2, first, last, [("F", r), "ones"], pst(bS))
                else:
                    for cm in range(2):
                        mm(Ab[:, cm, q0:256], V[:, kt, hd * 128:(hd + 1) * 128], Eb[r][:, cm, q0:256], first,
                           last and cm == 1, [("V", kt), ("F", r)], pst(bA))
                    for cm in range(2):
                        mm(Sb[:, cm, q0:256], ones[:], Eb[r][:, cm, q0:256], first, last and cm == 1,
                           [("F", r), "ones"], pst(bS))
                if last:
                    b0 = 3 + 3 * ipar
                    rr = Fr[:, b0, :]
                    tt_ = Fr[:, b0 + 1, :]
                    ob = Fr[:, b0 + 2, 0:256]
                    P.add("dve", lambda e: e.reciprocal(out=rr, in_=PS[:, bS, :]),
                          reads=pst(bS), writes=[("F", b0)])
                    P.add("dve", lambda e: e.tensor_tensor(out=tt_, in0=PS[:, bA, :], in1=rr, op=ALU.mult),
                          reads=pst(bA) + [("F", b0)], writes=[("F", b0 + 1)])
                    P.add("dve", lambda e: e.scalar_tensor_tensor(out=H[:, hd, ql:ql + 256], in0=tt_[:, 256:512],
                                                                  scalar=neglam, in1=tt_[:, 0:256],
                                                                  op0=ALU.mult, op1=ALU.add),
                          reads=[("F", b0 + 1), "small"], writes=[("H", hd, ql // 512)])

            n = len(steps)
            emit_S(0)
            it = 0
            for i in range(n):
                if i + 1 < n:
                    emit_S(i + 1)
                emit_exp(i)
                emit_PV(i, it)
                if steps[i][2] == 2 * steps[i][1] + 1:
                    it += 1
                for pnd in list(pending):
                    pnd[0] -= 1
                    if pnd[0] <= 0:
                        pending.remove(pnd)
                        pnd[1]()
            for pnd in pending:
                pnd[1]()
            for tb in range(2):
                for hd in range(8):
                    rs, rtok = rstd_of(lambda j, hd=hd, tb=tb: H[:, hd, tbs(tb)], lambda j, hd=hd, tb=tb: ("H", hd, tb),
                                       512, 1.0 / 128.0, 1, lnexp=False)
                    P.add("dve", (lambda e, hd=hd, tb=tb, rs=rs: e.scalar_tensor_tensor(
                        out=H[:, hd, tbs(tb)], in0=H[:, hd, tbs(tb)], scalar=gs, in1=rs,
                        op0=ALU.mult, op1=ALU.mult)),
                        reads=[("H", hd, tb), rtok, "small"], writes=[("H", hd, tb)])
            proj(wview(wo_d[jl]), 0, H, "H", evac_to_F, hook)

        def load_x(c):
            for tt in range(8):
                P.add("sp", (lambda e, tt=tt: e.dma_start(
                    out=Fr[:, 2 * tt:2 * tt + 2, :].rearrange("p a b -> p (a b)"),
                    in_=x_d[c * TCH + tt * 128:c * TCH + (tt + 1) * 128, :])),
                    writes=[("F", 2 * tt), ("F", 2 * tt + 1)], lane=("x", tt % 4))
            for j in range(8):
                for g in range(2):
                    bank = ctr["u"] % 4
                    ctr["u"] += 1
                    for ti in range(4):
                        tt = g * 4 + ti
                        xs = Fr[:, 2 * tt:2 * tt + 2, :].rearrange("p a b -> p (a b)")
                        P.add("pe", (lambda e, bank=bank, ti=ti, xs=xs, j=j: e.transpose(
                            PS[:, bank, ti * 128:(ti + 1) * 128], xs[:, j * 128:(j + 1) * 128], ident)),
                            reads=[("F", 2 * tt), ("F", 2 * tt + 1), "cst"], writes=pst(bank))
                    if (j + g) % 2 == 0:
                        P.add("act", (lambda e, bank=bank, j=j, g=g: e.copy(out=X[:, j, tbs(g)], in_=PS[:, bank, :])),
                              reads=pst(bank), writes=[("X", j, g)])
                    else:
                        P.add("dve", (lambda e, bank=bank, j=j, g=g: e.tensor_copy(out=X[:, j, tbs(g)], in_=PS[:, bank, :])),
                              reads=pst(bank), writes=[("X", j, g)])

        def store_x(c):
            for tt in range(8):
                for g in range(2):
                    bank = ctr["u"] % 4
                    ctr["u"] += 1
                    for ji in range(4):
                        j = g * 4 + ji
                        P.add("pe", (lambda e, bank=bank, ji=ji, j=j, tt=tt: e.transpose(
                            PS[:, bank, ji * 128:(ji + 1) * 128], X[:, j, tt * 128:(tt + 1) * 128], ident)),
                            reads=[("X", j, tt // 4), "cst"], writes=pst(bank))
                    blk = 2 * tt + g
                    if g == 0:
                        P.add("act", (lambda e, bank=bank, blk=blk: e.copy(out=Fr[:, blk, :], in_=PS[:, bank, :])),
                              reads=pst(bank), writes=[("F", blk)])
                    else:
                        P.add("dve", (lambda e, bank=bank, blk=blk: e.tensor_copy(out=Fr[:, blk, :], in_=PS[:, bank, :])),
                              reads=pst(bank), writes=[("F", blk)])
                P.add("sp", (lambda e, tt=tt: e.dma_start(
                    out=out_d[c * TCH + tt * 128:c * TCH + (tt + 1) * 128, :],
                    in_=Fr[:, 2 * tt:2 * tt + 2, :].rearrange("p a b -> p (a b)"))),
                    reads=[("F", 2 * tt), ("F", 2 * tt + 1)], writes=[("OUT", c, tt)], lane=("o", tt % 4))

        for c in range(NCH):
            load_x(c)
            subs = []
            for l in range(4):
                subs.append((lambda hook, l=l: ffn(l, 0, hook), gidx(l, 0), gidx(l, 1)))
                if l < 2:
                    subs.append((lambda hook, l=l, c=c: conv_mixer(l, c, hook), gidx(l, 2), gidx(l, 3)))
                else:
                    subs.append((lambda hook, l=l, c=c: attn(l, c, hook), gidx(l, 2), gidx(l, 3)))
                subs.append((lambda hook, l=l: ffn(l, 1, hook), gidx(l, 4), gidx(l, 5)))
                if l == 1:
                    subs.append((lambda hook, c=c: kv_proj(c), 192, None))
            nrun = nstop if nstop < 6 else (nstop if nstop > 6 else 6)
            if nstop > 6:
                nrun = nstop + 1
            subs = subs[:min(len(subs), nrun)]
            if subs:
                for tb in range(2):
                    prenorm(subs[0][1], tb)
            for i, (fn, gpre, gpost) in enumerate(subs):
                stash = {}

                def hook(m, stash=stash, gpost=gpost):
                    if m == 1:
                        stash["s0"] = post_stats(0)
                    elif m >= 2:
                        js = {2: (0, 1), 3: (2, 3), 4: (4,), 5: (5,), 6: (6,), 7: (7,)}[m]
                        post_apply(gpost, 0, stash["s0"][0], stash["s0"][1], js)
                        stash["done0"] = True

                fn(hook if gpost is not None else None)
                nxt = subs[i + 1][1] if i + 1 < len(subs) else None
                for tb in range(2):
                    if gpost is not None and not (tb == 0 and stash.get("done0")):
                        rs, rtok = stash["s0"] if (tb == 0 and "s0" in stash) else post_stats(tb)
                        post_apply(gpost, tb, rs, rtok)
                    if nxt is not None:
                        prenorm(nxt, tb)
            store_x(c)
        P.add("sp", None, reads=[("OUT", c, tt) for c in range(NCH) for tt in range(8)])
        P.emit(nc, st)
    nc._prog_stats = P.stats
    return nc


_CACHE = {}


def _consts():
    c = np.zeros((128, 256), np.float32)
    c[:, 0:128] = np.eye(128, dtype=np.float32)
    c[:, 128:256] = np.triu(np.ones((128, 128), np.float32))
    return c


def kernel(**inputs):
    nstop = int(inputs.pop("_nstop", 999))
    ncores = int(inputs.pop("_ncores", NB))
    if nstop not in _CACHE:
        _CACHE[nstop] = build_nc(nstop)
    nc = _CACHE[nstop]
    names = ["g_norm", "w_ffn_gate", "w_ffn_up", "w_ffn_down", "w_conv_in", "w_conv", "w_conv_out", "g_kv",
             "w_kv", "w_q", "lambda_q1", "lambda_k1", "lambda_q2", "lambda_k2", "g_subln", "w_o"]
    shared = {k: np.asarray(inputs[k], dtype=np.float32) for k in names}
    if DBG["lite"]:
        for k in ("w_ffn_gate", "w_ffn_up", "w_ffn_down"):
            shared[k] = shared[k][:1, :1]
        for k in ("w_conv_in", "w_conv_out", "w_q", "w_o"):
            shared[k] = shared[k][:1]
    shared = {k: np.ascontiguousarray(v) for k, v in shared.items()}
    shared["consts"] = _consts()
    x = np.asarray(inputs["x"], dtype=np.float32)
    in_maps = []
    for b in range(ncores):
        m = dict(shared)
        m["x"] = np.ascontiguousarray(x[b])
        in_maps.append(m)
    res = run_bass_kernel_spmd(nc, in_maps, core_ids=list(range(ncores)))
    out = np.stack([np.asarray(res.results[b]["out"], dtype=np.float32) for b in range(ncores)], axis=0)
    return out
```
